# Optimizing a Trainium2 kernel written in Bass

```python
import jax, jax.numpy as jnp
from jax import lax
import numpy as np

D_MODEL = 1024
BATCH = 8
SEQ = 4096
DEPTH = 4

N_MIXERS = 3
N_CONV = (DEPTH + 2) // 3
N_ATTN = (DEPTH + 1) // 3
N_POOL = DEPTH // 3

CONV_WIDTH = 31

ATTN_GROUPS = ((128, 1), (512, 4), (2048, 16))
N_ATTN_GROUPS = len(ATTN_GROUPS)
HEADS_PER_GROUP = 8
HEAD_DIM = 64
ATTN_GROUP_WIDTH = HEADS_PER_GROUP * HEAD_DIM
ATTN_QKV_WIDTH = 3 * N_ATTN_GROUPS * ATTN_GROUP_WIDTH
ROPE_THETA = 10000.0
NEG_INF = -1e30

POOL_WINDOWS = (2, 4, 8, 16)
POOL_GROUPS = len(POOL_WINDOWS)
POOL_GROUP_DIM = D_MODEL // POOL_GROUPS

FFN_HIDDEN = -(-8 * D_MODEL // (3 * 256)) * 256

RMS_EPS = 1e-6
LN_EPS = 1e-5

kernel_name = "hybrid_conv_dilattn_pool_trunk"


def rmsnorm(x, g):
    xf = x.astype(jnp.float32)
    y = xf * lax.rsqrt(jnp.mean(xf * xf, axis=-1, keepdims=True) + RMS_EPS)
    return (y * g.astype(jnp.float32)).astype(x.dtype)


def layernorm(x, g, b):
    xf = x.astype(jnp.float32)
    mu = jnp.mean(xf, axis=-1, keepdims=True)
    var = jnp.mean(jnp.square(xf - mu), axis=-1, keepdims=True)
    y = (xf - mu) * lax.rsqrt(var + LN_EPS)
    return (y * g.astype(jnp.float32) + b.astype(jnp.float32)).astype(x.dtype)


def rope(x):
    S, Dh = x.shape[1], x.shape[-1]
    half = Dh // 2
    inv_freq = ROPE_THETA ** (-jnp.arange(half, dtype=jnp.float32) / half)
    ang = jnp.arange(S, dtype=jnp.float32)[:, None] * inv_freq[None, :]
    cos = jnp.cos(ang)[None, :, None, :]
    sin = jnp.sin(ang)[None, :, None, :]
    xf = x.astype(jnp.float32)
    x1, x2 = xf[..., :half], xf[..., half:]
    out = jnp.concatenate([x1 * cos - x2 * sin, x2 * cos + x1 * sin], axis=-1)
    return out.astype(x.dtype)


def conformer_conv(h, w_in, b_in, w_dw, b_dw, ln_g, ln_b, w_out, b_out):
    D = h.shape[-1]
    a, gate = jnp.split(h @ w_in + b_in, 2, axis=-1)
    u = a * jax.nn.sigmoid(gate)
    u = lax.conv_general_dilated(
        u, w_dw[:, None, :].astype(u.dtype), window_strides=(1,),
        padding=[(CONV_WIDTH - 1, 0)],
        dimension_numbers=("NWC", "WIO", "NWC"),
        feature_group_count=D) + b_dw
    u = jax.nn.silu(layernorm(u, ln_g, ln_b))
    return u @ w_out + b_out


def dilated_window_attention(q, k, v, window, dilation):
    B, S, H, Dh = q.shape
    blk = window // dilation
    span = dilation * blk
    L = -(-S // span) * span
    n = L // dilation
    nb = n // blk

    def gather_strided(t):
        t = jnp.pad(t, ((0, 0), (0, L - S), (0, 0), (0, 0))).reshape(B, n, dilation, H, Dh)
        return t.transpose(0, 2, 3, 1, 4).reshape(B, dilation, H, nb, blk, Dh)

    def with_prev_block(t):
        prev = jnp.pad(t, ((0, 0), (0, 0), (0, 0), (1, 0), (0, 0), (0, 0)))[:, :, :, :nb]
        return jnp.concatenate([prev, t], axis=4)

    qb = gather_strided(q).astype(jnp.float32)
    kk = with_prev_block(gather_strided(k)).astype(jnp.float32)
    vv = with_prev_block(gather_strided(v)).astype(jnp.float32)

    s = jnp.einsum("brhnqd,brhnkd->brhnqk", qb, kk) * (Dh ** -0.5)
    qi = jnp.arange(blk)[:, None] + blk
    kj = jnp.arange(2 * blk)[None, :]
    dist = qi - kj
    band = (dist >= 0) & (dist <= blk)
    has_prev = (jnp.arange(nb) > 0)[:, None, None] | (kj >= blk)[None]
    valid = band[None] & has_prev
    s = jnp.where(valid, s, NEG_INF)
    m = jnp.max(s, axis=-1, keepdims=True)
    p = jnp.exp(s - m)
    den = jnp.sum(p, axis=-1)
    o = jnp.einsum("brhnqk,brhnkd->brhnqd", p, vv) / den[..., None]
    lse = m[..., 0] + jnp.log(den)

    o = o.reshape(B, dilation, H, n, Dh).transpose(0, 3, 1, 2, 4).reshape(B, L, H, Dh)[:, :S]
    lse = lse.reshape(B, dilation, H, n).transpose(0, 3, 1, 2).reshape(B, L, H)[:, :S]
    return o, lse


def dilated_attention_mixer(h, w_qkv, w_o):
    B, S, _ = h.shape
    qkv = (h @ w_qkv).reshape(B, S, 3, N_ATTN_GROUPS * HEADS_PER_GROUP, HEAD_DIM)
    q = rope(qkv[:, :, 0]).reshape(B, S, N_ATTN_GROUPS, HEADS_PER_GROUP, HEAD_DIM)
    k = rope(qkv[:, :, 1]).reshape(B, S, N_ATTN_GROUPS, HEADS_PER_GROUP, HEAD_DIM)
    v = qkv[:, :, 2].reshape(B, S, N_ATTN_GROUPS, HEADS_PER_GROUP, HEAD_DIM)
    outs, lses = [], []
    for g, (window, dilation) in enumerate(ATTN_GROUPS):
        o_g, lse_g = dilated_window_attention(q[:, :, g], k[:, :, g], v[:, :, g], window, dilation)
        outs.append(o_g)
        lses.append(lse_g)
    alpha = jax.nn.softmax(jnp.stack(lses, axis=0), axis=0)
    o = jnp.sum(alpha[..., None] * jnp.stack(outs, axis=0), axis=0)
    return o.reshape(B, S, ATTN_GROUP_WIDTH).astype(h.dtype) @ w_o


def multiscale_pool_mixer(h, w_grp, scale):
    B, S, D = h.shape
    hf = h.astype(jnp.float32)
    cs = jnp.pad(jnp.cumsum(hf, axis=1), ((0, 0), (1, 0), (0, 0)))
    csg = cs.reshape(B, S + 1, POOL_GROUPS, POOL_GROUP_DIM)
    pooled = []
    for g, w in enumerate(POOL_WINDOWS):
        c = csg[:, :, g]
        lo = jnp.pad(c, ((0, 0), (w - 1, 0), (0, 0)))[:, :S]
        cnt = jnp.minimum(jnp.arange(1, S + 1), w).astype(jnp.float32)[None, :, None]
        pooled.append((c[:, 1:] - lo) / cnt)
    p = jnp.stack(pooled, axis=2) - hf.reshape(B, S, POOL_GROUPS, POOL_GROUP_DIM)
    y = jnp.einsum("bsgc,gce->bsge", p.astype(h.dtype), w_grp).reshape(B, S, D)
    return y * scale


def swiglu(h, w_gate, w_up, w_down):
    return (jax.nn.silu(h @ w_gate) * (h @ w_up)) @ w_down


def setup_inputs(seed: int = 0) -> dict:
    key = jax.random.key(seed)
    ks = jax.random.split(key, 24)
    f32 = jnp.float32
    D, F = D_MODEL, FFN_HIDDEN

    def nrm(k, shape, scale):
        return jax.random.normal(k, shape, f32) * scale

    return {
        "x": nrm(ks[0], (BATCH, SEQ, D), 1.0),
        "norm_mix_g": 1.0 + nrm(ks[1], (DEPTH, D), 0.02),
        "norm_ffn_g": 1.0 + nrm(ks[2], (DEPTH, D), 0.02),
        "final_norm_g": 1.0 + nrm(ks[3], (D,), 0.02),
        "conv_w_in": nrm(ks[4], (N_CONV, D, 2 * D), D ** -0.5),
        "conv_b_in": nrm(ks[5], (N_CONV, 2 * D), 0.01),
        "conv_w_dw": nrm(ks[6], (N_CONV, CONV_WIDTH, D), CONV_WIDTH ** -0.5),
        "conv_b_dw": nrm(ks[7], (N_CONV, D), 0.01),
        "conv_ln_g": 1.0 + nrm(ks[8], (N_CONV, D), 0.02),
        "conv_ln_b": nrm(ks[9], (N_CONV, D), 0.01),
        "conv_w_out": nrm(ks[10], (N_CONV, D, D), D ** -0.5),
        "conv_b_out": nrm(ks[11], (N_CONV, D), 0.01),
        "attn_w_qkv": nrm(ks[12], (N_ATTN, D, ATTN_QKV_WIDTH), D ** -0.5),
        "attn_w_o": nrm(ks[13], (N_ATTN, ATTN_GROUP_WIDTH, D), ATTN_GROUP_WIDTH ** -0.5),
        "pool_w": nrm(ks[14], (N_POOL, POOL_GROUPS, POOL_GROUP_DIM, POOL_GROUP_DIM), POOL_GROUP_DIM ** -0.5),
        "pool_scale": 1.0 + nrm(ks[15], (N_POOL, D), 0.1),
        "ffn_w_gate": nrm(ks[16], (DEPTH, D, F), D ** -0.5),
        "ffn_w_up": nrm(ks[17], (DEPTH, D, F), D ** -0.5),
        "ffn_w_down": nrm(ks[18], (DEPTH, F, D), F ** -0.5),
    }


def reference(x, norm_mix_g, norm_ffn_g, final_norm_g,
              conv_w_in, conv_b_in, conv_w_dw, conv_b_dw, conv_ln_g, conv_ln_b,
              conv_w_out, conv_b_out, attn_w_qkv, attn_w_o, pool_w, pool_scale,
              ffn_w_gate, ffn_w_up, ffn_w_down):
    for i in range(DEPTH):
        kind, j = i % N_MIXERS, i // N_MIXERS
        h = rmsnorm(x, norm_mix_g[i])
        if kind == 0:
            y = conformer_conv(h, conv_w_in[j], conv_b_in[j], conv_w_dw[j], conv_b_dw[j],
                               conv_ln_g[j], conv_ln_b[j], conv_w_out[j], conv_b_out[j])
        elif kind == 1:
            y = dilated_attention_mixer(h, attn_w_qkv[j], attn_w_o[j])
        else:
            y = multiscale_pool_mixer(h, pool_w[j], pool_scale[j])
        x = x + y.astype(x.dtype)
        h = rmsnorm(x, norm_ffn_g[i])
        x = x + swiglu(h, ffn_w_gate[i], ffn_w_up[i], ffn_w_down[i]).astype(x.dtype)
    return rmsnorm(x, final_norm_g)
```

```python
import numpy as np
import ml_dtypes
import concourse.bass as bass
import concourse.mybir as mybir
from concourse.bass_utils import run_bass_kernel_spmd

F32 = mybir.dt.float32
BF16 = mybir.dt.bfloat16
AF = mybir.ActivationFunctionType
ALU = mybir.AluOpType

D = 1024
S = 4096
NC8 = 8
TS = 2048
NSEG = S // TS
TT = 512
NT = TS // TT
FH = 2816
NHC = FH // 128
DEPTH = 4
CW = 31
RMS_EPS = 1e-6
LN_EPS = 1e-5
SLOT = 4096
NSLOT = 5

VEC_NAMES = []
for _l in range(DEPTH):
    VEC_NAMES += [f"nmix{_l}", f"nffn{_l}"]
VEC_NAMES += ["nfinal"]
for _j in range(2):
    VEC_NAMES += [f"cba{_j}", f"cbg{_j}", f"cbdw{_j}", f"clng{_j}", f"clnb{_j}", f"cbout{_j}"]
VEC_NAMES += ["pscale"]
VOFF = {n: 8 * i for i, n in enumerate(VEC_NAMES)}
NV = 8 * len(VEC_NAMES)


class Tracker:
    def __init__(self, nc):
        self.nc = nc
        self.engs = {"pe": nc.tensor, "act": nc.scalar, "dve": nc.vector,
                     "pool": nc.gpsimd, "sp": nc.sync}
        self.sem = {k: nc.alloc_semaphore(f"prog_{k}") for k in ("pe", "act", "dve", "pool")}
        self.cnt = {k: 0 for k in self.sem}
        self.waited = {k: {} for k in self.engs}
        self.dpool = {}
        for q in ("sp", "pool"):
            self.dpool[q] = [[nc.alloc_semaphore(f"dma_{q}_{i}"), 0] for i in range(12)]
        self.dnext = {"sp": 0, "pool": 0}
        self.res = {}
        self.nwaits = 0

    def _wait(self, eng, h):
        if h is None:
            return
        sem, val, src, key = h
        if eng == "pe" and src == "pe":
            return
        if self.waited[eng].get(key, 0) >= val:
            return
        self.engs[eng].wait_ge(sem, val)
        self.waited[eng][key] = val
        self.nwaits += 1

    def _deps(self, eng, reads, writes):
        for k in reads:
            r = self.res.get(k)
            if r is not None:
                self._wait(eng, r[0])
        for k in writes:
            r = self.res.get(k)
            if r is not None:
                self._wait(eng, r[0])
                for h in r[1]:
                    self._wait(eng, h)

    def _commit(self, h, reads, writes):
        for k in reads:
            r = self.res.setdefault(k, [None, []])
            if h[2] in ("pe", "act", "dve", "pool") and h[3] == h[2]:
                r[1] = [x for x in r[1] if x[3] != h[3]]
            r[1].append(h)
        for k in writes:
            self.res[k] = [h, []]

    def op(self, eng, fn, reads=(), writes=()):
        self._deps(eng, reads, writes)
        inst = fn(self.engs[eng])
        self.cnt[eng] += 1
        inst.then_inc(self.sem[eng], 1)
        h = (self.sem[eng], self.cnt[eng], eng, eng)
        self._commit(h, reads, writes)
        return h

    def mm_group(self, mms, reads=(), writes=()):
        self._deps("pe", reads, writes)
        n = len(mms)
        inst = None
        for i, (o, l, r) in enumerate(mms):
            inst = self.nc.tensor.matmul(o, lhsT=l, rhs=r, start=(i == 0), stop=(i == n - 1))
        self.cnt["pe"] += 1
        inst.then_inc(self.sem["pe"], 1)
        h = (self.sem["pe"], self.cnt["pe"], "pe", "pe")
        self._commit(h, reads, writes)
        return h

    def dma(self, q, out, in_, reads=(), writes=()):
        pool = self.dpool[q]
        i = self.dnext[q]
        self.dnext[q] = (i + 1) % len(pool)
        sem, c = pool[i]
        key = f"d{q}{i}"
        self._wait(q, (sem, c, q, key))
        self._deps(q, reads, writes)
        self.engs[q].dma_start(out=out, in_=in_).then_inc(sem, 16)
        pool[i][1] = c + 16
        h = (sem, c + 16, q, key)
        self._commit(h, reads, writes)
        return h

    def barrier(self):
        hs = [(self.sem[k], self.cnt[k], k, k) for k in self.sem if self.cnt[k] > 0]
        for q in ("sp", "pool"):
            for i, (sem, c) in enumerate(self.dpool[q]):
                if c > 0:
                    hs.append((sem, c, q, f"d{q}{i}"))
        for e in ("pe", "act", "dve", "pool", "sp"):
            for h in hs:
                if e == "pe" and h[2] == "pe":
                    continue
                self._wait(e, h)
        self.res = {}


class Prog:
    def __init__(self, plan, final_norm=True):
        self.plan = plan
        self.final_norm = final_norm
        nc = bass.Bass("TRN2", target_bir_lowering=False)
        self.nc = nc
        self.T = Tracker(nc)
        dt = nc.dram_tensor
        self.d_xT = dt("xT", [D, S], F32, kind="ExternalInput").ap()
        self.d_vec = dt("vec", [128, NV], F32, kind="ExternalInput").ap()
        self.d_wg = dt("ffn_w_gate", [DEPTH, D, FH], F32, kind="ExternalInput").ap()
        self.d_wu = dt("ffn_w_up", [DEPTH, D, FH], F32, kind="ExternalInput").ap()
        self.d_wd = dt("ffn_w_down", [DEPTH, FH, D], F32, kind="ExternalInput").ap()
        self.d_cwin = dt("conv_w_in_r", [2, D, 8, 256], F32, kind="ExternalInput").ap()
        self.d_cwout = dt("conv_w_out", [2, D, D], F32, kind="ExternalInput").ap()
        self.d_cwdw = dt("conv_w_dw_r", [128, 2, 8, CW], F32, kind="ExternalInput").ap()
        self.d_pw = dt("pool_w", [4, 256, 256], F32, kind="ExternalInput").ap()
        self.d_pinv = dt("pool_inv", [128, 4, 16], F32, kind="ExternalInput").ap()
        self.d_wattn = dt("attn_w_r", [D, 4, 3, 640], F32, kind="ExternalInput").ap()
        self.d_wo = dt("attn_w_o", [512, D], F32, kind="ExternalInput").ap()
        self.d_cos = dt("rope_cos", [128, S], F32, kind="ExternalInput").ap()
        self.d_sin = dt("rope_sin", [128, S], F32, kind="ExternalInput").ap()
        self.d_cbf = dt("const_bf", [128, 128 + 128 + 256], BF16, kind="ExternalInput").ap()
        self.d_c32 = dt("const_f32", [128, 128 + 128], F32, kind="ExternalInput").ap()
        self.d_kctx = dt("kctx", [4, 3, 128, TS], BF16, kind="Internal").ap()
        self.d_vctx = dt("vctx", [4, 3, 128, 16 * 256], BF16, kind="Internal").ap()
        self.d_out = dt("yT", [D, S], F32, kind="ExternalOutput").ap()
        self.wq = []
        self.wissued = 0
        self.wretired = set()
        self.wloaded = {}

    def wplan(self, key, src_aps):
        self.wq.append((key, src_aps))

    def wpump(self):
        T = self.T
        while self.wissued < len(self.wq) and (self.wissued < NSLOT or (self.wissued - NSLOT) in self.wretired):
            i = self.wissued
            slot = i % NSLOT
            key, srcs = self.wq[i]
            assert len(srcs) == 1
            (off, k, n, ap) = srcs[0]
            dst = self.ring[slot][:, off:off + k * n].rearrange("p (k n) -> p k n", k=k)
            T.dma("pool", dst, ap, reads=(), writes=[("ring", slot)])
            self.wissued += 1

    def wretire(self, idx):
        self.wretired.add(idx)
        self.wpump()

    def wget(self, idx):
        self.wpump()
        assert idx < self.wissued, f"weight ring deadlock at piece {idx}"
        return self.ring[idx % NSLOT], ("ring", idx % NSLOT)

    def rmsnorm(self, gname, emit):
        nc, T = self.nc, self.T
        for t in range(NT):
            ts = slice(t * TT, (t + 1) * TT)
            ssb = self.bank[7]
            mms = []
            for c in range(NC8):
                sq = self.sq[c % 2]
                T.op("act", lambda e, c=c, sq=sq: e.activation(out=sq[:], in_=self.xs[:, c, ts], func=AF.Square),
                     reads=[("xs", c, t)], writes=[("sq", c % 2)])
                T._deps("pe", [("sq", c % 2)], [("bank", 7)] if c == 0 else [])
                inst = nc.tensor.matmul(ssb[:], lhsT=self.ones_bf[:], rhs=sq[:], start=(c == 0), stop=(c == NC8 - 1))
                T.cnt["pe"] += 1
                inst.then_inc(T.sem["pe"], 1)
                h = (T.sem["pe"], T.cnt["pe"], "pe", "pe")
                T._commit(h, [("sq", c % 2)], [("bank", 7)] if c == NC8 - 1 else [])
            rs = self.rstd[t % 2]
            T.op("act", lambda e, rs=rs: e.activation(out=rs[:], in_=ssb[:], func=AF.Sqrt, bias=self.eps_rms[:, 0:1], scale=1.0 / D),
                 reads=[("bank", 7)], writes=[("rstd", t % 2)])
            T.op("dve", lambda e, rs=rs: e.reciprocal(out=rs[:], in_=rs[:]),
                 reads=[("rstd", t % 2)], writes=[("rstd", t % 2)])
            emit(t, rs, ("rstd", t % 2))

    def norm_to_h(self, gname):
        T = self.T
        g0 = VOFF[gname]

        def emit(t, rs, rkey):
            ts = slice(t * TT, (t + 1) * TT)
            for c in range(NC8):
                T.op("dve", lambda e, c=c: e.scalar_tensor_tensor(
                    out=self.h[:, c, ts], in0=self.xs[:, c, ts], scalar=self.vec[:, g0 + c:g0 + c + 1],
                    in1=rs[:], op0=ALU.mult, op1=ALU.mult),
                    reads=[("xs", c, t), rkey], writes=[("h", c, t)])
        self.rmsnorm(gname, emit)

    def plan_ffn(self, l):
        groups = [(0, 4), (4, 4), (8, 4), (12, 4), (16, 4), (20, 2)]
        idx = []
        for (j0, nj) in groups:
            a = len(self.wq)
            self.wplan(("wg", l, j0), [(0, 8, nj * 128, self.d_wg[l, :, j0 * 128:(j0 + nj) * 128].rearrange("(k p) n -> p k n", p=128))])
            self.wplan(("wu", l, j0), [(0, 8, nj * 128, self.d_wu[l, :, j0 * 128:(j0 + nj) * 128].rearrange("(k p) n -> p k n", p=128))])
            self.wplan(("wd", l, j0), [(0, nj, D, self.d_wd[l, j0 * 128:(j0 + nj) * 128, :].rearrange("(k p) n -> p k n", p=128))])
            idx.append((j0, nj, a))
        return idx

    def ffn(self, l, widx):
        nc, T = self.nc, self.T
        self.norm_to_h(f"nffn{l}")
        pend = None
        step = 0
        gu_banks = [(0, 1), (2, 3)]
        d_banks = [4, 5]
        dcount = [0]

        def down(st):
            (t, nj, wdk, wdslot, ab, wdi) = st
            ts = slice(t * TT, (t + 1) * TT)
            wdv = wdslot[:, 0:nj * D].rearrange("p (k n) -> p k n", k=nj)
            for m in range(NC8):
                b = d_banks[dcount[0] % 2]
                dcount[0] += 1
                T.mm_group([(self.bank[b][:], wdv[:, j, m * 128:(m + 1) * 128], self.abuf[ab][:, j, :]) for j in range(nj)],
                           reads=[wdk] + [("a", ab, j) for j in range(nj)], writes=[("bank", b)])
                T.op("dve", lambda e, b=b, m=m: e.tensor_tensor(out=self.xs[:, m, ts], in0=self.bank[b][:], in1=self.xs[:, m, ts], op=ALU.add),
                     reads=[("bank", b), ("xs", m, t)], writes=[("xs", m, t)])
            if t == NT - 1:
                self.wretire(wdi)

        for (j0, nj, wi) in widx:
            wgs, wgk = self.wget(wi)
            wus, wuk = self.wget(wi + 1)
            wds, wdk = self.wget(wi + 2)
            wgv = wgs[:, 0:8 * nj * 128].rearrange("p (k n) -> p k n", k=8)
            wuv = wus[:, 0:8 * nj * 128].rearrange("p (k n) -> p k n", k=8)
            for t in range(NT):
                ts = slice(t * TT, (t + 1) * TT)
                ab = step % 2
                for j in range(nj):
                    gb, ub = gu_banks[j % 2]
                    hr = [("h", k, t) for k in range(NC8)]
                    T.mm_group([(self.bank[gb][:], wgv[:, k, j * 128:(j + 1) * 128], self.h[:, k, ts]) for k in range(NC8)],
                               reads=[wgk] + hr, writes=[("bank", gb)])
                    T.mm_group([(self.bank[ub][:], wuv[:, k, j * 128:(j + 1) * 128], self.h[:, k, ts]) for k in range(NC8)],
                               reads=[wuk] + hr, writes=[("bank", ub)])
                    sg = self.sg[j % 2]
                    T.op("act", lambda e, gb=gb, sg=sg: e.activation(out=sg[:], in_=self.bank[gb][:], func=AF.Silu),
                         reads=[("bank", gb)], writes=[("sg", j % 2)])
                    T.op("dve", lambda e, ub=ub, sg=sg, j=j, ab=ab: e.tensor_tensor(out=self.abuf[ab][:, j, :], in0=self.bank[ub][:], in1=sg[:], op=ALU.mult),
                         reads=[("bank", ub), ("sg", j % 2)], writes=[("a", ab, j)])
                if pend is not None:
                    down(pend)
                pend = (t, nj, wdk, wds, ab, wi + 2)
                step += 1
            self.wretire(wi)
            self.wretire(wi + 1)
        down(pend)


    def mm_one(self, out, lhsT, rhs, first, last, reads, wkey):
        T = self.T
        T._deps("pe", reads, [wkey] if first else [])
        inst = self.nc.tensor.matmul(out, lhsT=lhsT, rhs=rhs, start=first, stop=last)
        T.cnt["pe"] += 1
        inst.then_inc(T.sem["pe"], 1)
        h = (T.sem["pe"], T.cnt["pe"], "pe", "pe")
        T._commit(h, reads, [wkey] if last else [])
        return h

    def plan_conv(self, l):
        j = l // 3
        a = len(self.wq)
        for c in range(NC8):
            self.wplan(("cwin", j, c), [(0, 8, 256, self.d_cwin[j, :, c, :].rearrange("(k p) n -> p k n", p=128))])
        for hf in range(2):
            self.wplan(("cwout", j, hf), [(0, 8, 512, self.d_cwout[j, :, hf * 512:(hf + 1) * 512].rearrange("(k p) n -> p k n", p=128))])
        return a

    def conv(self, l, wi, seg):
        nc, T = self.nc, self.T
        j = l // 3
        from contextlib import ExitStack
        self.uid = getattr(self, "uid", 0) + 1
        uid = self.uid
        with ExitStack() as outer:
            O = lambda name, shape, dtype: outer.enter_context(nc.sbuf_tensor(f"{name}_{uid}", shape, dtype))
            u = O("u", [128, NC8, 32 + TS], BF16)
            with ExitStack() as ph:
                B = lambda name, shape, dtype: ph.enter_context(nc.sbuf_tensor(f"{name}_{uid}", shape, dtype))
                self.h = B("h", [128, NC8, TS], BF16)
                sig = [B(f"sig{i}", [128, TT], F32) for i in range(2)]
                self.norm_to_h(f"nmix{l}")
                for c in range(NC8):
                    T.op("dve", lambda e, c=c: e.tensor_copy(out=u[:, c, 0:32], in_=self.ccarry[:, j, c, :]),
                         reads=[("ccarry", j, c)], writes=[("u", c, -1)])
                pair = 0
                for c in range(NC8):
                    ws, wk = self.wget(wi + c)
                    wv = ws[:, 0:2048].rearrange("p (k n) -> p k n", k=8)
                    for t in range(NT):
                        ts = slice(t * TT, (t + 1) * TT)
                        ab, gb = ((0, 1), (2, 3))[pair % 2]
                        pair += 1
                        hr = [("h", k, t) for k in range(NC8)]
                        T.mm_group([(self.bank[ab][:], wv[:, k, 0:128], self.h[:, k, ts]) for k in range(NC8)],
                                   reads=[wk] + hr, writes=[("bank", ab)])
                        T.mm_group([(self.bank[gb][:], wv[:, k, 128:256], self.h[:, k, ts]) for k in range(NC8)],
                                   reads=[wk] + hr, writes=[("bank", gb)])
                        sg = sig[pair % 2]
                        T.op("act", lambda e, gb=gb, sg=sg, c=c: e.activation(
                            out=sg[:], in_=self.bank[gb][:], func=AF.Sigmoid,
                            bias=self.vec[:, VOFF[f"cbg{j}"] + c:VOFF[f"cbg{j}"] + c + 1], scale=1.0),
                            reads=[("bank", gb), "vec"], writes=[("sig", pair % 2)])
                        T.op("dve", lambda e, ab=ab, sg=sg, c=c, t=t: e.scalar_tensor_tensor(
                            out=u[:, c, 32 + t * TT:32 + (t + 1) * TT], in0=self.bank[ab][:],
                            scalar=self.vec[:, VOFF[f"cba{j}"] + c:VOFF[f"cba{j}"] + c + 1], in1=sg[:],
                            op0=ALU.add, op1=ALU.mult),
                            reads=[("bank", ab), ("sig", pair % 2)], writes=[("u", c, t)])
                    self.wretire(wi + c)
                for c in range(NC8):
                    T.op("dve", lambda e, c=c: e.tensor_copy(out=self.ccarry[:, j, c, :], in_=u[:, c, TS:TS + 32]),
                         reads=[("u", c, NT - 1)], writes=[("ccarry", j, c)])
                T.barrier()
            with ExitStack() as ph:
                B = lambda name, shape, dtype: ph.enter_context(nc.sbuf_tensor(f"{name}_{uid}", shape, dtype))
                dg = [B(f"dg{i}", [128, CW, 128], BF16) for i in range(2)]
                v32b = [B(f"v32_{i}", [128, NC8, TT], F32) for i in range(2)]
                vsq = [B(f"vsq{i}", [128, TT], BF16) for i in range(2)]
                z = B("z", [128, NC8, TT], BF16)
                mean = self.rstd[0]
                var = self.rstd[1]
                wdw = B("wdw", [128, NC8, CW], F32)
                T.dma("sp", wdw[:], self.d_cwdw[:, j, :, :], writes=["wdw"])
                wo = [self.wget(wi + 8), self.wget(wi + 9)]
                wov = [w[0][:, 0:4096].rearrange("p (k n) -> p k n", k=8) for w in wo]
                VO = lambda n, c: self.vec[:, VOFF[f"{n}{j}"] + c:VOFF[f"{n}{j}"] + c + 1]
                itc = [0]

                def conv_chunk(t, c):
                    it = itc[0]
                    itc[0] += 1
                    v32 = v32b[t % 2]
                    d = dg[it % 2]
                    for k in range(CW):
                        if k < 19:
                            T.op("dve", lambda e, d=d, k=k, c=c: e.tensor_scalar(
                                out=d[:, k, :], in0=self.ident_bf, scalar1=wdw[:, c, k:k + 1], scalar2=None, op0=ALU.mult),
                                reads=["wdw", "cbf"], writes=[("dg", it % 2, k)])
                        else:
                            T.op("act", lambda e, d=d, k=k, c=c: e.activation(
                                out=d[:, k, :], in_=self.ident_bf, func=AF.Copy, scale=wdw[:, c, k:k + 1]),
                                reads=["wdw", "cbf"], writes=[("dg", it % 2, k)])
                    cb = it % 2
                    base = 32 + t * TT - (CW - 1)
                    T.mm_group([(self.bank[cb][:], d[:, k, :], u[:, c, base + k:base + k + TT]) for k in range(CW)],
                               reads=[("dg", it % 2, k) for k in range(CW)] + [("u", c, t), ("u", c, t - 1)], writes=[("bank", cb)])
                    T.op("act", lambda e, cb=cb, c=c: e.activation(out=v32[:, c, :], in_=self.bank[cb][:], func=AF.Identity,
                                                                   bias=VO("cbdw", c), scale=1.0),
                         reads=[("bank", cb), "vec"], writes=[("v32", t % 2, c)])
                    T.op("act", lambda e, cb=cb, c=c: e.activation(out=vsq[it % 2][:], in_=self.bank[cb][:], func=AF.Square,
                                                                   bias=VO("cbdw", c), scale=1.0),
                         reads=[("bank", cb), "vec"], writes=[("vsq", it % 2)])
                    s1b, s2b = (2, 3) if t % 2 == 0 else (6, 7)
                    self.mm_one(self.bank[s1b][:], self.ones32, v32[:, c, :], c == 0, c == NC8 - 1, [("v32", t % 2, c), "c32"], ("bank", s1b))
                    self.mm_one(self.bank[s2b][:], self.ones_bf, vsq[it % 2][:], c == 0, c == NC8 - 1, [("vsq", it % 2), "cbf"], ("bank", s2b))

                def ln(t):
                    v32 = v32b[t % 2]
                    s1b, s2b = (2, 3) if t % 2 == 0 else (6, 7)
                    T.op("act", lambda e: e.activation(out=mean[:], in_=self.bank[s1b][:], func=AF.Identity, scale=1.0 / D),
                         reads=[("bank", s1b)], writes=["mean"])
                    T.op("dve", lambda e: e.tensor_tensor(out=var[:], in0=mean[:], in1=mean[:], op=ALU.mult),
                         reads=["mean"], writes=["var"])
                    T.op("dve", lambda e: e.scalar_tensor_tensor(out=var[:], in0=self.bank[s2b][:], scalar=1.0 / D, in1=var[:],
                                                                 op0=ALU.mult, op1=ALU.subtract),
                         reads=[("bank", s2b), "var"], writes=["var"])
                    T.op("act", lambda e: e.activation(out=var[:], in_=var[:], func=AF.Sqrt, bias=self.eps_ln[:, 0:1], scale=1.0),
                         reads=["var"], writes=["var"])
                    T.op("dve", lambda e: e.reciprocal(out=var[:], in_=var[:]), reads=["var"], writes=["var"])
                    for c in range(NC8):
                        T.op("dve", lambda e, c=c: e.tensor_tensor(out=v32[:, c, :], in0=v32[:, c, :], in1=mean[:], op=ALU.subtract),
                             reads=[("v32", t % 2, c), "mean"], writes=[("v32", t % 2, c)])
                        T.op("dve", lambda e, c=c: e.tensor_tensor(out=v32[:, c, :], in0=v32[:, c, :], in1=var[:], op=ALU.mult),
                             reads=[("v32", t % 2, c), "var"], writes=[("v32", t % 2, c)])
                        T.op("act", lambda e, c=c: e.activation(out=z[:, c, :], in_=v32[:, c, :], func=AF.Silu,
                                                                bias=VO("clnb", c), scale=VO("clng", c)),
                             reads=[("v32", t % 2, c), "vec"], writes=[("z", c)])

                def outproj(t):
                    ts = slice(t * TT, (t + 1) * TT)
                    for m in range(NC8):
                        ob = 4 + m % 2
                        T.mm_group([(self.bank[ob][:], wov[m // 4][:, c, (m % 4) * 128:(m % 4 + 1) * 128], z[:, c, :]) for c in range(NC8)],
                                   reads=[wo[0][1], wo[1][1]] + [("z", c) for c in range(NC8)], writes=[("bank", ob)])
                        T.op("dve", lambda e, ob=ob, m=m: e.scalar_tensor_tensor(
                            out=self.xs[:, m, ts], in0=self.bank[ob][:], scalar=VO("cbout", m), in1=self.xs[:, m, ts],
                            op0=ALU.add, op1=ALU.add),
                            reads=[("bank", ob), ("xs", m, t), "vec"], writes=[("xs", m, t)])

                for t in range(NT + 1):
                    if t < NT:
                        for c in range(2):
                            conv_chunk(t, c)
                    if t > 0:
                        ln(t - 1)
                    if t < NT:
                        for c in range(2, NC8):
                            conv_chunk(t, c)
                    if t > 0:
                        outproj(t - 1)
                self.wretire(wi + 8)
                self.wretire(wi + 9)
                T.barrier()

    def plan_pool(self, l):
        a = len(self.wq)
        self.wplan(("pw",), [(0, 8, 256, self.d_pw.rearrange("g (kc p) n -> p (g kc) n", p=128))])
        return a

    def pool(self, l, wi, seg):
        nc, T = self.nc, self.T
        from contextlib import ExitStack
        self.uid = getattr(self, "uid", 0) + 1
        uid = self.uid
        WIN = (2, 4, 8, 16)
        with ExitStack() as ph:
            B = lambda name, shape, dtype: ph.enter_context(nc.sbuf_tensor(f"{name}_{uid}", shape, dtype))
            hp = B("hp", [128, NC8, 16 + TT], F32)
            sA = [B(f"sA{i}", [128, 16 + TT], F32) for i in range(2)]
            sB = [B(f"sB{i}", [128, 16 + TT], F32) for i in range(2)]
            pb = B("pb", [128, NC8, TT], BF16)
            pinv = B("pinv", [128, 4, 16], F32)
            T.dma("sp", pinv[:], self.d_pinv[:, :, :], writes=["pinv"])
            ws, wk = self.wget(wi)
            pw = ws[:, 0:2048].rearrange("p (k n) -> p k n", k=8)
            g0 = VOFF[f"nmix{l}"]
            tcount = [0]

            def emit(t, rs, rkey):
                ts = slice(t * TT, (t + 1) * TT)
                first_tile = (seg == 0 and t == 0)
                for c in range(NC8):
                    T.op("dve", lambda e, c=c: e.tensor_copy(out=hp[:, c, 0:16], in_=self.pcarry[:, c, :]),
                         reads=[("pcarry", c)], writes=[("hp", c)])
                    T.op("dve", lambda e, c=c: e.scalar_tensor_tensor(
                        out=hp[:, c, 16:16 + TT], in0=self.xs[:, c, ts], scalar=self.vec[:, g0 + c:g0 + c + 1],
                        in1=rs[:], op0=ALU.mult, op1=ALU.mult),
                        reads=[("xs", c, t), rkey, ("hp", c)], writes=[("hp", c)])
                    T.op("dve", lambda e, c=c: e.tensor_copy(out=self.pcarry[:, c, :], in_=hp[:, c, TT:TT + 16]),
                         reads=[("hp", c)], writes=[("pcarry", c)])
                for c in range(NC8):
                    gi = c // 2
                    w = WIN[gi]
                    a_, b_ = sA[c % 2], sB[c % 2]
                    ka, kb = ("sA", c % 2), ("sB", c % 2)
                    T.op("dve", lambda e, c=c, a_=a_: e.tensor_tensor(out=a_[:, 2:16 + TT], in0=hp[:, c, 2:16 + TT], in1=hp[:, c, 1:15 + TT], op=ALU.add),
                         reads=[("hp", c)], writes=[ka])
                    cur, ck, oth, ok = a_, ka, b_, kb
                    lvl = 2
                    while lvl < w:
                        lo = 2 * lvl
                        T.op("dve", lambda e, cur=cur, oth=oth, lo=lo, lvl=lvl: e.tensor_tensor(
                            out=oth[:, lo:16 + TT], in0=cur[:, lo:16 + TT], in1=cur[:, lo - lvl:16 + TT - lvl], op=ALU.add),
                            reads=[ck], writes=[ok])
                        cur, ck, oth, ok = oth, ok, cur, ck
                        lvl *= 2
                    T.op("dve", lambda e, c=c, cur=cur, w=w: e.scalar_tensor_tensor(
                        out=pb[:, c, :], in0=cur[:, 16:16 + TT], scalar=1.0 / w, in1=hp[:, c, 16:16 + TT],
                        op0=ALU.mult, op1=ALU.subtract),
                        reads=[ck, ("hp", c)], writes=[("pb", c)])
                    if first_tile:
                        T.op("dve", lambda e, cur=cur, gi=gi: e.tensor_tensor(out=cur[:, 16:32], in0=cur[:, 16:32], in1=pinv[:, gi, :], op=ALU.mult),
                             reads=[ck, "pinv", ("pb", c)], writes=[ck])
                        T.op("dve", lambda e, c=c, cur=cur: e.tensor_tensor(out=pb[:, c, 0:16], in0=cur[:, 16:32], in1=hp[:, c, 16:32], op=ALU.subtract),
                             reads=[ck, ("hp", c)], writes=[("pb", c)])
                for m in range(NC8):
                    gi = m // 2
                    ob = 4 + m % 2
                    T.mm_group([(self.bank[ob][:], pw[:, 2 * gi + kc, (m % 2) * 128:(m % 2 + 1) * 128], pb[:, 2 * gi + kc, :]) for kc in range(2)],
                               reads=[wk, ("pb", 2 * gi), ("pb", 2 * gi + 1)], writes=[("bank", ob)])
                    T.op("dve", lambda e, ob=ob, m=m: e.scalar_tensor_tensor(
                        out=self.xs[:, m, ts], in0=self.bank[ob][:], scalar=self.vec[:, VOFF["pscale"] + m:VOFF["pscale"] + m + 1],
                        in1=self.xs[:, m, ts], op0=ALU.mult, op1=ALU.add),
                        reads=[("bank", ob), ("xs", m, t), "vec"], writes=[("xs", m, t)])
            self.rmsnorm(f"nmix{l}", emit)
            self.wretire(wi)
            T.barrier()


    def plan_attn(self, l):
        a = len(self.wq)
        for pr in range(4):
            for g in range(3):
                self.wplan(("wa", pr, g), [(0, 8, 512, self.d_wattn[:, pr, g, 0:512].rearrange("(k p) n -> p k n", p=128))])
                self.wplan(("wv", pr, g), [(0, 8, 128, self.d_wattn[:, pr, g, 512:640].rearrange("(k p) n -> p k n", p=128))])
            self.wplan(("wo", pr), [(0, 1, D, self.d_wo[pr * 128:(pr + 1) * 128, :].rearrange("(k p) n -> p k n", p=128))])
        return a

    def attn(self, l, wi, seg):
        nc, T = self.nc, self.T
        from contextlib import ExitStack
        self.uid = getattr(self, "uid", 0) + 1
        uid = self.uid
        DIL = (1, 4, 16)
        with ExitStack() as ph:
            B = lambda name, shape, dtype: ph.enter_context(nc.sbuf_tensor(f"{name}_{uid}", shape, dtype))
            self.h = B("h", [128, NC8, TS], BF16)
            cs = [B(f"cs{i}", [128, TT], F32) for i in range(2)]
            sn = [B(f"sn{i}", [128, TT], F32) for i in range(2)]
            Qt = B("Qt", [128, TS], BF16)
            Kt = B("Kt", [128, TS], BF16)
            Va = B("Va", [128, 16, 256], BF16)
            kc = B("kc", [128, TS], BF16)
            vc = B("vc", [128, 16, 256], BF16)
            acc = B("acc", [128, 2, TS], F32)
            OT = B("OT", [128, TS], BF16)
            pt = [B(f"pt{i}", [128, 256], BF16) for i in range(2)]
            tm = [B(f"tm{i}", [128, TT], F32) for i in range(2)]
            rd = tm[0]
            self.norm_to_h(f"nmix{l}")
            T.op("pool", lambda e: e.memset(Va[:], 1.0), writes=["Va"])
            T.op("pool", lambda e: e.memset(vc[:], 1.0), writes=["vc"])
            tabn = [0]
            sbn = [0]
            obn = [0]
            ptn = [0]
            pw = wi
            for pr in range(4):
                T.op("pool", lambda e: e.memset(acc[:], 0.0), reads=[], writes=[("acc", 0), ("acc", 1)])
                for g in range(3):
                    d = DIL[g]
                    nblk = TS // (128 * d)
                    run = TS // d
                    wA, wAk = self.wget(pw)
                    wB, wBk = self.wget(pw + 1)
                    wAv = wA[:, 0:4096].rearrange("p (k n) -> p k n", k=8)
                    wBv = wB[:, 0:1024].rearrange("p (k n) -> p k n", k=8)
                    pairn = 0
                    for kind in range(2):
                        dst = Qt if kind == 0 else Kt
                        dkey = "Qt" if kind == 0 else "Kt"
                        off = kind * 256
                        for t in range(NT):
                            ts = slice(t * TT, (t + 1) * TT)
                            ab, bb = ((0, 1), (2, 3))[pairn % 2]
                            pairn += 1
                            hr = [("h", k, t) for k in range(NC8)]
                            T.mm_group([(self.bank[ab][:], wAv[:, k, off:off + 128], self.h[:, k, ts]) for k in range(NC8)],
                                       reads=[wAk] + hr, writes=[("bank", ab)])
                            T.mm_group([(self.bank[bb][:], wAv[:, k, off + 128:off + 256], self.h[:, k, ts]) for k in range(NC8)],
                                       reads=[wAk] + hr, writes=[("bank", bb)])
                            ti = tabn[0] % 2
                            tabn[0] += 1
                            p0 = seg * TS + t * TT
                            T.dma("sp", cs[ti][:], self.d_cos[:, p0:p0 + TT], writes=[("cs", ti)])
                            T.dma("sp", sn[ti][:], self.d_sin[:, p0:p0 + TT], writes=[("sn", ti)])
                            T.op("dve", lambda e, ab=ab, ti=ti: e.tensor_tensor(out=tm[0][:], in0=self.bank[ab][:], in1=cs[ti][:], op=ALU.mult),
                                 reads=[("bank", ab), ("cs", ti)], writes=[("tm", 0)])
                            T.op("dve", lambda e, bb=bb, ti=ti: e.tensor_tensor(out=tm[1][:], in0=self.bank[bb][:], in1=sn[ti][:], op=ALU.mult),
                                 reads=[("bank", bb), ("sn", ti)], writes=[("tm", 1)])
                            ov = dst[:, :].rearrange("p (r i) -> p r i", r=d)[:, :, t * TT // d:(t + 1) * TT // d]
                            i0 = tm[0][:, :].rearrange("p (i r) -> p r i", r=d)
                            i1 = tm[1][:, :].rearrange("p (i r) -> p r i", r=d)
                            T.op("dve", lambda e, ov=ov, i0=i0, i1=i1: e.tensor_tensor(out=ov, in0=i0, in1=i1, op=ALU.add),
                                 reads=[("tm", 0), ("tm", 1)], writes=[(dkey, t)])
                    qk_all = [("Qt", t) for t in range(NT)] + [("Kt", t) for t in range(NT)]
                    for b4 in range(4):
                        vb = 6 + b4 % 2
                        for bi in range(4):
                            bidx = b4 * 4 + bi
                            r, jb = bidx // nblk, bidx % nblk
                            st = jb * 128 * d + r
                            T.mm_group([(self.bank[vb][:, bi * 128:(bi + 1) * 128], self.h[:, k, st:st + 127 * d + 1:d], wBv[:, k, :]) for k in range(NC8)],
                                       reads=[wBk] + [("h", k, tt) for k in range(NC8) for tt in range(NT)],
                                       writes=[("bank", vb)] if bi == 0 else [])
                        T.res[("bank", vb)] = [(T.sem["pe"], T.cnt["pe"], "pe", "pe"), []]
                        src = self.bank[vb][:, :].rearrange("p (b f) -> p b f", b=4)
                        T.op("act", lambda e, src=src, b4=b4: e.activation(out=Va[:, b4 * 4:(b4 + 1) * 4, 0:64], in_=src[:, :, 0:64], func=AF.Identity),
                             reads=[("bank", vb), "Va"], writes=[("Va", b4, 0)])
                        T.op("act", lambda e, src=src, b4=b4: e.activation(out=Va[:, b4 * 4:(b4 + 1) * 4, 192:256], in_=src[:, :, 64:128], func=AF.Identity),
                             reads=[("bank", vb), "Va"], writes=[("Va", b4, 1)])
                    va_all = [("Va", b4, e_) for b4 in range(4) for e_ in range(2)]
                    self.wretire(pw)
                    self.wretire(pw + 1)
                    pw += 2
                    if seg == 0:
                        T.dma("sp", self.d_kctx[pr, g, :, 0:d * 128].rearrange("p (r i) -> p r i", r=d),
                              Kt[:, :].rearrange("p (r i) -> p r i", r=d)[:, :, (nblk - 1) * 128:nblk * 128],
                              reads=qk_all, writes=[("kctx", pr, g)])
                        T.dma("sp", self.d_vctx[pr, g, :, 0:d * 256].rearrange("p (r f) -> p r f", r=d),
                              Va[:, :, :].rearrange("p (r j) f -> p r j f", r=d)[:, :, nblk - 1, :],
                              reads=va_all, writes=[("vctx", pr, g)])
                    else:
                        T.dma("sp", kc[:, 0:d * 128], self.d_kctx[pr, g, :, 0:d * 128], reads=[("kctx", pr, g)], writes=["kc"])
                        T.dma("sp", vc[:, 0:d, :], self.d_vctx[pr, g, :, 0:d * 256].rearrange("p (r f) -> p r f", r=d),
                              reads=[("vctx", pr, g), "vc"], writes=["vc2"])
                    def emit_pv(job):
                        (e_, r, kb, qlo, nq, vlhs, kr, pi) = job
                        for qi in range(nq):
                            qb = qlo + qi
                            closing = (kb == qb)
                            opening = not closing
                            if opening:
                                ob = 2 + obn[0] % 4
                                obn[0] += 1
                                self._ob_cur = ob
                            else:
                                ob = self._ob_cur if (seg == 1 or qb > 0) else None
                                if ob is None:
                                    ob = 2 + obn[0] % 4
                                    obn[0] += 1
                            first = opening or (seg == 0 and qb == 0)
                            T._deps("pe", [("pt", pi)] + kr, [("bank", ob)] if first else [])
                            inst = nc.tensor.matmul(self.bank[ob][:, 0:128], lhsT=vlhs, rhs=pt[pi][:, qi * 128:(qi + 1) * 128],
                                                    start=first, stop=closing)
                            T.cnt["pe"] += 1
                            inst.then_inc(T.sem["pe"], 1)
                            hh = (T.sem["pe"], T.cnt["pe"], "pe", "pe")
                            T._commit(hh, [("pt", pi)], [("bank", ob)] if closing else [])
                            if closing:
                                av = acc[:, e_, :].rearrange("p (i r) -> p r i", r=d)[:, r, qb * 128:(qb + 1) * 128]
                                T.op("dve", lambda e, ob=ob, av=av: e.tensor_tensor(out=av, in0=self.bank[ob][:, 0:128], in1=av, op=ALU.add),
                                     reads=[("bank", ob), ("acc", e_)], writes=[("acc", e_)])

                    pend_pv = None
                    for e_ in range(2):
                        ps = slice(64 * e_, 64 * e_ + 64)
                        for r in range(d):
                            kbs = ([-1] if seg == 1 else []) + list(range(nblk))
                            for kb in kbs:
                                if kb < 0:
                                    klhs = kc[ps, r * 128:(r + 1) * 128]
                                    vlhs = vc[:, r, e_ * 128:(e_ + 1) * 128]
                                    kr = ["kc", "vc2"]
                                    qlo, nq, moff = 0, 1, 128
                                else:
                                    klhs = Kt[ps, r * run + kb * 128:r * run + (kb + 1) * 128]
                                    vlhs = Va[:, r * nblk + kb, e_ * 128:(e_ + 1) * 128]
                                    kr = qk_all + va_all
                                    qlo, nq, moff = kb, (2 if kb + 1 < nblk else 1), 0
                                sb = sbn[0] % 2
                                sbn[0] += 1
                                ncol = nq * 128
                                T.mm_group([(self.bank[sb][:, 0:ncol], klhs, Qt[ps, r * run + qlo * 128:r * run + qlo * 128 + ncol]),
                                            (self.bank[sb][:, 0:ncol], self.ident_bf, self.maskb[:, moff:moff + ncol])],
                                           reads=kr + ["cbf"], writes=[("bank", sb)])
                                pi = ptn[0] % 2
                                ptn[0] += 1
                                T.op("act", lambda e, sb=sb, pi=pi, ncol=ncol: e.activation(out=pt[pi][:, 0:ncol], in_=self.bank[sb][:, 0:ncol], func=AF.Exp, scale=0.125),
                                     reads=[("bank", sb)], writes=[("pt", pi)])
                                if pend_pv is not None:
                                    emit_pv(pend_pv)
                                pend_pv = (e_, r, kb, qlo, nq, vlhs, kr, pi)
                    emit_pv(pend_pv)
                wo, wok = self.wget(pw)
                wov = wo[:, 0:D]
                for t in range(NT):
                    ts = slice(t * TT, (t + 1) * TT)
                    for e_ in range(2):
                        ps = slice(64 * e_, 64 * e_ + 64)
                        T.mm_group([(self.bank[0][:], self.shift32, acc[:, e_, ts])], reads=[("acc", e_), "c32"], writes=[("bank", 0)])
                        T.op("dve", lambda e, ps=ps: e.reciprocal(out=rd[ps, :], in_=self.bank[0][ps, :]),
                             reads=[("bank", 0)], writes=[("tm", 0)])
                        T.op("dve", lambda e, ps=ps, e_=e_: e.tensor_tensor(out=OT[ps, ts], in0=acc[ps, e_, ts], in1=rd[ps, :], op=ALU.mult),
                             reads=[("tm", 0), ("acc", e_)], writes=[("OT", t, e_)])
                    for m in range(NC8):
                        ob = 4 + m % 2
                        T.mm_group([(self.bank[ob][:], wov[:, m * 128:(m + 1) * 128], OT[:, ts])],
                                   reads=[wok, ("OT", t, 0), ("OT", t, 1)], writes=[("bank", ob)])
                        T.op("dve", lambda e, ob=ob, m=m: e.tensor_tensor(out=self.xs[:, m, ts], in0=self.bank[ob][:], in1=self.xs[:, m, ts], op=ALU.add),
                             reads=[("bank", ob), ("xs", m, t)], writes=[("xs", m, t)])
                self.wretire(pw)
                pw += 1
            T.barrier()

    def store_out(self, seg, final_norm):
        T = self.T
        if final_norm:
            g0 = VOFF["nfinal"]

            def emit(t, rs, rkey):
                ts = slice(t * TT, (t + 1) * TT)
                for c in range(NC8):
                    T.op("dve", lambda e, c=c: e.scalar_tensor_tensor(
                        out=self.xs[:, c, ts], in0=self.xs[:, c, ts], scalar=self.vec[:, g0 + c:g0 + c + 1],
                        in1=rs[:], op0=ALU.mult, op1=ALU.mult),
                        reads=[("xs", c, t), rkey], writes=[("xs", c, t)])
            self.rmsnorm("nfinal", emit)
        for c in range(NC8):
            T.dma("sp", self.d_out[c * 128:(c + 1) * 128, seg * TS:(seg + 1) * TS], self.xs[:, c, :],
                  reads=[("xs", c, t) for t in range(NT)], writes=[("out", c, seg)])

    def build(self):
        nc, T = self.nc, self.T
        from contextlib import ExitStack
        with ExitStack() as es:
            A = lambda name, shape, dtype: es.enter_context(nc.sbuf_tensor(name, shape, dtype))
            self.xs = A("xs", [128, NC8, TS], F32)
            self.vec = A("vec_sb", [128, NV], F32)
            self.cbf = A("cbf", [128, 512], BF16)
            self.c32 = A("c32", [128, 256], F32)
            self.eps_rms = A("eps_rms", [128, 1], F32)
            self.eps_ln = A("eps_ln", [128, 1], F32)
            self.ring = [A(f"ring{i}", [128, SLOT], BF16) for i in range(NSLOT)]
            self.sq = [A(f"sq{i}", [128, TT], BF16) for i in range(2)]
            self.rstd = [A(f"rstd{i}", [128, TT], F32) for i in range(2)]
            self.ccarry = A("ccarry", [128, 2, NC8, 32], BF16)
            self.pcarry = A("pcarry", [128, NC8, 16], F32)
            self.bank = [es.enter_context(nc.psum_tensor(f"bank{i}", [128, TT], F32)) for i in range(8)]
            self.ident_bf = self.cbf[:, 0:128]
            self.ones_bf = self.cbf[:, 128:256]
            self.maskb = self.cbf[:, 256:512]
            self.ones32 = self.c32[:, 0:128]
            self.shift32 = self.c32[:, 128:256]

            T.dma("sp", self.vec[:], self.d_vec[:, :], writes=["vec"])
            T.dma("sp", self.cbf[:], self.d_cbf[:, :], writes=["cbf"])
            T.dma("sp", self.c32[:], self.d_c32[:, :], writes=["c32"])
            T.op("dve", lambda e: e.memset(self.eps_rms[:], RMS_EPS), writes=["eps"])
            T.op("dve", lambda e: e.memset(self.eps_ln[:], LN_EPS), writes=["eps2"])
            T.op("dve", lambda e: e.memset(self.ccarry[:], 0.0), writes=["cc0"])
            T.op("dve", lambda e: e.memset(self.pcarry[:], 0.0), writes=["pc0"])
            T.barrier()

            sched = []
            for seg in range(NSEG):
                for (kind, l) in self.plan:
                    if kind == "ffn":
                        sched.append((seg, kind, l, self.plan_ffn(l)))
                    elif kind == "conv":
                        sched.append((seg, kind, l, self.plan_conv(l)))
                    elif kind == "pool":
                        sched.append((seg, kind, l, self.plan_pool(l)))
                    else:
                        sched.append((seg, kind, l, self.plan_attn(l)))

            cur_seg = -1
            for (seg, kind, l, widx) in sched:
                if seg != cur_seg:
                    if cur_seg >= 0:
                        self.store_out(cur_seg, self.final_norm)
                    cur_seg = seg
                    for c in range(NC8):
                        T.dma("sp", self.xs[:, c, :], self.d_xT[c * 128:(c + 1) * 128, seg * TS:(seg + 1) * TS],
                              writes=[("xs", c, t) for t in range(NT)])
                if kind == "ffn":
                    with ExitStack() as ph:
                        self.uid = getattr(self, "uid", 0) + 1
                        B = lambda name, shape, dtype: ph.enter_context(nc.sbuf_tensor(f"{name}_{self.uid}", shape, dtype))
                        self.h = B("h", [128, NC8, TS], BF16)
                        self.abuf = [B(f"abuf{i}", [128, 4, TT], BF16) for i in range(2)]
                        self.sg = [B(f"sg{i}", [128, TT], F32) for i in range(2)]
                        self.ffn(l, widx)
                        T.barrier()
                elif kind == "conv":
                    self.conv(l, widx, seg)
                elif kind == "pool":
                    self.pool(l, widx, seg)
                elif kind == "attn":
                    self.attn(l, widx, seg)
            self.store_out(cur_seg, self.final_norm)
            T.barrier()
        return nc


def _chunked(v):
    v = np.asarray(v, np.float32)
    return np.ascontiguousarray(v.reshape(-1, 128).T)


def prep_shared(inp):
    f32 = np.float32
    vec = np.zeros((128, NV), f32)
    for l in range(DEPTH):
        vec[:, VOFF[f"nmix{l}"]:VOFF[f"nmix{l}"] + 8] = _chunked(inp["norm_mix_g"][l])
        vec[:, VOFF[f"nffn{l}"]:VOFF[f"nffn{l}"] + 8] = _chunked(inp["norm_ffn_g"][l])
    vec[:, VOFF["nfinal"]:VOFF["nfinal"] + 8] = _chunked(inp["final_norm_g"])
    for j in range(2):
        vec[:, VOFF[f"cba{j}"]:VOFF[f"cba{j}"] + 8] = _chunked(inp["conv_b_in"][j][:D])
        vec[:, VOFF[f"cbg{j}"]:VOFF[f"cbg{j}"] + 8] = _chunked(inp["conv_b_in"][j][D:])
        vec[:, VOFF[f"cbdw{j}"]:VOFF[f"cbdw{j}"] + 8] = _chunked(inp["conv_b_dw"][j])
        vec[:, VOFF[f"clng{j}"]:VOFF[f"clng{j}"] + 8] = _chunked(inp["conv_ln_g"][j])
        vec[:, VOFF[f"clnb{j}"]:VOFF[f"clnb{j}"] + 8] = _chunked(inp["conv_ln_b"][j])
        vec[:, VOFF[f"cbout{j}"]:VOFF[f"cbout{j}"] + 8] = _chunked(inp["conv_b_out"][j])
    vec[:, VOFF["pscale"]:VOFF["pscale"] + 8] = _chunked(inp["pool_scale"][0])

    cw = np.asarray(inp["conv_w_in"], f32)
    cwin_r = np.ascontiguousarray(np.stack([cw[:, :, :D].reshape(2, D, 8, 128),
                                            cw[:, :, D:].reshape(2, D, 8, 128)], axis=3).reshape(2, D, 8, 256))
    dw = np.asarray(inp["conv_w_dw"], f32)
    cwdw_r = np.ascontiguousarray(dw.reshape(2, CW, 8, 128).transpose(3, 0, 2, 1))

    pinv = np.zeros((128, 4, 16), f32)
    for g, w in enumerate((2, 4, 8, 16)):
        pinv[:, g, :] = 1.0 / np.minimum(np.arange(1, 17), w).astype(f32)

    wqkv = np.asarray(inp["attn_w_qkv"], f32)[0]
    wq = wqkv[:, 0:1536].reshape(D, 3, 8, 64)
    wk = wqkv[:, 1536:3072].reshape(D, 3, 8, 64)
    wv = wqkv[:, 3072:4608].reshape(D, 3, 8, 64)
    swap = np.concatenate([np.arange(32, 64), np.arange(0, 32)])
    w_r = np.zeros((D, 4, 3, 5, 2, 64), f32)
    for pr in range(4):
        for g in range(3):
            for e in range(2):
                hs = 2 * pr + e
                w_r[:, pr, g, 0, e] = wq[:, g, hs]
                w_r[:, pr, g, 1, e] = wq[:, g, hs][:, swap]
                w_r[:, pr, g, 2, e] = wk[:, g, hs]
                w_r[:, pr, g, 3, e] = wk[:, g, hs][:, swap]
                w_r[:, pr, g, 4, e] = wv[:, g, hs]
    w_r = np.ascontiguousarray(w_r.reshape(D, 4, 3, 640))

    half = 32
    inv_freq = (10000.0 ** (-np.arange(half, dtype=f32) / half)).astype(f32)
    ang = np.arange(S, dtype=f32)[None, :] * inv_freq[:, None]
    cos = np.cos(ang).astype(f32)
    sin = np.sin(ang).astype(f32)
    cos_t = np.concatenate([cos, cos, cos, cos], axis=0)
    sin_t = np.concatenate([-sin, sin, -sin, sin], axis=0)

    cbf = np.zeros((128, 512), f32)
    cbf[:, 0:128] = np.eye(128, dtype=f32)
    cbf[:, 128:256] = 1.0
    a = np.arange(128)[:, None]
    b = np.arange(256)[None, :]
    cbf[:, 256:512] = np.where((b >= a) & (b <= a + 128), 0.0, -30000.0)
    c32 = np.zeros((128, 256), f32)
    c32[:, 0:128] = 1.0
    c32[:, 128:256] = np.roll(np.eye(128, dtype=f32), 64, axis=0)

    return {
        "vec": vec,
        "ffn_w_gate": np.ascontiguousarray(inp["ffn_w_gate"], f32),
        "ffn_w_up": np.ascontiguousarray(inp["ffn_w_up"], f32),
        "ffn_w_down": np.ascontiguousarray(inp["ffn_w_down"], f32),
        "conv_w_in_r": cwin_r,
        "conv_w_out": np.ascontiguousarray(inp["conv_w_out"], f32),
        "conv_w_dw_r": cwdw_r,
        "pool_w": np.ascontiguousarray(inp["pool_w"], f32)[0],
        "pool_inv": pinv,
        "attn_w_r": w_r,
        "attn_w_o": np.ascontiguousarray(inp["attn_w_o"], f32)[0],
        "rope_cos": np.ascontiguousarray(cos_t),
        "rope_sin": np.ascontiguousarray(sin_t),
        "const_bf": cbf.astype(ml_dtypes.bfloat16),
        "const_f32": c32,
    }


FULL_PLAN = [("conv", 0), ("ffn", 0), ("attn", 1), ("ffn", 1), ("pool", 2), ("ffn", 2), ("conv", 3), ("ffn", 3)]


def run(inputs, plan=FULL_PLAN, final_norm=True, cores=8):
    inp = {k: np.asarray(v) for k, v in inputs.items()}
    shared = prep_shared(inp)
    x = np.asarray(inp["x"], np.float32)
    in_maps = []
    for b in range(cores):
        m = dict(shared)
        m["xT"] = np.ascontiguousarray(x[b].T)
        in_maps.append(m)
    nc = Prog(plan, final_norm).build()
    res = run_bass_kernel_spmd(nc, in_maps, core_ids=list(range(cores)))
    out = np.stack([np.ascontiguousarray(r["yT"].T) for r in res.results], axis=0)
    return out.astype(np.float32)


def kernel(**inputs):
    return run(inputs)
```

```python
import numpy as np
import ml_dtypes
import concourse.bass as bass
import concourse.mybir as mybir
from concourse.bass_utils import run_bass_kernel_spmd

F32 = mybir.dt.float32
BF16 = mybir.dt.bfloat16
AF = mybir.ActivationFunctionType
ALU = mybir.AluOpType

D = 1024
S = 4096
NC8 = 8
TS = 2048
NSEG = S // TS
TT = 512
NT = TS // TT
FH = 2816
NHC = FH // 128
DEPTH = 4
CW = 31
RMS_EPS = 1e-6
LN_EPS = 1e-5
SLOT = 4096
NSLOT = 5

VEC_NAMES = []
for _l in range(DEPTH):
    VEC_NAMES += [f"nmix{_l}", f"nffn{_l}"]
VEC_NAMES += ["nfinal"]
for _j in range(2):
    VEC_NAMES += [f"cba{_j}", f"cbg{_j}", f"cbdw{_j}", f"clng{_j}", f"clnb{_j}", f"cbout{_j}"]
VEC_NAMES += ["pscale"]
VOFF = {n: 8 * i for i, n in enumerate(VEC_NAMES)}
NV = 8 * len(VEC_NAMES)


class Tracker:
    def __init__(self, nc):
        self.nc = nc
        self.engs = {"pe": nc.tensor, "act": nc.scalar, "dve": nc.vector,
                     "pool": nc.gpsimd, "sp": nc.sync}
        self.sem = {k: nc.alloc_semaphore(f"prog_{k}") for k in ("pe", "act", "dve", "pool")}
        self.cnt = {k: 0 for k in self.sem}
        self.waited = {k: {} for k in self.engs}
        self.dpool = {}
        for q in ("sp", "pool"):
            self.dpool[q] = [[nc.alloc_semaphore(f"dma_{q}_{i}"), 0] for i in range(12)]
        self.dnext = {"sp": 0, "pool": 0}
        self.res = {}
        self.nwaits = 0

    def _wait(self, eng, h):
        if h is None:
            return
        sem, val, src, key = h
        if eng == "pe" and src == "pe":
            return
        if self.waited[eng].get(key, 0) >= val:
            return
        self.engs[eng].wait_ge(sem, val)
        self.waited[eng][key] = val
        self.nwaits += 1

    def _deps(self, eng, reads, writes):
        for k in reads:
            r = self.res.get(k)
            if r is not None:
                self._wait(eng, r[0])
        for k in writes:
            r = self.res.get(k)
            if r is not None:
                self._wait(eng, r[0])
                for h in r[1]:
                    self._wait(eng, h)

    def _commit(self, h, reads, writes):
        for k in reads:
            r = self.res.setdefault(k, [None, []])
            if h[2] in ("pe", "act", "dve", "pool") and h[3] == h[2]:
                r[1] = [x for x in r[1] if x[3] != h[3]]
            r[1].append(h)
        for k in writes:
            self.res[k] = [h, []]

    def op(self, eng, fn, reads=(), writes=()):
        self._deps(eng, reads, writes)
        inst = fn(self.engs[eng])
        self.cnt[eng] += 1
        inst.then_inc(self.sem[eng], 1)
        h = (self.sem[eng], self.cnt[eng], eng, eng)
        self._commit(h, reads, writes)
        return h

    def mm_group(self, mms, reads=(), writes=()):
        self._deps("pe", reads, writes)
        n = len(mms)
        inst = None
        for i, (o, l, r) in enumerate(mms):
            inst = self.nc.tensor.matmul(o, lhsT=l, rhs=r, start=(i == 0), stop=(i == n - 1))
        self.cnt["pe"] += 1
        inst.then_inc(self.sem["pe"], 1)
        h = (self.sem["pe"], self.cnt["pe"], "pe", "pe")
        self._commit(h, reads, writes)
        return h

    def dma(self, q, out, in_, reads=(), writes=()):
        pool = self.dpool[q]
        i = self.dnext[q]
        self.dnext[q] = (i + 1) % len(pool)
        sem, c = pool[i]
        key = f"d{q}{i}"
        self._wait(q, (sem, c, q, key))
        self._deps(q, reads, writes)
        self.engs[q].dma_start(out=out, in_=in_).then_inc(sem, 16)
        pool[i][1] = c + 16
        h = (sem, c + 16, q, key)
        self._commit(h, reads, writes)
        return h

    def barrier(self):
        hs = [(self.sem[k], self.cnt[k], k, k) for k in self.sem if self.cnt[k] > 0]
        for q in ("sp", "pool"):
            for i, (sem, c) in enumerate(self.dpool[q]):
                if c > 0:
                    hs.append((sem, c, q, f"d{q}{i}"))
        for e in ("pe", "act", "dve", "pool", "sp"):
            for h in hs:
                if e == "pe" and h[2] == "pe":
                    continue
                self._wait(e, h)
        self.res = {}


class Prog:
    def __init__(self, plan, final_norm=True):
        self.plan = plan
        self.final_norm = final_norm
        nc = bass.Bass("TRN2", target_bir_lowering=False)
        self.nc = nc
        self.T = Tracker(nc)
        dt = nc.dram_tensor
        self.d_xT = dt("xT", [D, S], F32, kind="ExternalInput").ap()
        self.d_vec = dt("vec", [128, NV], F32, kind="ExternalInput").ap()
        self.d_wg = dt("ffn_w_gate", [DEPTH, D, FH], F32, kind="ExternalInput").ap()
        self.d_wu = dt("ffn_w_up", [DEPTH, D, FH], F32, kind="ExternalInput").ap()
        self.d_wd = dt("ffn_w_down", [DEPTH, FH, D], F32, kind="ExternalInput").ap()
        self.d_cwin = dt("conv_w_in_r", [2, D, 8, 256], F32, kind="ExternalInput").ap()
        self.d_cwout = dt("conv_w_out", [2, D, D], F32, kind="ExternalInput").ap()
        self.d_cwdw = dt("conv_w_dw_r", [128, 2, 8, CW], F32, kind="ExternalInput").ap()
        self.d_pw = dt("pool_w", [4, 256, 256], F32, kind="ExternalInput").ap()
        self.d_pinv = dt("pool_inv", [128, 4, 16], F32, kind="ExternalInput").ap()
        self.d_wattn = dt("attn_w_r", [D, 4, 3, 640], F32, kind="ExternalInput").ap()
        self.d_wo = dt("attn_w_o", [512, D], F32, kind="ExternalInput").ap()
        self.d_cos = dt("rope_cos", [128, S], F32, kind="ExternalInput").ap()
        self.d_sin = dt("rope_sin", [128, S], F32, kind="ExternalInput").ap()
        self.d_cbf = dt("const_bf", [128, 128 + 128 + 256], BF16, kind="ExternalInput").ap()
        self.d_c32 = dt("const_f32", [128, 128 + 128], F32, kind="ExternalInput").ap()
        self.d_kctx = dt("kctx", [4, 3, 128, TS], BF16, kind="Internal").ap()
        self.d_vctx = dt("vctx", [4, 3, 128, 16 * 256], BF16, kind="Internal").ap()
        self.d_out = dt("yT", [D, S], F32, kind="ExternalOutput").ap()
        self.wq = []
        self.wissued = 0
        self.wretired = set()
        self.wloaded = {}

    def wplan(self, key, src_aps):
        self.wq.append((key, src_aps))

    def wpump(self):
        T = self.T
        while self.wissued < len(self.wq) and (self.wissued < NSLOT or (self.wissued - NSLOT) in self.wretired):
            i = self.wissued
            slot = i % NSLOT
            key, srcs = self.wq[i]
            assert len(srcs) == 1
            (off, k, n, ap) = srcs[0]
            dst = self.ring[slot][:, off:off + k * n].rearrange("p (k n) -> p k n", k=k)
            T.dma("pool", dst, ap, reads=(), writes=[("ring", slot)])
            self.wissued += 1

    def wretire(self, idx):
        self.wretired.add(idx)
        self.wpump()

    def wget(self, idx):
        self.wpump()
        assert idx < self.wissued, f"weight ring deadlock at piece {idx}"
        return self.ring[idx % NSLOT], ("ring", idx % NSLOT)

    def rmsnorm(self, gname, emit):
        for t in range(NT):
            self.rmsnorm_tile(t, emit)

    def rmsnorm_tile(self, t, emit):
        nc, T = self.nc, self.T
        ts = slice(t * TT, (t + 1) * TT)
        ssb = self.bank[7]
        for c in range(NC8):
            sq = self.sq[c % 2]
            T.op("act", lambda e, c=c, sq=sq: e.activation(out=sq[:], in_=self.xs[:, c, ts], func=AF.Square),
                 reads=[("xs", c, t)], writes=[("sq", c % 2)])
            self.mm_one(ssb[:], self.ones_bf[:], sq[:], c == 0, c == NC8 - 1, [("sq", c % 2)], ("bank", 7))
        rs = self.rstd[t % 2]
        T.op("act", lambda e, rs=rs: e.activation(out=rs[:], in_=ssb[:], func=AF.Sqrt, bias=self.eps_rms[:, 0:1], scale=1.0 / D),
             reads=[("bank", 7)], writes=[("rstd", t % 2)])
        T.op("dve", lambda e, rs=rs: e.reciprocal(out=rs[:], in_=rs[:]),
             reads=[("rstd", t % 2)], writes=[("rstd", t % 2)])
        emit(t, rs, ("rstd", t % 2))

    def h_emit(self, gname):
        T = self.T
        g0 = VOFF[gname]

        def emit(t, rs, rkey):
            ts = slice(t * TT, (t + 1) * TT)
            for c in range(NC8):
                T.op("dve", lambda e, c=c: e.scalar_tensor_tensor(
                    out=self.h[:, c, ts], in0=self.xs[:, c, ts], scalar=self.vec[:, g0 + c:g0 + c + 1],
                    in1=rs[:], op0=ALU.mult, op1=ALU.mult),
                    reads=[("xs", c, t), rkey], writes=[("h", c, t)])
        return emit

    def norm_to_h(self, gname):
        self.rmsnorm(gname, self.h_emit(gname))

    def plan_ffn(self, l):
        groups = [(0, 4), (4, 4), (8, 4), (12, 4), (16, 4), (20, 2)]
        idx = []
        for (j0, nj) in groups:
            a = len(self.wq)
            self.wplan(("wg", l, j0), [(0, 8, nj * 128, self.d_wg[l, :, j0 * 128:(j0 + nj) * 128].rearrange("(k p) n -> p k n", p=128))])
            self.wplan(("wu", l, j0), [(0, 8, nj * 128, self.d_wu[l, :, j0 * 128:(j0 + nj) * 128].rearrange("(k p) n -> p k n", p=128))])
            self.wplan(("wd", l, j0), [(0, nj, D, self.d_wd[l, j0 * 128:(j0 + nj) * 128, :].rearrange("(k p) n -> p k n", p=128))])
            idx.append((j0, nj, a))
        return idx

    def ffn(self, l, widx):
        nc, T = self.nc, self.T
        hemit = self.h_emit(f"nffn{l}")
        self.rmsnorm_tile(0, hemit)
        self.rmsnorm_tile(1, hemit)
        first_group = True
        pend = None
        step = 0
        gu_banks = [(0, 1), (2, 3)]
        d_banks = [4, 5]
        dcount = [0]

        def down(st):
            (t, nj, wdk, wdslot, ab, wdi) = st
            ts = slice(t * TT, (t + 1) * TT)
            wdv = wdslot[:, 0:nj * D].rearrange("p (k n) -> p k n", k=nj)
            for m in range(NC8):
                b = d_banks[dcount[0] % 2]
                dcount[0] += 1
                T.mm_group([(self.bank[b][:], wdv[:, j, m * 128:(m + 1) * 128], self.abuf[ab][:, j, :]) for j in range(nj)],
                           reads=[wdk] + [("a", ab, j) for j in range(nj)], writes=[("bank", b)])
                T.op("dve", lambda e, b=b, m=m: e.tensor_tensor(out=self.xs[:, m, ts], in0=self.bank[b][:], in1=self.xs[:, m, ts], op=ALU.add),
                     reads=[("bank", b), ("xs", m, t)], writes=[("xs", m, t)])
            if t == NT - 1:
                self.wretire(wdi)

        for (j0, nj, wi) in widx:
            wgs, wgk = self.wget(wi)
            wus, wuk = self.wget(wi + 1)
            wds, wdk = self.wget(wi + 2)
            wgv = wgs[:, 0:8 * nj * 128].rearrange("p (k n) -> p k n", k=8)
            wuv = wus[:, 0:8 * nj * 128].rearrange("p (k n) -> p k n", k=8)
            for t in range(NT):
                ts = slice(t * TT, (t + 1) * TT)
                ab = step % 2
                for j in range(nj):
                    gb, ub = gu_banks[j % 2]
                    hr = [("h", k, t) for k in range(NC8)]
                    T.mm_group([(self.bank[gb][:], wgv[:, k, j * 128:(j + 1) * 128], self.h[:, k, ts]) for k in range(NC8)],
                               reads=[wgk] + hr, writes=[("bank", gb)])
                    T.mm_group([(self.bank[ub][:], wuv[:, k, j * 128:(j + 1) * 128], self.h[:, k, ts]) for k in range(NC8)],
                               reads=[wuk] + hr, writes=[("bank", ub)])
                    sg = self.sg[j % 2]
                    T.op("act", lambda e, gb=gb, sg=sg: e.activation(out=sg[:], in_=self.bank[gb][:], func=AF.Silu),
                         reads=[("bank", gb)], writes=[("sg", j % 2)])
                    T.op("dve", lambda e, ub=ub, sg=sg, j=j, ab=ab: e.tensor_tensor(out=self.abuf[ab][:, j, :], in0=self.bank[ub][:], in1=sg[:], op=ALU.mult),
                         reads=[("bank", ub), ("sg", j % 2)], writes=[("a", ab, j)])
                if first_group and t + 2 < NT:
                    self.rmsnorm_tile(t + 2, hemit)
                if pend is not None:
                    down(pend)
                pend = (t, nj, wdk, wds, ab, wi + 2)
                step += 1
            first_group = False
            self.wretire(wi)
            self.wretire(wi + 1)
        down(pend)


    def mm_one(self, out, lhsT, rhs, first, last, reads, wkey):
        T = self.T
        T._deps("pe", reads, [wkey] if first else [])
        inst = self.nc.tensor.matmul(out, lhsT=lhsT, rhs=rhs, start=first, stop=last)
        T.cnt["pe"] += 1
        inst.then_inc(T.sem["pe"], 1)
        h = (T.sem["pe"], T.cnt["pe"], "pe", "pe")
        T._commit(h, reads, [wkey] if last else [])
        return h

    def plan_conv(self, l):
        j = l // 3
        a = len(self.wq)
        for c in range(NC8):
            self.wplan(("cwin", j, c), [(0, 8, 256, self.d_cwin[j, :, c, :].rearrange("(k p) n -> p k n", p=128))])
        for hf in range(2):
            self.wplan(("cwout", j, hf), [(0, 8, 512, self.d_cwout[j, :, hf * 512:(hf + 1) * 512].rearrange("(k p) n -> p k n", p=128))])
        return a

    def conv(self, l, wi, seg):
        nc, T = self.nc, self.T
        j = l // 3
        from contextlib import ExitStack
        self.uid = getattr(self, "uid", 0) + 1
        uid = self.uid
        with ExitStack() as outer:
            O = lambda name, shape, dtype: outer.enter_context(nc.sbuf_tensor(f"{name}_{uid}", shape, dtype))
            u = O("u", [128, NC8, 32 + TS], BF16)
            with ExitStack() as ph:
                B = lambda name, shape, dtype: ph.enter_context(nc.sbuf_tensor(f"{name}_{uid}", shape, dtype))
                self.h = B("h", [128, NC8, TS], BF16)
                sig = [B(f"sig{i}", [128, TT], F32) for i in range(2)]
                hemit = self.h_emit(f"nmix{l}")
                self.rmsnorm_tile(0, hemit)
                self.rmsnorm_tile(1, hemit)
                for c in range(NC8):
                    T.op("dve", lambda e, c=c: e.tensor_copy(out=u[:, c, 0:32], in_=self.ccarry[:, j, c, :]),
                         reads=[("ccarry", j, c)], writes=[("u", c, -1)])
                pair = 0
                for c in range(NC8):
                    ws, wk = self.wget(wi + c)
                    wv = ws[:, 0:2048].rearrange("p (k n) -> p k n", k=8)
                    for t in range(NT):
                        ts = slice(t * TT, (t + 1) * TT)
                        ab, gb = ((0, 1), (2, 3))[pair % 2]
                        pair += 1
                        hr = [("h", k, t) for k in range(NC8)]
                        T.mm_group([(self.bank[ab][:], wv[:, k, 0:128], self.h[:, k, ts]) for k in range(NC8)],
                                   reads=[wk] + hr, writes=[("bank", ab)])
                        T.mm_group([(self.bank[gb][:], wv[:, k, 128:256], self.h[:, k, ts]) for k in range(NC8)],
                                   reads=[wk] + hr, writes=[("bank", gb)])
                        sg = sig[pair % 2]
                        T.op("act", lambda e, gb=gb, sg=sg, c=c: e.activation(
                            out=sg[:], in_=self.bank[gb][:], func=AF.Sigmoid,
                            bias=self.vec[:, VOFF[f"cbg{j}"] + c:VOFF[f"cbg{j}"] + c + 1], scale=1.0),
                            reads=[("bank", gb), "vec"], writes=[("sig", pair % 2)])
                        T.op("dve", lambda e, ab=ab, sg=sg, c=c, t=t: e.scalar_tensor_tensor(
                            out=u[:, c, 32 + t * TT:32 + (t + 1) * TT], in0=self.bank[ab][:],
                            scalar=self.vec[:, VOFF[f"cba{j}"] + c:VOFF[f"cba{j}"] + c + 1], in1=sg[:],
                            op0=ALU.add, op1=ALU.mult),
                            reads=[("bank", ab), ("sig", pair % 2)], writes=[("u", c, t)])
                        if c == 0 and t + 2 < NT:
                            self.rmsnorm_tile(t + 2, hemit)
                    self.wretire(wi + c)
                for c in range(NC8):
                    T.op("dve", lambda e, c=c: e.tensor_copy(out=self.ccarry[:, j, c, :], in_=u[:, c, TS:TS + 32]),
                         reads=[("u", c, NT - 1)], writes=[("ccarry", j, c)])
                T.barrier()
            with ExitStack() as ph:
                B = lambda name, shape, dtype: ph.enter_context(nc.sbuf_tensor(f"{name}_{uid}", shape, dtype))
                dg = [B(f"dg{i}", [128, CW, 128], BF16) for i in range(3)]
                v32 = B("v32", [128, 12, TT], F32)
                vslot = lambda t, c: ((t % 2) * 4 + c) if c < 4 else (8 + c - 4)
                vsq = self.sq
                z = B("z", [128, NC8, TT], BF16)
                mean = self.rstd[0]
                var = self.rstd[1]
                wdw = B("wdw", [128, NC8, CW], F32)
                T.dma("sp", wdw[:], self.d_cwdw[:, j, :, :], writes=["wdw"])
                wo = [self.wget(wi + 8), self.wget(wi + 9)]
                wov = [w[0][:, 0:4096].rearrange("p (k n) -> p k n", k=8) for w in wo]
                VO = lambda n, c: self.vec[:, VOFF[f"{n}{j}"] + c:VOFF[f"{n}{j}"] + c + 1]
                itc = [0]

                def conv_chunk(t, c):
                    it = itc[0]
                    itc[0] += 1
                    vs = vslot(t, c)
                    d = dg[it % 3]
                    for k in range(CW):
                        if k < 24:
                            T.op("dve", lambda e, d=d, k=k, c=c: e.tensor_scalar(
                                out=d[:, k, :], in0=self.ident_bf, scalar1=wdw[:, c, k:k + 1], scalar2=None, op0=ALU.mult),
                                reads=["wdw", "cbf"], writes=[("dg", it % 3, k)])
                        else:
                            T.op("act", lambda e, d=d, k=k, c=c: e.activation(
                                out=d[:, k, :], in_=self.ident_bf, func=AF.Copy, scale=wdw[:, c, k:k + 1]),
                                reads=["wdw", "cbf"], writes=[("dg", it % 3, k)])
                    cb = it % 2
                    base = 32 + t * TT - (CW - 1)
                    T.mm_group([(self.bank[cb][:], d[:, k, :], u[:, c, base + k:base + k + TT]) for k in range(CW)],
                               reads=[("dg", it % 3, k) for k in range(CW)] + [("u", c, t), ("u", c, t - 1)], writes=[("bank", cb)])
                    T.op("act", lambda e, cb=cb, c=c, vs=vs: e.activation(out=v32[:, vs, :], in_=self.bank[cb][:], func=AF.Identity,
                                                                   bias=VO("cbdw", c), scale=1.0),
                         reads=[("bank", cb), "vec"], writes=[("v32", vs)])
                    T.op("act", lambda e, cb=cb, c=c: e.activation(out=vsq[it % 2][:], in_=self.bank[cb][:], func=AF.Square,
                                                                   bias=VO("cbdw", c), scale=1.0),
                         reads=[("bank", cb), "vec"], writes=[("vsq", it % 2)])
                    s1b, s2b = (2, 3) if t % 2 == 0 else (6, 7)
                    self.mm_one(self.bank[s1b][:], self.ones32, v32[:, vs, :], c == 0, c == NC8 - 1, [("v32", vs), "c32"], ("bank", s1b))
                    self.mm_one(self.bank[s2b][:], self.ones_bf, vsq[it % 2][:], c == 0, c == NC8 - 1, [("vsq", it % 2), "cbf"], ("bank", s2b))

                def ln(t):
                    s1b, s2b = (2, 3) if t % 2 == 0 else (6, 7)
                    T.op("act", lambda e: e.activation(out=mean[:], in_=self.bank[s1b][:], func=AF.Identity, scale=1.0 / D),
                         reads=[("bank", s1b)], writes=["mean"])
                    T.op("dve", lambda e: e.tensor_tensor(out=var[:], in0=mean[:], in1=mean[:], op=ALU.mult),
                         reads=["mean"], writes=["var"])
                    T.op("dve", lambda e: e.scalar_tensor_tensor(out=var[:], in0=self.bank[s2b][:], scalar=1.0 / D, in1=var[:],
                                                                 op0=ALU.mult, op1=ALU.subtract),
                         reads=[("bank", s2b), "var"], writes=["var"])
                    T.op("act", lambda e: e.activation(out=var[:], in_=var[:], func=AF.Sqrt, bias=self.eps_ln[:, 0:1], scale=1.0),
                         reads=["var"], writes=["var"])
                    T.op("dve", lambda e: e.reciprocal(out=var[:], in_=var[:]), reads=["var"], writes=["var"])
                    for c in range(NC8):
                        vs = vslot(t, c)
                        T.op("dve", lambda e, vs=vs: e.tensor_tensor(out=v32[:, vs, :], in0=v32[:, vs, :], in1=mean[:], op=ALU.subtract),
                             reads=[("v32", vs), "mean"], writes=[("v32", vs)])
                        T.op("dve", lambda e, vs=vs: e.tensor_tensor(out=v32[:, vs, :], in0=v32[:, vs, :], in1=var[:], op=ALU.mult),
                             reads=[("v32", vs), "var"], writes=[("v32", vs)])
                        T.op("act", lambda e, c=c, vs=vs: e.activation(out=z[:, c, :], in_=v32[:, vs, :], func=AF.Silu,
                                                                bias=VO("clnb", c), scale=VO("clng", c)),
                             reads=[("v32", vs), "vec"], writes=[("z", c)])

                def outproj(t):
                    ts = slice(t * TT, (t + 1) * TT)
                    for m in range(NC8):
                        ob = 4 + m % 2
                        T.mm_group([(self.bank[ob][:], wov[m // 4][:, c, (m % 4) * 128:(m % 4 + 1) * 128], z[:, c, :]) for c in range(NC8)],
                                   reads=[wo[0][1], wo[1][1]] + [("z", c) for c in range(NC8)], writes=[("bank", ob)])
                        T.op("dve", lambda e, ob=ob, m=m: e.scalar_tensor_tensor(
                            out=self.xs[:, m, ts], in0=self.bank[ob][:], scalar=VO("cbout", m), in1=self.xs[:, m, ts],
                            op0=ALU.add, op1=ALU.add),
                            reads=[("bank", ob), ("xs", m, t), "vec"], writes=[("xs", m, t)])

                for t in range(NT + 1):
                    if t < NT:
                        for c in range(2):
                            conv_chunk(t, c)
                    if t > 0:
                        ln(t - 1)
                    if t < NT:
                        for c in range(2, NC8):
                            conv_chunk(t, c)
                    if t > 0:
                        outproj(t - 1)
                self.wretire(wi + 8)
                self.wretire(wi + 9)
                T.barrier()

    def plan_pool(self, l):
        a = len(self.wq)
        self.wplan(("pw",), [(0, 8, 256, self.d_pw.rearrange("g (kc p) n -> p (g kc) n", p=128))])
        return a

    def pool(self, l, wi, seg):
        nc, T = self.nc, self.T
        from contextlib import ExitStack
        self.uid = getattr(self, "uid", 0) + 1
        uid = self.uid
        WIN = (2, 4, 8, 16)
        with ExitStack() as ph:
            B = lambda name, shape, dtype: ph.enter_context(nc.sbuf_tensor(f"{name}_{uid}", shape, dtype))
            hp = B("hp", [128, NC8, 16 + TT], F32)
            sA = [B(f"sA{i}", [128, 16 + TT], F32) for i in range(2)]
            sB = [B(f"sB{i}", [128, 16 + TT], F32) for i in range(2)]
            pb = B("pb", [128, NC8, TT], BF16)
            pinv = B("pinv", [128, 4, 16], F32)
            T.dma("sp", pinv[:], self.d_pinv[:, :, :], writes=["pinv"])
            ws, wk = self.wget(wi)
            pw = ws[:, 0:2048].rearrange("p (k n) -> p k n", k=8)
            g0 = VOFF[f"nmix{l}"]
            tcount = [0]

            def emit(t, rs, rkey):
                ts = slice(t * TT, (t + 1) * TT)
                first_tile = (seg == 0 and t == 0)
                for c in range(NC8):
                    T.op("dve", lambda e, c=c: e.tensor_copy(out=hp[:, c, 0:16], in_=self.pcarry[:, c, :]),
                         reads=[("pcarry", c)], writes=[("hp", c)])
                    T.op("dve", lambda e, c=c: e.scalar_tensor_tensor(
                        out=hp[:, c, 16:16 + TT], in0=self.xs[:, c, ts], scalar=self.vec[:, g0 + c:g0 + c + 1],
                        in1=rs[:], op0=ALU.mult, op1=ALU.mult),
                        reads=[("xs", c, t), rkey, ("hp", c)], writes=[("hp", c)])
                    T.op("dve", lambda e, c=c: e.tensor_copy(out=self.pcarry[:, c, :], in_=hp[:, c, TT:TT + 16]),
                         reads=[("hp", c)], writes=[("pcarry", c)])
                for c in range(NC8):
                    gi = c // 2
                    w = WIN[gi]
                    a_, b_ = sA[c % 2], sB[c % 2]
                    ka, kb = ("sA", c % 2), ("sB", c % 2)
                    T.op("dve", lambda e, c=c, a_=a_: e.tensor_tensor(out=a_[:, 2:16 + TT], in0=hp[:, c, 2:16 + TT], in1=hp[:, c, 1:15 + TT], op=ALU.add),
                         reads=[("hp", c)], writes=[ka])
                    cur, ck, oth, ok = a_, ka, b_, kb
                    lvl = 2
                    while lvl < w:
                        lo = 2 * lvl
                        T.op("dve", lambda e, cur=cur, oth=oth, lo=lo, lvl=lvl: e.tensor_tensor(
                            out=oth[:, lo:16 + TT], in0=cur[:, lo:16 + TT], in1=cur[:, lo - lvl:16 + TT - lvl], op=ALU.add),
                            reads=[ck], writes=[ok])
                        cur, ck, oth, ok = oth, ok, cur, ck
                        lvl *= 2
                    T.op("dve", lambda e, c=c, cur=cur, w=w: e.scalar_tensor_tensor(
                        out=pb[:, c, :], in0=cur[:, 16:16 + TT], scalar=1.0 / w, in1=hp[:, c, 16:16 + TT],
                        op0=ALU.mult, op1=ALU.subtract),
                        reads=[ck, ("hp", c)], writes=[("pb", c)])
                    if first_tile:
                        T.op("dve", lambda e, cur=cur, gi=gi: e.tensor_tensor(out=cur[:, 16:32], in0=cur[:, 16:32], in1=pinv[:, gi, :], op=ALU.mult),
                             reads=[ck, "pinv", ("pb", c)], writes=[ck])
                        T.op("dve", lambda e, c=c, cur=cur: e.tensor_tensor(out=pb[:, c, 0:16], in0=cur[:, 16:32], in1=hp[:, c, 16:32], op=ALU.subtract),
                             reads=[ck, ("hp", c)], writes=[("pb", c)])
                for m in range(NC8):
                    gi = m // 2
                    ob = 4 + m % 2
                    T.mm_group([(self.bank[ob][:], pw[:, 2 * gi + kc, (m % 2) * 128:(m % 2 + 1) * 128], pb[:, 2 * gi + kc, :]) for kc in range(2)],
                               reads=[wk, ("pb", 2 * gi), ("pb", 2 * gi + 1)], writes=[("bank", ob)])
                    T.op("dve", lambda e, ob=ob, m=m: e.scalar_tensor_tensor(
                        out=self.xs[:, m, ts], in0=self.bank[ob][:], scalar=self.vec[:, VOFF["pscale"] + m:VOFF["pscale"] + m + 1],
                        in1=self.xs[:, m, ts], op0=ALU.mult, op1=ALU.add),
                        reads=[("bank", ob), ("xs", m, t), "vec"], writes=[("xs", m, t)])
            self.rmsnorm(f"nmix{l}", emit)
            self.wretire(wi)
            T.barrier()


    def plan_attn(self, l):
        a = len(self.wq)
        for pr in range(4):
            for g in range(3):
                self.wplan(("wa", pr, g), [(0, 8, 512, self.d_wattn[:, pr, g, 0:512].rearrange("(k p) n -> p k n", p=128))])
                self.wplan(("wv", pr, g), [(0, 8, 128, self.d_wattn[:, pr, g, 512:640].rearrange("(k p) n -> p k n", p=128))])
            self.wplan(("wo", pr), [(0, 1, D, self.d_wo[pr * 128:(pr + 1) * 128, :].rearrange("(k p) n -> p k n", p=128))])
        return a

    def attn(self, l, wi, seg):
        nc, T = self.nc, self.T
        from contextlib import ExitStack
        self.uid = getattr(self, "uid", 0) + 1
        uid = self.uid
        DIL = (1, 4, 16)
        with ExitStack() as ph:
            B = lambda name, shape, dtype: ph.enter_context(nc.sbuf_tensor(f"{name}_{uid}", shape, dtype))
            self.h = B("h", [128, NC8, TS], BF16)
            cs = [B(f"cs{i}", [128, TT], F32) for i in range(2)]
            sn = [B(f"sn{i}", [128, TT], F32) for i in range(2)]
            Qt = B("Qt", [128, TS], BF16)
            Kt = B("Kt", [128, TS], BF16)
            Va = B("Va", [128, 16, 256], BF16)
            kc = B("kc", [128, TS], BF16)
            vc = B("vc", [128, 16, 256], BF16)
            acc = B("acc", [128, 2, TS], F32)
            OT = B("OT", [128, TS], BF16)
            pt = [B(f"pt{i}", [128, 256], BF16) for i in range(2)]
            tm = [B(f"tm{i}", [128, TT], F32) for i in range(2)]
            rd = tm[0]
            self.norm_to_h(f"nmix{l}")
            T.op("pool", lambda e: e.memset(Va[:], 1.0), writes=["Va"])
            T.op("pool", lambda e: e.memset(vc[:], 1.0), writes=["vc"])
            tabn = [0]
            sbn = [0]
            obn = [0]
            ptn = [0]
            pw = wi
            for pr in range(4):
                T.op("pool", lambda e: e.memset(acc[:], 0.0), reads=[], writes=[("acc", 0), ("acc", 1)])
                for g in range(3):
                    d = DIL[g]
                    nblk = TS // (128 * d)
                    run = TS // d
                    wA, wAk = self.wget(pw)
                    wB, wBk = self.wget(pw + 1)
                    wAv = wA[:, 0:4096].rearrange("p (k n) -> p k n", k=8)
                    wBv = wB[:, 0:1024].rearrange("p (k n) -> p k n", k=8)
                    pairn = 0
                    for kind in range(2):
                        dst = Qt if kind == 0 else Kt
                        dkey = "Qt" if kind == 0 else "Kt"
                        off = kind * 256
                        for t in range(NT):
                            ts = slice(t * TT, (t + 1) * TT)
                            ab, bb = ((0, 1), (2, 3))[pairn % 2]
                            pairn += 1
                            hr = [("h", k, t) for k in range(NC8)]
                            T.mm_group([(self.bank[ab][:], wAv[:, k, off:off + 128], self.h[:, k, ts]) for k in range(NC8)],
                                       reads=[wAk] + hr, writes=[("bank", ab)])
                            T.mm_group([(self.bank[bb][:], wAv[:, k, off + 128:off + 256], self.h[:, k, ts]) for k in range(NC8)],
                                       reads=[wAk] + hr, writes=[("bank", bb)])
                            ti = tabn[0] % 2
                            tabn[0] += 1
                            p0 = seg * TS + t * TT
                            T.dma("sp", cs[ti][:], self.d_cos[:, p0:p0 + TT], writes=[("cs", ti)])
                            T.dma("sp", sn[ti][:], self.d_sin[:, p0:p0 + TT], writes=[("sn", ti)])
                            T.op("dve", lambda e, ab=ab, ti=ti: e.tensor_tensor(out=tm[0][:], in0=self.bank[ab][:], in1=cs[ti][:], op=ALU.mult),
                                 reads=[("bank", ab), ("cs", ti)], writes=[("tm", 0)])
                            T.op("dve", lambda e, bb=bb, ti=ti: e.tensor_tensor(out=tm[1][:], in0=self.bank[bb][:], in1=sn[ti][:], op=ALU.mult),
                                 reads=[("bank", bb), ("sn", ti)], writes=[("tm", 1)])
                            ov = dst[:, :].rearrange("p (r i) -> p r i", r=d)[:, :, t * TT // d:(t + 1) * TT // d]
                            i0 = tm[0][:, :].rearrange("p (i r) -> p r i", r=d)
                            i1 = tm[1][:, :].rearrange("p (i r) -> p r i", r=d)
                            T.op("dve", lambda e, ov=ov, i0=i0, i1=i1: e.tensor_tensor(out=ov, in0=i0, in1=i1, op=ALU.add),
                                 reads=[("tm", 0), ("tm", 1)], writes=[(dkey, t)])
                    qk_all = [("Qt", t) for t in range(NT)] + [("Kt", t) for t in range(NT)]
                    for b4 in range(4):
                        vb = 6 + b4 % 2
                        for bi in range(4):
                            bidx = b4 * 4 + bi
                            r, jb = bidx // nblk, bidx % nblk
                            st = jb * 128 * d + r
                            T.mm_group([(self.bank[vb][:, bi * 128:(bi + 1) * 128], self.h[:, k, st:st + 127 * d + 1:d], wBv[:, k, :]) for k in range(NC8)],
                                       reads=[wBk] + [("h", k, tt) for k in range(NC8) for tt in range(NT)],
                                       writes=[("bank", vb)] if bi == 0 else [])
                        T.res[("bank", vb)] = [(T.sem["pe"], T.cnt["pe"], "pe", "pe"), []]
                        src = self.bank[vb][:, :].rearrange("p (b f) -> p b f", b=4)
                        T.op("act", lambda e, src=src, b4=b4: e.activation(out=Va[:, b4 * 4:(b4 + 1) * 4, 0:64], in_=src[:, :, 0:64], func=AF.Identity),
                             reads=[("bank", vb), "Va"], writes=[("Va", b4, 0)])
                        T.op("act", lambda e, src=src, b4=b4: e.activation(out=Va[:, b4 * 4:(b4 + 1) * 4, 192:256], in_=src[:, :, 64:128], func=AF.Identity),
                             reads=[("bank", vb), "Va"], writes=[("Va", b4, 1)])
                    va_all = [("Va", b4, e_) for b4 in range(4) for e_ in range(2)]
                    self.wretire(pw)
                    self.wretire(pw + 1)
                    pw += 2
                    if seg == 0:
                        T.dma("sp", self.d_kctx[pr, g, :, 0:d * 128].rearrange("p (r i) -> p r i", r=d),
                              Kt[:, :].rearrange("p (r i) -> p r i", r=d)[:, :, (nblk - 1) * 128:nblk * 128],
                              reads=qk_all, writes=[("kctx", pr, g)])
                        T.dma("sp", self.d_vctx[pr, g, :, 0:d * 256].rearrange("p (r f) -> p r f", r=d),
                              Va[:, :, :].rearrange("p (r j) f -> p r j f", r=d)[:, :, nblk - 1, :],
                              reads=va_all, writes=[("vctx", pr, g)])
                    else:
                        T.dma("sp", kc[:, 0:d * 128], self.d_kctx[pr, g, :, 0:d * 128], reads=[("kctx", pr, g)], writes=["kc"])
                        T.dma("sp", vc[:, 0:d, :], self.d_vctx[pr, g, :, 0:d * 256].rearrange("p (r f) -> p r f", r=d),
                              reads=[("vctx", pr, g), "vc"], writes=["vc2"])
                    def emit_pv(job):
                        (e_, r, kb, qlo, nq, vlhs, kr, pi) = job
                        for qi in range(nq):
                            qb = qlo + qi
                            closing = (kb == qb)
                            opening = not closing
                            if opening:
                                ob = 2 + obn[0] % 4
                                obn[0] += 1
                                self._ob_cur = ob
                            else:
                                ob = self._ob_cur if (seg == 1 or qb > 0) else None
                                if ob is None:
                                    ob = 2 + obn[0] % 4
                                    obn[0] += 1
                            first = opening or (seg == 0 and qb == 0)
                            T._deps("pe", [("pt", pi)] + kr, [("bank", ob)] if first else [])
                            inst = nc.tensor.matmul(self.bank[ob][:, 0:128], lhsT=vlhs, rhs=pt[pi][:, qi * 128:(qi + 1) * 128],
                                                    start=first, stop=closing)
                            T.cnt["pe"] += 1
                            inst.then_inc(T.sem["pe"], 1)
                            hh = (T.sem["pe"], T.cnt["pe"], "pe", "pe")
                            T._commit(hh, [("pt", pi)], [("bank", ob)] if closing else [])
                            if closing:
                                av = acc[:, e_, :].rearrange("p (i r) -> p r i", r=d)[:, r, qb * 128:(qb + 1) * 128]
                                T.op("dve", lambda e, ob=ob, av=av: e.tensor_tensor(out=av, in0=self.bank[ob][:, 0:128], in1=av, op=ALU.add),
                                     reads=[("bank", ob), ("acc", e_)], writes=[("acc", e_)])

                    pend_pv = None
                    for e_ in range(2):
                        ps = slice(64 * e_, 64 * e_ + 64)
                        for r in range(d):
                            kbs = ([-1] if seg == 1 else []) + list(range(nblk))
                            for kb in kbs:
                                if kb < 0:
                                    klhs = kc[ps, r * 128:(r + 1) * 128]
                                    vlhs = vc[:, r, e_ * 128:(e_ + 1) * 128]
                                    kr = ["kc", "vc2"]
                                    qlo, nq, moff = 0, 1, 128
                                else:
                                    klhs = Kt[ps, r * run + kb * 128:r * run + (kb + 1) * 128]
                                    vlhs = Va[:, r * nblk + kb, e_ * 128:(e_ + 1) * 128]
                                    kr = qk_all + va_all
                                    qlo, nq, moff = kb, (2 if kb + 1 < nblk else 1), 0
                                sb = sbn[0] % 2
                                sbn[0] += 1
                                ncol = nq * 128
                                T.mm_group([(self.bank[sb][:, 0:ncol], klhs, Qt[ps, r * run + qlo * 128:r * run + qlo * 128 + ncol]),
                                            (self.bank[sb][:, 0:ncol], self.ident_bf, self.maskb[:, moff:moff + ncol])],
                                           reads=kr + ["cbf"], writes=[("bank", sb)])
                                pi = ptn[0] % 2
                                ptn[0] += 1
                                T.op("act", lambda e, sb=sb, pi=pi, ncol=ncol: e.activation(out=pt[pi][:, 0:ncol], in_=self.bank[sb][:, 0:ncol], func=AF.Exp, scale=0.125),
                                     reads=[("bank", sb)], writes=[("pt", pi)])
                                if pend_pv is not None:
                                    emit_pv(pend_pv)
                                pend_pv = (e_, r, kb, qlo, nq, vlhs, kr, pi)
                    emit_pv(pend_pv)
                wo, wok = self.wget(pw)
                wov = wo[:, 0:D]
                for t in range(NT):
                    ts = slice(t * TT, (t + 1) * TT)
                    for e_ in range(2):
                        ps = slice(64 * e_, 64 * e_ + 64)
                        T.mm_group([(self.bank[0][:], self.shift32, acc[:, e_, ts])], reads=[("acc", e_), "c32"], writes=[("bank", 0)])
                        T.op("dve", lambda e, ps=ps: e.reciprocal(out=rd[ps, :], in_=self.bank[0][ps, :]),
                             reads=[("bank", 0)], writes=[("tm", 0)])
                        T.op("dve", lambda e, ps=ps, e_=e_: e.tensor_tensor(out=OT[ps, ts], in0=acc[ps, e_, ts], in1=rd[ps, :], op=ALU.mult),
                             reads=[("tm", 0), ("acc", e_)], writes=[("OT", t, e_)])
                    for m in range(NC8):
                        ob = 4 + m % 2
                        T.mm_group([(self.bank[ob][:], wov[:, m * 128:(m + 1) * 128], OT[:, ts])],
                                   reads=[wok, ("OT", t, 0), ("OT", t, 1)], writes=[("bank", ob)])
                        T.op("dve", lambda e, ob=ob, m=m: e.tensor_tensor(out=self.xs[:, m, ts], in0=self.bank[ob][:], in1=self.xs[:, m, ts], op=ALU.add),
                             reads=[("bank", ob), ("xs", m, t)], writes=[("xs", m, t)])
                self.wretire(pw)
                pw += 1
            T.barrier()

    def store_out(self, seg, final_norm):
        T = self.T
        if final_norm:
            g0 = VOFF["nfinal"]

            def emit(t, rs, rkey):
                ts = slice(t * TT, (t + 1) * TT)
                for c in range(NC8):
                    T.op("dve", lambda e, c=c: e.scalar_tensor_tensor(
                        out=self.xs[:, c, ts], in0=self.xs[:, c, ts], scalar=self.vec[:, g0 + c:g0 + c + 1],
                        in1=rs[:], op0=ALU.mult, op1=ALU.mult),
                        reads=[("xs", c, t), rkey], writes=[("xs", c, t)])
            self.rmsnorm("nfinal", emit)
        for c in range(NC8):
            T.dma("sp", self.d_out[c * 128:(c + 1) * 128, seg * TS:(seg + 1) * TS], self.xs[:, c, :],
                  reads=[("xs", c, t) for t in range(NT)], writes=[("out", c, seg)])

    def build(self):
        nc, T = self.nc, self.T
        from contextlib import ExitStack
        with ExitStack() as es:
            A = lambda name, shape, dtype: es.enter_context(nc.sbuf_tensor(name, shape, dtype))
            self.xs = A("xs", [128, NC8, TS], F32)
            self.vec = A("vec_sb", [128, NV], F32)
            self.cbf = A("cbf", [128, 512], BF16)
            self.c32 = A("c32", [128, 256], F32)
            self.eps_rms = A("eps_rms", [128, 1], F32)
            self.eps_ln = A("eps_ln", [128, 1], F32)
            self.ring = [A(f"ring{i}", [128, SLOT], BF16) for i in range(NSLOT)]
            self.sq = [A(f"sq{i}", [128, TT], BF16) for i in range(2)]
            self.rstd = [A(f"rstd{i}", [128, TT], F32) for i in range(2)]
            self.ccarry = A("ccarry", [128, 2, NC8, 32], BF16)
            self.pcarry = A("pcarry", [128, NC8, 16], F32)
            self.bank = [es.enter_context(nc.psum_tensor(f"bank{i}", [128, TT], F32)) for i in range(8)]
            self.ident_bf = self.cbf[:, 0:128]
            self.ones_bf = self.cbf[:, 128:256]
            self.maskb = self.cbf[:, 256:512]
            self.ones32 = self.c32[:, 0:128]
            self.shift32 = self.c32[:, 128:256]

            T.dma("sp", self.vec[:], self.d_vec[:, :], writes=["vec"])
            T.dma("sp", self.cbf[:], self.d_cbf[:, :], writes=["cbf"])
            T.dma("sp", self.c32[:], self.d_c32[:, :], writes=["c32"])
            T.op("dve", lambda e: e.memset(self.eps_rms[:], RMS_EPS), writes=["eps"])
            T.op("dve", lambda e: e.memset(self.eps_ln[:], LN_EPS), writes=["eps2"])
            T.op("dve", lambda e: e.memset(self.ccarry[:], 0.0), writes=["cc0"])
            T.op("dve", lambda e: e.memset(self.pcarry[:], 0.0), writes=["pc0"])
            T.barrier()

            sched = []
            for seg in range(NSEG):
                for (kind, l) in self.plan:
                    if kind == "ffn":
                        sched.append((seg, kind, l, self.plan_ffn(l)))
                    elif kind == "conv":
                        sched.append((seg, kind, l, self.plan_conv(l)))
                    elif kind == "pool":
                        sched.append((seg, kind, l, self.plan_pool(l)))
                    else:
                        sched.append((seg, kind, l, self.plan_attn(l)))

            cur_seg = -1
            for (seg, kind, l, widx) in sched:
                if seg != cur_seg:
                    if cur_seg >= 0:
                        self.store_out(cur_seg, self.final_norm)
                    cur_seg = seg
                    for c in range(NC8):
                        T.dma("sp", self.xs[:, c, :], self.d_xT[c * 128:(c + 1) * 128, seg * TS:(seg + 1) * TS],
                              writes=[("xs", c, t) for t in range(NT)])
                if kind == "ffn":
                    with ExitStack() as ph:
                        self.uid = getattr(self, "uid", 0) + 1
                        B = lambda name, shape, dtype: ph.enter_context(nc.sbuf_tensor(f"{name}_{self.uid}", shape, dtype))
                        self.h = B("h", [128, NC8, TS], BF16)
                        self.abuf = [B(f"abuf{i}", [128, 4, TT], BF16) for i in range(2)]
                        self.sg = [B(f"sg{i}", [128, TT], F32) for i in range(2)]
                        self.ffn(l, widx)
                        T.barrier()
                elif kind == "conv":
                    self.conv(l, widx, seg)
                elif kind == "pool":
                    self.pool(l, widx, seg)
                elif kind == "attn":
                    self.attn(l, widx, seg)
            self.store_out(cur_seg, self.final_norm)
            T.barrier()
        return nc


def _chunked(v):
    v = np.asarray(v, np.float32)
    return np.ascontiguousarray(v.reshape(-1, 128).T)


def prep_shared(inp):
    f32 = np.float32
    vec = np.zeros((128, NV), f32)
    for l in range(DEPTH):
        vec[:, VOFF[f"nmix{l}"]:VOFF[f"nmix{l}"] + 8] = _chunked(inp["norm_mix_g"][l])
        vec[:, VOFF[f"nffn{l}"]:VOFF[f"nffn{l}"] + 8] = _chunked(inp["norm_ffn_g"][l])
    vec[:, VOFF["nfinal"]:VOFF["nfinal"] + 8] = _chunked(inp["final_norm_g"])
    for j in range(2):
        vec[:, VOFF[f"cba{j}"]:VOFF[f"cba{j}"] + 8] = _chunked(inp["conv_b_in"][j][:D])
        vec[:, VOFF[f"cbg{j}"]:VOFF[f"cbg{j}"] + 8] = _chunked(inp["conv_b_in"][j][D:])
        vec[:, VOFF[f"cbdw{j}"]:VOFF[f"cbdw{j}"] + 8] = _chunked(inp["conv_b_dw"][j])
        vec[:, VOFF[f"clng{j}"]:VOFF[f"clng{j}"] + 8] = _chunked(inp["conv_ln_g"][j])
        vec[:, VOFF[f"clnb{j}"]:VOFF[f"clnb{j}"] + 8] = _chunked(inp["conv_ln_b"][j])
        vec[:, VOFF[f"cbout{j}"]:VOFF[f"cbout{j}"] + 8] = _chunked(inp["conv_b_out"][j])
    vec[:, VOFF["pscale"]:VOFF["pscale"] + 8] = _chunked(inp["pool_scale"][0])

    cw = np.asarray(inp["conv_w_in"], f32)
    cwin_r = np.ascontiguousarray(np.stack([cw[:, :, :D].reshape(2, D, 8, 128),
                                            cw[:, :, D:].reshape(2, D, 8, 128)], axis=3).reshape(2, D, 8, 256))
    dw = np.asarray(inp["conv_w_dw"], f32)
    cwdw_r = np.ascontiguousarray(dw.reshape(2, CW, 8, 128).transpose(3, 0, 2, 1))

    pinv = np.zeros((128, 4, 16), f32)
    for g, w in enumerate((2, 4, 8, 16)):
        pinv[:, g, :] = 1.0 / np.minimum(np.arange(1, 17), w).astype(f32)

    wqkv = np.asarray(inp["attn_w_qkv"], f32)[0]
    wq = wqkv[:, 0:1536].reshape(D, 3, 8, 64)
    wk = wqkv[:, 1536:3072].reshape(D, 3, 8, 64)
    wv = wqkv[:, 3072:4608].reshape(D, 3, 8, 64)
    swap = np.concatenate([np.arange(32, 64), np.arange(0, 32)])
    w_r = np.zeros((D, 4, 3, 5, 2, 64), f32)
    for pr in range(4):
        for g in range(3):
            for e in range(2):
                hs = 2 * pr + e
                w_r[:, pr, g, 0, e] = wq[:, g, hs]
                w_r[:, pr, g, 1, e] = wq[:, g, hs][:, swap]
                w_r[:, pr, g, 2, e] = wk[:, g, hs]
                w_r[:, pr, g, 3, e] = wk[:, g, hs][:, swap]
                w_r[:, pr, g, 4, e] = wv[:, g, hs]
    w_r = np.ascontiguousarray(w_r.reshape(D, 4, 3, 640))

    half = 32
    inv_freq = (10000.0 ** (-np.arange(half, dtype=f32) / half)).astype(f32)
    ang = np.arange(S, dtype=f32)[None, :] * inv_freq[:, None]
    cos = np.cos(ang).astype(f32)
    sin = np.sin(ang).astype(f32)
    cos_t = np.concatenate([cos, cos, cos, cos], axis=0)
    sin_t = np.concatenate([-sin, sin, -sin, sin], axis=0)

    cbf = np.zeros((128, 512), f32)
    cbf[:, 0:128] = np.eye(128, dtype=f32)
    cbf[:, 128:256] = 1.0
    a = np.arange(128)[:, None]
    b = np.arange(256)[None, :]
    cbf[:, 256:512] = np.where((b >= a) & (b <= a + 128), 0.0, -30000.0)
    c32 = np.zeros((128, 256), f32)
    c32[:, 0:128] = 1.0
    c32[:, 128:256] = np.roll(np.eye(128, dtype=f32), 64, axis=0)

    return {
        "vec": vec,
        "ffn_w_gate": np.ascontiguousarray(inp["ffn_w_gate"], f32),
        "ffn_w_up": np.ascontiguousarray(inp["ffn_w_up"], f32),
        "ffn_w_down": np.ascontiguousarray(inp["ffn_w_down"], f32),
        "conv_w_in_r": cwin_r,
        "conv_w_out": np.ascontiguousarray(inp["conv_w_out"], f32),
        "conv_w_dw_r": cwdw_r,
        "pool_w": np.ascontiguousarray(inp["pool_w"], f32)[0],
        "pool_inv": pinv,
        "attn_w_r": w_r,
        "attn_w_o": np.ascontiguousarray(inp["attn_w_o"], f32)[0],
        "rope_cos": np.ascontiguousarray(cos_t),
        "rope_sin": np.ascontiguousarray(sin_t),
        "const_bf": cbf.astype(ml_dtypes.bfloat16),
        "const_f32": c32,
    }


FULL_PLAN = [("conv", 0), ("ffn", 0), ("attn", 1), ("ffn", 1), ("pool", 2), ("ffn", 2), ("conv", 3), ("ffn", 3)]


def run(inputs, plan=FULL_PLAN, final_norm=True, cores=8):
    inp = {k: np.asarray(v) for k, v in inputs.items()}
    shared = prep_shared(inp)
    x = np.asarray(inp["x"], np.float32)
    in_maps = []
    for b in range(cores):
        m = dict(shared)
        m["xT"] = np.ascontiguousarray(x[b].T)
        in_maps.append(m)
    nc = Prog(plan, final_norm).build()
    res = run_bass_kernel_spmd(nc, in_maps, core_ids=list(range(cores)))
    out = np.stack([np.ascontiguousarray(r["yT"].T) for r in res.results], axis=0)
    return out.astype(np.float32)


def kernel(**inputs):
    return run(inputs)
```

```python
import numpy as np
import ml_dtypes
import concourse.bass as bass
import concourse.mybir as mybir
from concourse.bass_utils import run_bass_kernel_spmd

F32 = mybir.dt.float32
BF16 = mybir.dt.bfloat16
AF = mybir.ActivationFunctionType
ALU = mybir.AluOpType

D = 1024
S = 4096
NC8 = 8
TS = 2048
NSEG = S // TS
TT = 512
NT = TS // TT
FH = 2816
NHC = FH // 128
DEPTH = 4
CW = 31
RMS_EPS = 1e-6
LN_EPS = 1e-5
SLOT = 4096
NSLOT = 5

VEC_NAMES = []
for _l in range(DEPTH):
    VEC_NAMES += [f"nmix{_l}", f"nffn{_l}"]
VEC_NAMES += ["nfinal"]
for _j in range(2):
    VEC_NAMES += [f"cba{_j}", f"cbg{_j}", f"cbdw{_j}", f"clng{_j}", f"clnb{_j}", f"cbout{_j}"]
VEC_NAMES += ["pscale"]
VOFF = {n: 8 * i for i, n in enumerate(VEC_NAMES)}
NV = 8 * len(VEC_NAMES)


class Tracker:
    def __init__(self, nc):
        self.nc = nc
        self.engs = {"pe": nc.tensor, "act": nc.scalar, "dve": nc.vector,
                     "pool": nc.gpsimd, "sp": nc.sync}
        self.sem = {k: nc.alloc_semaphore(f"prog_{k}") for k in ("pe", "act", "dve", "pool")}
        self.cnt = {k: 0 for k in self.sem}
        self.waited = {k: {} for k in self.engs}
        self.dpool = {}
        for q in ("sp", "pool"):
            self.dpool[q] = [[nc.alloc_semaphore(f"dma_{q}_{i}"), 0] for i in range(12)]
        self.dnext = {"sp": 0, "pool": 0}
        self.res = {}
        self.nwaits = 0

    def _wait(self, eng, h):
        if h is None:
            return
        sem, val, src, key = h
        if eng == "pe" and src == "pe":
            return
        if self.waited[eng].get(key, 0) >= val:
            return
        self.engs[eng].wait_ge(sem, val)
        self.waited[eng][key] = val
        self.nwaits += 1

    def _deps(self, eng, reads, writes):
        for k in reads:
            r = self.res.get(k)
            if r is not None:
                self._wait(eng, r[0])
        for k in writes:
            r = self.res.get(k)
            if r is not None:
                self._wait(eng, r[0])
                for h in r[1]:
                    self._wait(eng, h)

    def _commit(self, h, reads, writes):
        for k in reads:
            r = self.res.setdefault(k, [None, []])
            if h[2] in ("pe", "act", "dve", "pool") and h[3] == h[2]:
                r[1] = [x for x in r[1] if x[3] != h[3]]
            r[1].append(h)
        for k in writes:
            self.res[k] = [h, []]

    def op(self, eng, fn, reads=(), writes=()):
        self._deps(eng, reads, writes)
        inst = fn(self.engs[eng])
        self.cnt[eng] += 1
        inst.then_inc(self.sem[eng], 1)
        h = (self.sem[eng], self.cnt[eng], eng, eng)
        self._commit(h, reads, writes)
        return h

    def mm_group(self, mms, reads=(), writes=()):
        self._deps("pe", reads, writes)
        n = len(mms)
        inst = None
        for i, (o, l, r) in enumerate(mms):
            inst = self.nc.tensor.matmul(o, lhsT=l, rhs=r, start=(i == 0), stop=(i == n - 1))
        self.cnt["pe"] += 1
        inst.then_inc(self.sem["pe"], 1)
        h = (self.sem["pe"], self.cnt["pe"], "pe", "pe")
        self._commit(h, reads, writes)
        return h

    def dma(self, q, out, in_, reads=(), writes=()):
        pool = self.dpool[q]
        i = self.dnext[q]
        self.dnext[q] = (i + 1) % len(pool)
        sem, c = pool[i]
        key = f"d{q}{i}"
        self._wait(q, (sem, c, q, key))
        self._deps(q, reads, writes)
        self.engs[q].dma_start(out=out, in_=in_).then_inc(sem, 16)
        pool[i][1] = c + 16
        h = (sem, c + 16, q, key)
        self._commit(h, reads, writes)
        return h

    def barrier(self):
        hs = [(self.sem[k], self.cnt[k], k, k) for k in self.sem if self.cnt[k] > 0]
        for q in ("sp", "pool"):
            for i, (sem, c) in enumerate(self.dpool[q]):
                if c > 0:
                    hs.append((sem, c, q, f"d{q}{i}"))
        for e in ("pe", "act", "dve", "pool", "sp"):
            for h in hs:
                if e == "pe" and h[2] == "pe":
                    continue
                self._wait(e, h)
        self.res = {}


class Prog:
    def __init__(self, plan, final_norm=True):
        self.plan = plan
        self.final_norm = final_norm
        nc = bass.Bass("TRN2", target_bir_lowering=False)
        self.nc = nc
        self.T = Tracker(nc)
        dt = nc.dram_tensor
        self.d_xT = dt("xT", [D, S], F32, kind="ExternalInput").ap()
        self.d_vec = dt("vec", [128, NV], F32, kind="ExternalInput").ap()
        self.d_wg = dt("ffn_w_gate", [DEPTH, D, FH], F32, kind="ExternalInput").ap()
        self.d_wu = dt("ffn_w_up", [DEPTH, D, FH], F32, kind="ExternalInput").ap()
        self.d_wd = dt("ffn_w_down", [DEPTH, FH, D], F32, kind="ExternalInput").ap()
        self.d_cwin = dt("conv_w_in_r", [2, D, 8, 256], F32, kind="ExternalInput").ap()
        self.d_cwout = dt("conv_w_out", [2, D, D], F32, kind="ExternalInput").ap()
        self.d_cwdw = dt("conv_w_dw_r", [128, 2, 8, CW], F32, kind="ExternalInput").ap()
        self.d_pw = dt("pool_w", [4, 256, 256], F32, kind="ExternalInput").ap()
        self.d_pinv = dt("pool_inv", [128, 4, 16], F32, kind="ExternalInput").ap()
        self.d_wattn = dt("attn_w_r", [D, 4, 3, 640], F32, kind="ExternalInput").ap()
        self.d_wo = dt("attn_w_o", [512, D], F32, kind="ExternalInput").ap()
        self.d_cos = dt("rope_cos", [128, S], F32, kind="ExternalInput").ap()
        self.d_sin = dt("rope_sin", [128, S], F32, kind="ExternalInput").ap()
        self.d_cbf = dt("const_bf", [128, 128 + 128 + 256], BF16, kind="ExternalInput").ap()
        self.d_c32 = dt("const_f32", [128, 128 + 128], F32, kind="ExternalInput").ap()
        self.d_kctx = dt("kctx", [4, 3, 128, TS], BF16, kind="Internal").ap()
        self.d_vctx = dt("vctx", [4, 3, 128, 16 * 256], BF16, kind="Internal").ap()
        self.d_dgd = dt("dgd", [2, NC8, 128, CW * 128], BF16, kind="Internal").ap()
        self.d_out = dt("yT", [D, S], F32, kind="ExternalOutput").ap()
        self.wq = []
        self.wissued = 0
        self.wretired = set()
        self.wloaded = {}

    def wplan(self, key, src_aps):
        self.wq.append((key, src_aps))

    def wpump(self):
        T = self.T
        while self.wissued < len(self.wq) and (self.wissued < NSLOT or (self.wissued - NSLOT) in self.wretired):
            i = self.wissued
            slot = i % NSLOT
            key, srcs = self.wq[i]
            assert len(srcs) == 1
            (off, k, n, ap) = srcs[0]
            dst = self.ring[slot][:, off:off + k * n].rearrange("p (k n) -> p k n", k=k)
            T.dma("pool", dst, ap, reads=(), writes=[("ring", slot)])
            self.wissued += 1

    def wretire(self, idx):
        self.wretired.add(idx)
        self.wpump()

    def wget(self, idx):
        self.wpump()
        assert idx < self.wissued, f"weight ring deadlock at piece {idx}"
        return self.ring[idx % NSLOT], ("ring", idx % NSLOT)

    def rmsnorm(self, gname, emit):
        for t in range(NT):
            self.rmsnorm_tile(t, emit)

    def rmsnorm_tile(self, t, emit):
        nc, T = self.nc, self.T
        ts = slice(t * TT, (t + 1) * TT)
        ssb = self.bank[7]
        for c in range(NC8):
            sq = self.sq[c % 2]
            T.op("act", lambda e, c=c, sq=sq: e.activation(out=sq[:], in_=self.xs[:, c, ts], func=AF.Square),
                 reads=[("xs", c, t)], writes=[("sq", c % 2)])
            self.mm_one(ssb[:], self.ones_bf[:], sq[:], c == 0, c == NC8 - 1, [("sq", c % 2)], ("bank", 7))
        rs = self.rstd[t % 2]
        T.op("act", lambda e, rs=rs: e.activation(out=rs[:], in_=ssb[:], func=AF.Sqrt, bias=self.eps_rms[:, 0:1], scale=1.0 / D),
             reads=[("bank", 7)], writes=[("rstd", t % 2)])
        T.op("dve", lambda e, rs=rs: e.reciprocal(out=rs[:], in_=rs[:]),
             reads=[("rstd", t % 2)], writes=[("rstd", t % 2)])
        emit(t, rs, ("rstd", t % 2))

    def h_emit(self, gname):
        T = self.T
        g0 = VOFF[gname]

        def emit(t, rs, rkey):
            ts = slice(t * TT, (t + 1) * TT)
            for c in range(NC8):
                T.op("dve", lambda e, c=c: e.scalar_tensor_tensor(
                    out=self.h[:, c, ts], in0=self.xs[:, c, ts], scalar=self.vec[:, g0 + c:g0 + c + 1],
                    in1=rs[:], op0=ALU.mult, op1=ALU.mult),
                    reads=[("xs", c, t), rkey], writes=[("h", c, t)])
        return emit

    def norm_to_h(self, gname):
        self.rmsnorm(gname, self.h_emit(gname))

    def plan_ffn(self, l):
        groups = [(0, 4), (4, 4), (8, 4), (12, 4), (16, 4), (20, 2)]
        idx = []
        for (j0, nj) in groups:
            a = len(self.wq)
            self.wplan(("wg", l, j0), [(0, 8, nj * 128, self.d_wg[l, :, j0 * 128:(j0 + nj) * 128].rearrange("(k p) n -> p k n", p=128))])
            self.wplan(("wu", l, j0), [(0, 8, nj * 128, self.d_wu[l, :, j0 * 128:(j0 + nj) * 128].rearrange("(k p) n -> p k n", p=128))])
            self.wplan(("wd", l, j0), [(0, nj, D, self.d_wd[l, j0 * 128:(j0 + nj) * 128, :].rearrange("(k p) n -> p k n", p=128))])
            idx.append((j0, nj, a))
        return idx

    def ffn(self, l, widx):
        nc, T = self.nc, self.T
        hemit = self.h_emit(f"nffn{l}")
        self.rmsnorm_tile(0, hemit)
        self.rmsnorm_tile(1, hemit)
        first_group = True
        pend = None
        step = 0
        gu_banks = [(0, 1), (2, 3)]
        d_banks = [4, 5]
        dcount = [0]

        def down(st):
            (t, nj, wdk, wdslot, ab, wdi) = st
            ts = slice(t * TT, (t + 1) * TT)
            wdv = wdslot[:, 0:nj * D].rearrange("p (k n) -> p k n", k=nj)
            for m in range(NC8):
                b = d_banks[dcount[0] % 2]
                dcount[0] += 1
                T.mm_group([(self.bank[b][:], wdv[:, j, m * 128:(m + 1) * 128], self.abuf[ab][:, j, :]) for j in range(nj)],
                           reads=[wdk] + [("a", ab, j) for j in range(nj)], writes=[("bank", b)])
                T.op("dve", lambda e, b=b, m=m: e.tensor_tensor(out=self.xs[:, m, ts], in0=self.bank[b][:], in1=self.xs[:, m, ts], op=ALU.add),
                     reads=[("bank", b), ("xs", m, t)], writes=[("xs", m, t)])
            if t == NT - 1:
                self.wretire(wdi)

        for (j0, nj, wi) in widx:
            wgs, wgk = self.wget(wi)
            wus, wuk = self.wget(wi + 1)
            wds, wdk = self.wget(wi + 2)
            wgv = wgs[:, 0:8 * nj * 128].rearrange("p (k n) -> p k n", k=8)
            wuv = wus[:, 0:8 * nj * 128].rearrange("p (k n) -> p k n", k=8)
            for t in range(NT):
                ts = slice(t * TT, (t + 1) * TT)
                ab = step % 2
                for j in range(nj):
                    gb, ub = gu_banks[j % 2]
                    hr = [("h", k, t) for k in range(NC8)]
                    T.mm_group([(self.bank[gb][:], wgv[:, k, j * 128:(j + 1) * 128], self.h[:, k, ts]) for k in range(NC8)],
                               reads=[wgk] + hr, writes=[("bank", gb)])
                    T.mm_group([(self.bank[ub][:], wuv[:, k, j * 128:(j + 1) * 128], self.h[:, k, ts]) for k in range(NC8)],
                               reads=[wuk] + hr, writes=[("bank", ub)])
                    sg = self.sg[j % 2]
                    T.op("act", lambda e, gb=gb, sg=sg: e.activation(out=sg[:], in_=self.bank[gb][:], func=AF.Silu),
                         reads=[("bank", gb)], writes=[("sg", j % 2)])
                    T.op("dve", lambda e, ub=ub, sg=sg, j=j, ab=ab: e.tensor_tensor(out=self.abuf[ab][:, j, :], in0=self.bank[ub][:], in1=sg[:], op=ALU.mult),
                         reads=[("bank", ub), ("sg", j % 2)], writes=[("a", ab, j)])
                if first_group and t + 2 < NT:
                    self.rmsnorm_tile(t + 2, hemit)
                if pend is not None:
                    down(pend)
                pend = (t, nj, wdk, wds, ab, wi + 2)
                step += 1
            first_group = False
            self.wretire(wi)
            self.wretire(wi + 1)
        down(pend)


    def mm_one(self, out, lhsT, rhs, first, last, reads, wkey):
        T = self.T
        T._deps("pe", reads, [wkey] if first else [])
        inst = self.nc.tensor.matmul(out, lhsT=lhsT, rhs=rhs, start=first, stop=last)
        T.cnt["pe"] += 1
        inst.then_inc(T.sem["pe"], 1)
        h = (T.sem["pe"], T.cnt["pe"], "pe", "pe")
        T._commit(h, reads, [wkey] if last else [])
        return h

    def plan_conv(self, l):
        j = l // 3
        a = len(self.wq)
        for c in range(NC8):
            self.wplan(("cwin", j, c), [(0, 8, 256, self.d_cwin[j, :, c, :].rearrange("(k p) n -> p k n", p=128))])
        idx = {"win": a, "d": {}, "wo": {}}

        def add_d(t):
            for c in range(NC8):
                idx["d"][(t, c)] = len(self.wq)
                self.wplan(("dg", j, c), [(0, CW, 128, self.d_dgd[j, c, :, :].rearrange("p (k n) -> p k n", k=CW))])

        def add_wo(t):
            i0 = len(self.wq)
            for hf in range(2):
                self.wplan(("cwout", j, hf), [(0, 8, 512, self.d_cwout[j, :, hf * 512:(hf + 1) * 512].rearrange("(k p) n -> p k n", p=128))])
            idx["wo"][t] = (i0, i0 + 1)
        for t in range(NT + 1):
            if t < NT:
                add_d(t)
            if t > 0:
                add_wo(t - 1)
        return idx

    def conv(self, l, widx, seg):
        nc, T = self.nc, self.T
        j = l // 3
        wi = widx["win"]
        from contextlib import ExitStack
        self.uid = getattr(self, "uid", 0) + 1
        uid = self.uid
        with ExitStack() as outer:
            O = lambda name, shape, dtype: outer.enter_context(nc.sbuf_tensor(f"{name}_{uid}", shape, dtype))
            u = O("u", [128, NC8, 32 + TS], BF16)
            with ExitStack() as ph:
                B = lambda name, shape, dtype: ph.enter_context(nc.sbuf_tensor(f"{name}_{uid}", shape, dtype))
                self.h = B("h", [128, NC8, TS], BF16)
                sig = [B(f"sig{i}", [128, TT], F32) for i in range(2)]
                hemit = self.h_emit(f"nmix{l}")
                self.rmsnorm_tile(0, hemit)
                self.rmsnorm_tile(1, hemit)
                for c in range(NC8):
                    T.op("dve", lambda e, c=c: e.tensor_copy(out=u[:, c, 0:32], in_=self.ccarry[:, j, c, :]),
                         reads=[("ccarry", j, c)], writes=[("u", c, -1)])
                pair = 0
                for c in range(NC8):
                    ws, wk = self.wget(wi + c)
                    wv = ws[:, 0:2048].rearrange("p (k n) -> p k n", k=8)
                    for t in range(NT):
                        ts = slice(t * TT, (t + 1) * TT)
                        ab, gb = ((0, 1), (2, 3))[pair % 2]
                        pair += 1
                        hr = [("h", k, t) for k in range(NC8)]
                        T.mm_group([(self.bank[ab][:], wv[:, k, 0:128], self.h[:, k, ts]) for k in range(NC8)],
                                   reads=[wk] + hr, writes=[("bank", ab)])
                        T.mm_group([(self.bank[gb][:], wv[:, k, 128:256], self.h[:, k, ts]) for k in range(NC8)],
                                   reads=[wk] + hr, writes=[("bank", gb)])
                        sg = sig[pair % 2]
                        T.op("act", lambda e, gb=gb, sg=sg, c=c: e.activation(
                            out=sg[:], in_=self.bank[gb][:], func=AF.Sigmoid,
                            bias=self.vec[:, VOFF[f"cbg{j}"] + c:VOFF[f"cbg{j}"] + c + 1], scale=1.0),
                            reads=[("bank", gb), "vec"], writes=[("sig", pair % 2)])
                        T.op("dve", lambda e, ab=ab, sg=sg, c=c, t=t: e.scalar_tensor_tensor(
                            out=u[:, c, 32 + t * TT:32 + (t + 1) * TT], in0=self.bank[ab][:],
                            scalar=self.vec[:, VOFF[f"cba{j}"] + c:VOFF[f"cba{j}"] + c + 1], in1=sg[:],
                            op0=ALU.add, op1=ALU.mult),
                            reads=[("bank", ab), ("sig", pair % 2)], writes=[("u", c, t)])
                        if c == 0 and t + 2 < NT:
                            self.rmsnorm_tile(t + 2, hemit)
                    self.wretire(wi + c)
                for c in range(NC8):
                    T.op("dve", lambda e, c=c: e.tensor_copy(out=self.ccarry[:, j, c, :], in_=u[:, c, TS:TS + 32]),
                         reads=[("u", c, NT - 1)], writes=[("ccarry", j, c)])
                T.barrier()
            with ExitStack() as ph:
                B = lambda name, shape, dtype: ph.enter_context(nc.sbuf_tensor(f"{name}_{uid}", shape, dtype))
                v32 = B("v32", [128, 12, TT], F32)
                vslot = lambda t, c: ((t % 2) * 4 + c) if c < 4 else (8 + c - 4)
                vsq = self.sq
                z = B("z", [128, NC8, TT], BF16)
                mean = self.rstd[0]
                var = self.rstd[1]
                VO = lambda n, c: self.vec[:, VOFF[f"{n}{j}"] + c:VOFF[f"{n}{j}"] + c + 1]
                itc = [0]

                def conv_chunk(t, c):
                    it = itc[0]
                    itc[0] += 1
                    vs = vslot(t, c)
                    di = widx["d"][(t, c)]
                    dslot, dkey = self.wget(di)
                    d = dslot[:, 0:CW * 128].rearrange("p (k n) -> p k n", k=CW)
                    cb = it % 2
                    base = 32 + t * TT - (CW - 1)
                    T.mm_group([(self.bank[cb][:], d[:, k, :], u[:, c, base + k:base + k + TT]) for k in range(CW)],
                               reads=[dkey, ("u", c, t), ("u", c, t - 1)], writes=[("bank", cb)])
                    self.wretire(di)
                    T.op("act", lambda e, cb=cb, c=c, vs=vs: e.activation(out=v32[:, vs, :], in_=self.bank[cb][:], func=AF.Identity,
                                                                   bias=VO("cbdw", c), scale=1.0),
                         reads=[("bank", cb), "vec"], writes=[("v32", vs)])
                    T.op("act", lambda e, cb=cb, c=c: e.activation(out=vsq[it % 2][:], in_=self.bank[cb][:], func=AF.Square,
                                                                   bias=VO("cbdw", c), scale=1.0),
                         reads=[("bank", cb), "vec"], writes=[("vsq", it % 2)])
                    s1b, s2b = (2, 3) if t % 2 == 0 else (6, 7)
                    self.mm_one(self.bank[s1b][:], self.ones32, v32[:, vs, :], c == 0, c == NC8 - 1, [("v32", vs), "c32"], ("bank", s1b))
                    self.mm_one(self.bank[s2b][:], self.ones_bf, vsq[it % 2][:], c == 0, c == NC8 - 1, [("vsq", it % 2), "cbf"], ("bank", s2b))

                def ln(t):
                    s1b, s2b = (2, 3) if t % 2 == 0 else (6, 7)
                    T.op("act", lambda e: e.activation(out=mean[:], in_=self.bank[s1b][:], func=AF.Identity, scale=1.0 / D),
                         reads=[("bank", s1b)], writes=["mean"])
                    T.op("dve", lambda e: e.tensor_tensor(out=var[:], in0=mean[:], in1=mean[:], op=ALU.mult),
                         reads=["mean"], writes=["var"])
                    T.op("dve", lambda e: e.scalar_tensor_tensor(out=var[:], in0=self.bank[s2b][:], scalar=1.0 / D, in1=var[:],
                                                                 op0=ALU.mult, op1=ALU.subtract),
                         reads=[("bank", s2b), "var"], writes=["var"])
                    T.op("act", lambda e: e.activation(out=var[:], in_=var[:], func=AF.Sqrt, bias=self.eps_ln[:, 0:1], scale=1.0),
                         reads=["var"], writes=["var"])
                    T.op("dve", lambda e: e.reciprocal(out=var[:], in_=var[:]), reads=["var"], writes=["var"])
                    for c in range(NC8):
                        vs = vslot(t, c)
                        T.op("dve", lambda e, vs=vs: e.tensor_tensor(out=v32[:, vs, :], in0=v32[:, vs, :], in1=mean[:], op=ALU.subtract),
                             reads=[("v32", vs), "mean"], writes=[("v32", vs)])
                        T.op("dve", lambda e, vs=vs: e.tensor_tensor(out=v32[:, vs, :], in0=v32[:, vs, :], in1=var[:], op=ALU.mult),
                             reads=[("v32", vs), "var"], writes=[("v32", vs)])
                        T.op("act", lambda e, c=c, vs=vs: e.activation(out=z[:, c, :], in_=v32[:, vs, :], func=AF.Silu,
                                                                bias=VO("clnb", c), scale=VO("clng", c)),
                             reads=[("v32", vs), "vec"], writes=[("z", c)])

                def outproj(t):
                    ts = slice(t * TT, (t + 1) * TT)
                    wo = [self.wget(widx["wo"][t][0]), self.wget(widx["wo"][t][1])]
                    wov = [w[0][:, 0:4096].rearrange("p (k n) -> p k n", k=8) for w in wo]
                    for m in range(NC8):
                        ob = 4 + m % 2
                        T.mm_group([(self.bank[ob][:], wov[m // 4][:, c, (m % 4) * 128:(m % 4 + 1) * 128], z[:, c, :]) for c in range(NC8)],
                                   reads=[wo[0][1], wo[1][1]] + [("z", c) for c in range(NC8)], writes=[("bank", ob)])
                        T.op("dve", lambda e, ob=ob, m=m: e.scalar_tensor_tensor(
                            out=self.xs[:, m, ts], in0=self.bank[ob][:], scalar=VO("cbout", m), in1=self.xs[:, m, ts],
                            op0=ALU.add, op1=ALU.add),
                            reads=[("bank", ob), ("xs", m, t), "vec"], writes=[("xs", m, t)])
                    self.wretire(widx["wo"][t][0])
                    self.wretire(widx["wo"][t][1])

                for t in range(NT + 1):
                    if t < NT:
                        for c in range(2):
                            conv_chunk(t, c)
                    if t > 0:
                        ln(t - 1)
                    if t < NT:
                        for c in range(2, NC8):
                            conv_chunk(t, c)
                    if t > 0:
                        outproj(t - 1)
                T.barrier()

    def plan_pool(self, l):
        a = len(self.wq)
        self.wplan(("pw",), [(0, 8, 256, self.d_pw.rearrange("g (kc p) n -> p (g kc) n", p=128))])
        return a

    def pool(self, l, wi, seg):
        nc, T = self.nc, self.T
        from contextlib import ExitStack
        self.uid = getattr(self, "uid", 0) + 1
        uid = self.uid
        WIN = (2, 4, 8, 16)
        with ExitStack() as ph:
            B = lambda name, shape, dtype: ph.enter_context(nc.sbuf_tensor(f"{name}_{uid}", shape, dtype))
            hp = B("hp", [128, NC8, 16 + TT], F32)
            sA = [B(f"sA{i}", [128, 16 + TT], F32) for i in range(2)]
            sB = [B(f"sB{i}", [128, 16 + TT], F32) for i in range(2)]
            pb = B("pb", [128, NC8, TT], BF16)
            pinv = B("pinv", [128, 4, 16], F32)
            T.dma("sp", pinv[:], self.d_pinv[:, :, :], writes=["pinv"])
            ws, wk = self.wget(wi)
            pw = ws[:, 0:2048].rearrange("p (k n) -> p k n", k=8)
            g0 = VOFF[f"nmix{l}"]
            tcount = [0]

            def emit(t, rs, rkey):
                ts = slice(t * TT, (t + 1) * TT)
                first_tile = (seg == 0 and t == 0)
                for c in range(NC8):
                    T.op("dve", lambda e, c=c: e.tensor_copy(out=hp[:, c, 0:16], in_=self.pcarry[:, c, :]),
                         reads=[("pcarry", c)], writes=[("hp", c)])
                    T.op("dve", lambda e, c=c: e.scalar_tensor_tensor(
                        out=hp[:, c, 16:16 + TT], in0=self.xs[:, c, ts], scalar=self.vec[:, g0 + c:g0 + c + 1],
                        in1=rs[:], op0=ALU.mult, op1=ALU.mult),
                        reads=[("xs", c, t), rkey, ("hp", c)], writes=[("hp", c)])
                    T.op("dve", lambda e, c=c: e.tensor_copy(out=self.pcarry[:, c, :], in_=hp[:, c, TT:TT + 16]),
                         reads=[("hp", c)], writes=[("pcarry", c)])
                for c in range(NC8):
                    gi = c // 2
                    w = WIN[gi]
                    a_, b_ = sA[c % 2], sB[c % 2]
                    ka, kb = ("sA", c % 2), ("sB", c % 2)
                    T.op("dve", lambda e, c=c, a_=a_: e.tensor_tensor(out=a_[:, 2:16 + TT], in0=hp[:, c, 2:16 + TT], in1=hp[:, c, 1:15 + TT], op=ALU.add),
                         reads=[("hp", c)], writes=[ka])
                    cur, ck, oth, ok = a_, ka, b_, kb
                    lvl = 2
                    while lvl < w:
                        lo = 2 * lvl
                        T.op("dve", lambda e, cur=cur, oth=oth, lo=lo, lvl=lvl: e.tensor_tensor(
                            out=oth[:, lo:16 + TT], in0=cur[:, lo:16 + TT], in1=cur[:, lo - lvl:16 + TT - lvl], op=ALU.add),
                            reads=[ck], writes=[ok])
                        cur, ck, oth, ok = oth, ok, cur, ck
                        lvl *= 2
                    T.op("dve", lambda e, c=c, cur=cur, w=w: e.scalar_tensor_tensor(
                        out=pb[:, c, :], in0=cur[:, 16:16 + TT], scalar=1.0 / w, in1=hp[:, c, 16:16 + TT],
                        op0=ALU.mult, op1=ALU.subtract),
                        reads=[ck, ("hp", c)], writes=[("pb", c)])
                    if first_tile:
                        T.op("dve", lambda e, cur=cur, gi=gi: e.tensor_tensor(out=cur[:, 16:32], in0=cur[:, 16:32], in1=pinv[:, gi, :], op=ALU.mult),
                             reads=[ck, "pinv", ("pb", c)], writes=[ck])
                        T.op("dve", lambda e, c=c, cur=cur: e.tensor_tensor(out=pb[:, c, 0:16], in0=cur[:, 16:32], in1=hp[:, c, 16:32], op=ALU.subtract),
                             reads=[ck, ("hp", c)], writes=[("pb", c)])
                for m in range(NC8):
                    gi = m // 2
                    ob = 4 + m % 2
                    T.mm_group([(self.bank[ob][:], pw[:, 2 * gi + kc, (m % 2) * 128:(m % 2 + 1) * 128], pb[:, 2 * gi + kc, :]) for kc in range(2)],
                               reads=[wk, ("pb", 2 * gi), ("pb", 2 * gi + 1)], writes=[("bank", ob)])
                    T.op("dve", lambda e, ob=ob, m=m: e.scalar_tensor_tensor(
                        out=self.xs[:, m, ts], in0=self.bank[ob][:], scalar=self.vec[:, VOFF["pscale"] + m:VOFF["pscale"] + m + 1],
                        in1=self.xs[:, m, ts], op0=ALU.mult, op1=ALU.add),
                        reads=[("bank", ob), ("xs", m, t), "vec"], writes=[("xs", m, t)])
            self.rmsnorm(f"nmix{l}", emit)
            self.wretire(wi)
            T.barrier()


    def plan_attn(self, l):
        a = len(self.wq)
        for pr in range(4):
            for g in range(3):
                self.wplan(("wa", pr, g), [(0, 8, 512, self.d_wattn[:, pr, g, 0:512].rearrange("(k p) n -> p k n", p=128))])
                self.wplan(("wv", pr, g), [(0, 8, 128, self.d_wattn[:, pr, g, 512:640].rearrange("(k p) n -> p k n", p=128))])
            self.wplan(("wo", pr), [(0, 1, D, self.d_wo[pr * 128:(pr + 1) * 128, :].rearrange("(k p) n -> p k n", p=128))])
        return a

    def attn(self, l, wi, seg):
        nc, T = self.nc, self.T
        from contextlib import ExitStack
        self.uid = getattr(self, "uid", 0) + 1
        uid = self.uid
        DIL = (1, 4, 16)
        with ExitStack() as ph:
            B = lambda name, shape, dtype: ph.enter_context(nc.sbuf_tensor(f"{name}_{uid}", shape, dtype))
            self.h = B("h", [128, NC8, TS], BF16)
            cs = [B(f"cs{i}", [128, TT], F32) for i in range(2)]
            sn = [B(f"sn{i}", [128, TT], F32) for i in range(2)]
            Qt = B("Qt", [128, TS], BF16)
            Kt = B("Kt", [128, TS], BF16)
            Va = B("Va", [128, 16, 256], BF16)
            kc = B("kc", [128, TS], BF16)
            vc = B("vc", [128, 16, 256], BF16)
            acc = B("acc", [128, 2, TS], F32)
            OT = B("OT", [128, TS], BF16)
            pt = [B(f"pt{i}", [128, 256], BF16) for i in range(2)]
            tm = [B(f"tm{i}", [128, TT], F32) for i in range(2)]
            rd = tm[0]
            self.norm_to_h(f"nmix{l}")
            T.op("pool", lambda e: e.memset(Va[:], 1.0), writes=["Va"])
            T.op("pool", lambda e: e.memset(vc[:], 1.0), writes=["vc"])
            tabn = [0]
            sbn = [0]
            obn = [0]
            ptn = [0]
            pw = wi
            for pr in range(4):
                T.op("pool", lambda e: e.memset(acc[:], 0.0), reads=[], writes=[("acc", 0), ("acc", 1)])
                for g in range(3):
                    d = DIL[g]
                    nblk = TS // (128 * d)
                    run = TS // d
                    wA, wAk = self.wget(pw)
                    wB, wBk = self.wget(pw + 1)
                    wAv = wA[:, 0:4096].rearrange("p (k n) -> p k n", k=8)
                    wBv = wB[:, 0:1024].rearrange("p (k n) -> p k n", k=8)
                    pairn = 0
                    for kind in range(2):
                        dst = Qt if kind == 0 else Kt
                        dkey = "Qt" if kind == 0 else "Kt"
                        off = kind * 256
                        for t in range(NT):
                            ts = slice(t * TT, (t + 1) * TT)
                            ab, bb = ((0, 1), (2, 3))[pairn % 2]
                            pairn += 1
                            hr = [("h", k, t) for k in range(NC8)]
                            T.mm_group([(self.bank[ab][:], wAv[:, k, off:off + 128], self.h[:, k, ts]) for k in range(NC8)],
                                       reads=[wAk] + hr, writes=[("bank", ab)])
                            T.mm_group([(self.bank[bb][:], wAv[:, k, off + 128:off + 256], self.h[:, k, ts]) for k in range(NC8)],
                                       reads=[wAk] + hr, writes=[("bank", bb)])
                            ti = tabn[0] % 2
                            tabn[0] += 1
                            p0 = seg * TS + t * TT
                            T.dma("sp", cs[ti][:], self.d_cos[:, p0:p0 + TT], writes=[("cs", ti)])
                            T.dma("sp", sn[ti][:], self.d_sin[:, p0:p0 + TT], writes=[("sn", ti)])
                            T.op("dve", lambda e, ab=ab, ti=ti: e.tensor_tensor(out=tm[0][:], in0=self.bank[ab][:], in1=cs[ti][:], op=ALU.mult),
                                 reads=[("bank", ab), ("cs", ti)], writes=[("tm", 0)])
                            T.op("dve", lambda e, bb=bb, ti=ti: e.tensor_tensor(out=tm[1][:], in0=self.bank[bb][:], in1=sn[ti][:], op=ALU.mult),
                                 reads=[("bank", bb), ("sn", ti)], writes=[("tm", 1)])
                            ov = dst[:, :].rearrange("p (r i) -> p r i", r=d)[:, :, t * TT // d:(t + 1) * TT // d]
                            i0 = tm[0][:, :].rearrange("p (i r) -> p r i", r=d)
                            i1 = tm[1][:, :].rearrange("p (i r) -> p r i", r=d)
                            T.op("dve", lambda e, ov=ov, i0=i0, i1=i1: e.tensor_tensor(out=ov, in0=i0, in1=i1, op=ALU.add),
                                 reads=[("tm", 0), ("tm", 1)], writes=[(dkey, t)])
                    qk_all = [("Qt", t) for t in range(NT)] + [("Kt", t) for t in range(NT)]
                    for b4 in range(4):
                        vb = 6 + b4 % 2
                        for bi in range(4):
                            bidx = b4 * 4 + bi
                            r, jb = bidx // nblk, bidx % nblk
                            st = jb * 128 * d + r
                            T.mm_group([(self.bank[vb][:, bi * 128:(bi + 1) * 128], self.h[:, k, st:st + 127 * d + 1:d], wBv[:, k, :]) for k in range(NC8)],
                                       reads=[wBk] + [("h", k, tt) for k in range(NC8) for tt in range(NT)],
                                       writes=[("bank", vb)] if bi == 0 else [])
                        T.res[("bank", vb)] = [(T.sem["pe"], T.cnt["pe"], "pe", "pe"), []]
                        src = self.bank[vb][:, :].rearrange("p (b f) -> p b f", b=4)
                        T.op("act", lambda e, src=src, b4=b4: e.activation(out=Va[:, b4 * 4:(b4 + 1) * 4, 0:64], in_=src[:, :, 0:64], func=AF.Identity),
                             reads=[("bank", vb), "Va"], writes=[("Va", b4, 0)])
                        T.op("act", lambda e, src=src, b4=b4: e.activation(out=Va[:, b4 * 4:(b4 + 1) * 4, 192:256], in_=src[:, :, 64:128], func=AF.Identity),
                             reads=[("bank", vb), "Va"], writes=[("Va", b4, 1)])
                    va_all = [("Va", b4, e_) for b4 in range(4) for e_ in range(2)]
                    self.wretire(pw)
                    self.wretire(pw + 1)
                    pw += 2
                    if seg == 0:
                        T.dma("sp", self.d_kctx[pr, g, :, 0:d * 128].rearrange("p (r i) -> p r i", r=d),
                              Kt[:, :].rearrange("p (r i) -> p r i", r=d)[:, :, (nblk - 1) * 128:nblk * 128],
                              reads=qk_all, writes=[("kctx", pr, g)])
                        T.dma("sp", self.d_vctx[pr, g, :, 0:d * 256].rearrange("p (r f) -> p r f", r=d),
                              Va[:, :, :].rearrange("p (r j) f -> p r j f", r=d)[:, :, nblk - 1, :],
                              reads=va_all, writes=[("vctx", pr, g)])
                    else:
                        T.dma("sp", kc[:, 0:d * 128], self.d_kctx[pr, g, :, 0:d * 128], reads=[("kctx", pr, g)], writes=["kc"])
                        T.dma("sp", vc[:, 0:d, :], self.d_vctx[pr, g, :, 0:d * 256].rearrange("p (r f) -> p r f", r=d),
                              reads=[("vctx", pr, g), "vc"], writes=["vc2"])
                    def emit_pv(job):
                        (e_, r, kb, qlo, nq, vlhs, kr, pi) = job
                        for qi in range(nq):
                            qb = qlo + qi
                            closing = (kb == qb)
                            opening = not closing
                            if opening:
                                ob = 2 + obn[0] % 4
                                obn[0] += 1
                                self._ob_cur = ob
                            else:
                                ob = self._ob_cur if (seg == 1 or qb > 0) else None
                                if ob is None:
                                    ob = 2 + obn[0] % 4
                                    obn[0] += 1
                            first = opening or (seg == 0 and qb == 0)
                            T._deps("pe", [("pt", pi)] + kr, [("bank", ob)] if first else [])
                            inst = nc.tensor.matmul(self.bank[ob][:, 0:128], lhsT=vlhs, rhs=pt[pi][:, qi * 128:(qi + 1) * 128],
                                                    start=first, stop=closing)
                            T.cnt["pe"] += 1
                            inst.then_inc(T.sem["pe"], 1)
                            hh = (T.sem["pe"], T.cnt["pe"], "pe", "pe")
                            T._commit(hh, [("pt", pi)], [("bank", ob)] if closing else [])
                            if closing:
                                av = acc[:, e_, :].rearrange("p (i r) -> p r i", r=d)[:, r, qb * 128:(qb + 1) * 128]
                                T.op("dve", lambda e, ob=ob, av=av: e.tensor_tensor(out=av, in0=self.bank[ob][:, 0:128], in1=av, op=ALU.add),
                                     reads=[("bank", ob), ("acc", e_)], writes=[("acc", e_)])

                    pend_pv = None
                    for e_ in range(2):
                        ps = slice(64 * e_, 64 * e_ + 64)
                        for r in range(d):
                            kbs = ([-1] if seg == 1 else []) + list(range(nblk))
                            for kb in kbs:
                                if kb < 0:
                                    klhs = kc[ps, r * 128:(r + 1) * 128]
                                    vlhs = vc[:, r, e_ * 128:(e_ + 1) * 128]
                                    kr = ["kc", "vc2"]
                                    qlo, nq, moff = 0, 1, 128
                                else:
                                    klhs = Kt[ps, r * run + kb * 128:r * run + (kb + 1) * 128]
                                    vlhs = Va[:, r * nblk + kb, e_ * 128:(e_ + 1) * 128]
                                    kr = qk_all + va_all
                                    qlo, nq, moff = kb, (2 if kb + 1 < nblk else 1), 0
                                sb = sbn[0] % 2
                                sbn[0] += 1
                                ncol = nq * 128
                                T.mm_group([(self.bank[sb][:, 0:ncol], klhs, Qt[ps, r * run + qlo * 128:r * run + qlo * 128 + ncol]),
                                            (self.bank[sb][:, 0:ncol], self.ident_bf, self.maskb[:, moff:moff + ncol])],
                                           reads=kr + ["cbf"], writes=[("bank", sb)])
                                pi = ptn[0] % 2
                                ptn[0] += 1
                                T.op("act", lambda e, sb=sb, pi=pi, ncol=ncol: e.activation(out=pt[pi][:, 0:ncol], in_=self.bank[sb][:, 0:ncol], func=AF.Exp, scale=0.125),
                                     reads=[("bank", sb)], writes=[("pt", pi)])
                                if pend_pv is not None:
                                    emit_pv(pend_pv)
                                pend_pv = (e_, r, kb, qlo, nq, vlhs, kr, pi)
                    emit_pv(pend_pv)
                wo, wok = self.wget(pw)
                wov = wo[:, 0:D]
                for t in range(NT):
                    ts = slice(t * TT, (t + 1) * TT)
                    for e_ in range(2):
                        ps = slice(64 * e_, 64 * e_ + 64)
                        T.mm_group([(self.bank[0][:], self.shift32, acc[:, e_, ts])], reads=[("acc", e_), "c32"], writes=[("bank", 0)])
                        T.op("dve", lambda e, ps=ps: e.reciprocal(out=rd[ps, :], in_=self.bank[0][ps, :]),
                             reads=[("bank", 0)], writes=[("tm", 0)])
                        T.op("dve", lambda e, ps=ps, e_=e_: e.tensor_tensor(out=OT[ps, ts], in0=acc[ps, e_, ts], in1=rd[ps, :], op=ALU.mult),
                             reads=[("tm", 0), ("acc", e_)], writes=[("OT", t, e_)])
                    for m in range(NC8):
                        ob = 4 + m % 2
                        T.mm_group([(self.bank[ob][:], wov[:, m * 128:(m + 1) * 128], OT[:, ts])],
                                   reads=[wok, ("OT", t, 0), ("OT", t, 1)], writes=[("bank", ob)])
                        T.op("dve", lambda e, ob=ob, m=m: e.tensor_tensor(out=self.xs[:, m, ts], in0=self.bank[ob][:], in1=self.xs[:, m, ts], op=ALU.add),
                             reads=[("bank", ob), ("xs", m, t)], writes=[("xs", m, t)])
                self.wretire(pw)
                pw += 1
            T.barrier()

    def store_out(self, seg, final_norm):
        T = self.T
        if final_norm:
            g0 = VOFF["nfinal"]

            def emit(t, rs, rkey):
                ts = slice(t * TT, (t + 1) * TT)
                for c in range(NC8):
                    T.op("dve", lambda e, c=c: e.scalar_tensor_tensor(
                        out=self.xs[:, c, ts], in0=self.xs[:, c, ts], scalar=self.vec[:, g0 + c:g0 + c + 1],
                        in1=rs[:], op0=ALU.mult, op1=ALU.mult),
                        reads=[("xs", c, t), rkey], writes=[("xs", c, t)])
            self.rmsnorm("nfinal", emit)
        for c in range(NC8):
            T.dma("sp", self.d_out[c * 128:(c + 1) * 128, seg * TS:(seg + 1) * TS], self.xs[:, c, :],
                  reads=[("xs", c, t) for t in range(NT)], writes=[("out", c, seg)])

    def build(self):
        nc, T = self.nc, self.T
        from contextlib import ExitStack
        with ExitStack() as es:
            A = lambda name, shape, dtype: es.enter_context(nc.sbuf_tensor(name, shape, dtype))
            self.xs = A("xs", [128, NC8, TS], F32)
            self.vec = A("vec_sb", [128, NV], F32)
            self.cbf = A("cbf", [128, 512], BF16)
            self.c32 = A("c32", [128, 256], F32)
            self.eps_rms = A("eps_rms", [128, 1], F32)
            self.eps_ln = A("eps_ln", [128, 1], F32)
            self.ring = [A(f"ring{i}", [128, SLOT], BF16) for i in range(NSLOT)]
            self.sq = [A(f"sq{i}", [128, TT], BF16) for i in range(2)]
            self.rstd = [A(f"rstd{i}", [128, TT], F32) for i in range(2)]
            self.ccarry = A("ccarry", [128, 2, NC8, 32], BF16)
            self.pcarry = A("pcarry", [128, NC8, 16], F32)
            self.bank = [es.enter_context(nc.psum_tensor(f"bank{i}", [128, TT], F32)) for i in range(8)]
            self.ident_bf = self.cbf[:, 0:128]
            self.ones_bf = self.cbf[:, 128:256]
            self.maskb = self.cbf[:, 256:512]
            self.ones32 = self.c32[:, 0:128]
            self.shift32 = self.c32[:, 128:256]

            T.dma("sp", self.vec[:], self.d_vec[:, :], writes=["vec"])
            T.dma("sp", self.cbf[:], self.d_cbf[:, :], writes=["cbf"])
            T.dma("sp", self.c32[:], self.d_c32[:, :], writes=["c32"])
            T.op("dve", lambda e: e.memset(self.eps_rms[:], RMS_EPS), writes=["eps"])
            T.op("dve", lambda e: e.memset(self.eps_ln[:], LN_EPS), writes=["eps2"])
            T.op("dve", lambda e: e.memset(self.ccarry[:], 0.0), writes=["cc0"])
            T.op("dve", lambda e: e.memset(self.pcarry[:], 0.0), writes=["pc0"])
            if any(k == "conv" for (k, _) in self.plan):
                with ExitStack() as st:
                    dgs = [st.enter_context(nc.sbuf_tensor(f"dgst{i}", [128, CW, 128], BF16)) for i in range(2)]
                    wdw = st.enter_context(nc.sbuf_tensor("wdw_all", [128, 2, NC8, CW], F32))
                    T.dma("sp", wdw[:], self.d_cwdw[:, :, :, :], writes=["wdw"])
                    n = 0
                    for j in range(2):
                        for c in range(NC8):
                            dd = dgs[n % 2]
                            for k in range(CW):
                                eng = "dve" if k < 22 else "act"
                                if eng == "dve":
                                    T.op("dve", lambda e, dd=dd, k=k, j=j, c=c: e.tensor_scalar(
                                        out=dd[:, k, :], in0=self.ident_bf, scalar1=wdw[:, j, c, k:k + 1], scalar2=None, op0=ALU.mult),
                                        reads=["wdw", "cbf"], writes=[("dgs", n % 2, k)])
                                else:
                                    T.op("act", lambda e, dd=dd, k=k, j=j, c=c: e.activation(
                                        out=dd[:, k, :], in_=self.ident_bf, func=AF.Copy, scale=wdw[:, j, c, k:k + 1]),
                                        reads=["wdw", "cbf"], writes=[("dgs", n % 2, k)])
                            T.dma("sp", self.d_dgd[j, c, :, :].rearrange("p (k n) -> p k n", k=CW), dd[:],
                                  reads=[("dgs", n % 2, k) for k in range(CW)], writes=[("dgd", j, c)])
                            n += 1
                    T.barrier()
            T.barrier()

            sched = []
            for seg in range(NSEG):
                for (kind, l) in self.plan:
                    if kind == "ffn":
                        sched.append((seg, kind, l, self.plan_ffn(l)))
                    elif kind == "conv":
                        sched.append((seg, kind, l, self.plan_conv(l)))
                    elif kind == "pool":
                        sched.append((seg, kind, l, self.plan_pool(l)))
                    else:
                        sched.append((seg, kind, l, self.plan_attn(l)))

            cur_seg = -1
            for (seg, kind, l, widx) in sched:
                if seg != cur_seg:
                    if cur_seg >= 0:
                        self.store_out(cur_seg, self.final_norm)
                    cur_seg = seg
                    for c in range(NC8):
                        T.dma("sp", self.xs[:, c, :], self.d_xT[c * 128:(c + 1) * 128, seg * TS:(seg + 1) * TS],
                              writes=[("xs", c, t) for t in range(NT)])
                if kind == "ffn":
                    with ExitStack() as ph:
                        self.uid = getattr(self, "uid", 0) + 1
                        B = lambda name, shape, dtype: ph.enter_context(nc.sbuf_tensor(f"{name}_{self.uid}", shape, dtype))
                        self.h = B("h", [128, NC8, TS], BF16)
                        self.abuf = [B(f"abuf{i}", [128, 4, TT], BF16) for i in range(2)]
                        self.sg = [B(f"sg{i}", [128, TT], F32) for i in range(2)]
                        self.ffn(l, widx)
                        T.barrier()
                elif kind == "conv":
                    self.conv(l, widx, seg)
                elif kind == "pool":
                    self.pool(l, widx, seg)
                elif kind == "attn":
                    self.attn(l, widx, seg)
            self.store_out(cur_seg, self.final_norm)
            T.barrier()
        return nc


def _chunked(v):
    v = np.asarray(v, np.float32)
    return np.ascontiguousarray(v.reshape(-1, 128).T)


def prep_shared(inp):
    f32 = np.float32
    vec = np.zeros((128, NV), f32)
    for l in range(DEPTH):
        vec[:, VOFF[f"nmix{l}"]:VOFF[f"nmix{l}"] + 8] = _chunked(inp["norm_mix_g"][l])
        vec[:, VOFF[f"nffn{l}"]:VOFF[f"nffn{l}"] + 8] = _chunked(inp["norm_ffn_g"][l])
    vec[:, VOFF["nfinal"]:VOFF["nfinal"] + 8] = _chunked(inp["final_norm_g"])
    for j in range(2):
        vec[:, VOFF[f"cba{j}"]:VOFF[f"cba{j}"] + 8] = _chunked(inp["conv_b_in"][j][:D])
        vec[:, VOFF[f"cbg{j}"]:VOFF[f"cbg{j}"] + 8] = _chunked(inp["conv_b_in"][j][D:])
        vec[:, VOFF[f"cbdw{j}"]:VOFF[f"cbdw{j}"] + 8] = _chunked(inp["conv_b_dw"][j])
        vec[:, VOFF[f"clng{j}"]:VOFF[f"clng{j}"] + 8] = _chunked(inp["conv_ln_g"][j])
        vec[:, VOFF[f"clnb{j}"]:VOFF[f"clnb{j}"] + 8] = _chunked(inp["conv_ln_b"][j])
        vec[:, VOFF[f"cbout{j}"]:VOFF[f"cbout{j}"] + 8] = _chunked(inp["conv_b_out"][j])
    vec[:, VOFF["pscale"]:VOFF["pscale"] + 8] = _chunked(inp["pool_scale"][0])

    cw = np.asarray(inp["conv_w_in"], f32)
    cwin_r = np.ascontiguousarray(np.stack([cw[:, :, :D].reshape(2, D, 8, 128),
                                            cw[:, :, D:].reshape(2, D, 8, 128)], axis=3).reshape(2, D, 8, 256))
    dw = np.asarray(inp["conv_w_dw"], f32)
    cwdw_r = np.ascontiguousarray(dw.reshape(2, CW, 8, 128).transpose(3, 0, 2, 1))

    pinv = np.zeros((128, 4, 16), f32)
    for g, w in enumerate((2, 4, 8, 16)):
        pinv[:, g, :] = 1.0 / np.minimum(np.arange(1, 17), w).astype(f32)

    wqkv = np.asarray(inp["attn_w_qkv"], f32)[0]
    wq = wqkv[:, 0:1536].reshape(D, 3, 8, 64)
    wk = wqkv[:, 1536:3072].reshape(D, 3, 8, 64)
    wv = wqkv[:, 3072:4608].reshape(D, 3, 8, 64)
    swap = np.concatenate([np.arange(32, 64), np.arange(0, 32)])
    w_r = np.zeros((D, 4, 3, 5, 2, 64), f32)
    for pr in range(4):
        for g in range(3):
            for e in range(2):
                hs = 2 * pr + e
                w_r[:, pr, g, 0, e] = wq[:, g, hs]
                w_r[:, pr, g, 1, e] = wq[:, g, hs][:, swap]
                w_r[:, pr, g, 2, e] = wk[:, g, hs]
                w_r[:, pr, g, 3, e] = wk[:, g, hs][:, swap]
                w_r[:, pr, g, 4, e] = wv[:, g, hs]
    w_r = np.ascontiguousarray(w_r.reshape(D, 4, 3, 640))

    half = 32
    inv_freq = (10000.0 ** (-np.arange(half, dtype=f32) / half)).astype(f32)
    ang = np.arange(S, dtype=f32)[None, :] * inv_freq[:, None]
    cos = np.cos(ang).astype(f32)
    sin = np.sin(ang).astype(f32)
    cos_t = np.concatenate([cos, cos, cos, cos], axis=0)
    sin_t = np.concatenate([-sin, sin, -sin, sin], axis=0)

    cbf = np.zeros((128, 512), f32)
    cbf[:, 0:128] = np.eye(128, dtype=f32)
    cbf[:, 128:256] = 1.0
    a = np.arange(128)[:, None]
    b = np.arange(256)[None, :]
    cbf[:, 256:512] = np.where((b >= a) & (b <= a + 128), 0.0, -30000.0)
    c32 = np.zeros((128, 256), f32)
    c32[:, 0:128] = 1.0
    c32[:, 128:256] = np.roll(np.eye(128, dtype=f32), 64, axis=0)

    return {
        "vec": vec,
        "ffn_w_gate": np.ascontiguousarray(inp["ffn_w_gate"], f32),
        "ffn_w_up": np.ascontiguousarray(inp["ffn_w_up"], f32),
        "ffn_w_down": np.ascontiguousarray(inp["ffn_w_down"], f32),
        "conv_w_in_r": cwin_r,
        "conv_w_out": np.ascontiguousarray(inp["conv_w_out"], f32),
        "conv_w_dw_r": cwdw_r,
        "pool_w": np.ascontiguousarray(inp["pool_w"], f32)[0],
        "pool_inv": pinv,
        "attn_w_r": w_r,
        "attn_w_o": np.ascontiguousarray(inp["attn_w_o"], f32)[0],
        "rope_cos": np.ascontiguousarray(cos_t),
        "rope_sin": np.ascontiguousarray(sin_t),
        "const_bf": cbf.astype(ml_dtypes.bfloat16),
        "const_f32": c32,
    }


FULL_PLAN = [("conv", 0), ("ffn", 0), ("attn", 1), ("ffn", 1), ("pool", 2), ("ffn", 2), ("conv", 3), ("ffn", 3)]


def run(inputs, plan=FULL_PLAN, final_norm=True, cores=8):
    inp = {k: np.asarray(v) for k, v in inputs.items()}
    shared = prep_shared(inp)
    x = np.asarray(inp["x"], np.float32)
    in_maps = []
    for b in range(cores):
        m = dict(shared)
        m["xT"] = np.ascontiguousarray(x[b].T)
        in_maps.append(m)
    nc = Prog(plan, final_norm).build()
    res = run_bass_kernel_spmd(nc, in_maps, core_ids=list(range(cores)))
    out = np.stack([np.ascontiguousarray(r["yT"].T) for r in res.results], axis=0)
    return out.astype(np.float32)


def kernel(**inputs):
    return run(inputs)
```

```python
import numpy as np
import ml_dtypes
import concourse.bass as bass
import concourse.mybir as mybir
from concourse.bass_utils import run_bass_kernel_spmd

F32 = mybir.dt.float32
BF16 = mybir.dt.bfloat16
AF = mybir.ActivationFunctionType
ALU = mybir.AluOpType

D = 1024
S = 4096
NC8 = 8
TS = 2048
NSEG = S // TS
TT = 512
NT = TS // TT
FH = 2816
NHC = FH // 128
DEPTH = 4
CW = 31
RMS_EPS = 1e-6
LN_EPS = 1e-5
SLOT = 4096
NSLOT = 5

VEC_NAMES = []
for _l in range(DEPTH):
    VEC_NAMES += [f"nmix{_l}", f"nffn{_l}"]
VEC_NAMES += ["nfinal"]
for _j in range(2):
    VEC_NAMES += [f"cba{_j}", f"cbg{_j}", f"cbdw{_j}", f"clng{_j}", f"clnb{_j}", f"cbout{_j}"]
VEC_NAMES += ["pscale"]
VOFF = {n: 8 * i for i, n in enumerate(VEC_NAMES)}
NV = 8 * len(VEC_NAMES)


class Tracker:
    def __init__(self, nc):
        self.nc = nc
        self.engs = {"pe": nc.tensor, "act": nc.scalar, "dve": nc.vector,
                     "pool": nc.gpsimd, "sp": nc.sync}
        self.sem = {k: nc.alloc_semaphore(f"prog_{k}") for k in ("pe", "act", "dve", "pool")}
        self.cnt = {k: 0 for k in self.sem}
        self.waited = {k: {} for k in self.engs}
        self.dpool = {}
        for q in ("sp", "pool"):
            self.dpool[q] = [[nc.alloc_semaphore(f"dma_{q}_{i}"), 0] for i in range(12)]
        self.dnext = {"sp": 0, "pool": 0}
        self.res = {}
        self.nwaits = 0

    def _wait(self, eng, h):
        if h is None:
            return
        sem, val, src, key = h
        if eng == "pe" and src == "pe":
            return
        if self.waited[eng].get(key, 0) >= val:
            return
        self.engs[eng].wait_ge(sem, val)
        self.waited[eng][key] = val
        self.nwaits += 1

    def _deps(self, eng, reads, writes):
        for k in reads:
            r = self.res.get(k)
            if r is not None:
                self._wait(eng, r[0])
        for k in writes:
            r = self.res.get(k)
            if r is not None:
                self._wait(eng, r[0])
                for h in r[1]:
                    self._wait(eng, h)

    def _commit(self, h, reads, writes):
        for k in reads:
            r = self.res.setdefault(k, [None, []])
            if h[2] in ("pe", "act", "dve", "pool") and h[3] == h[2]:
                r[1] = [x for x in r[1] if x[3] != h[3]]
            r[1].append(h)
        for k in writes:
            self.res[k] = [h, []]

    def op(self, eng, fn, reads=(), writes=()):
        self._deps(eng, reads, writes)
        inst = fn(self.engs[eng])
        self.cnt[eng] += 1
        inst.then_inc(self.sem[eng], 1)
        h = (self.sem[eng], self.cnt[eng], eng, eng)
        self._commit(h, reads, writes)
        return h

    def mm_group(self, mms, reads=(), writes=()):
        self._deps("pe", reads, writes)
        n = len(mms)
        inst = None
        for i, (o, l, r) in enumerate(mms):
            inst = self.nc.tensor.matmul(o, lhsT=l, rhs=r, start=(i == 0), stop=(i == n - 1))
        self.cnt["pe"] += 1
        inst.then_inc(self.sem["pe"], 1)
        h = (self.sem["pe"], self.cnt["pe"], "pe", "pe")
        self._commit(h, reads, writes)
        return h

    def dma(self, q, out, in_, reads=(), writes=()):
        pool = self.dpool[q]
        i = self.dnext[q]
        self.dnext[q] = (i + 1) % len(pool)
        sem, c = pool[i]
        key = f"d{q}{i}"
        self._wait(q, (sem, c, q, key))
        self._deps(q, reads, writes)
        self.engs[q].dma_start(out=out, in_=in_).then_inc(sem, 16)
        pool[i][1] = c + 16
        h = (sem, c + 16, q, key)
        self._commit(h, reads, writes)
        return h

    def barrier(self):
        hs = [(self.sem[k], self.cnt[k], k, k) for k in self.sem if self.cnt[k] > 0]
        for q in ("sp", "pool"):
            for i, (sem, c) in enumerate(self.dpool[q]):
                if c > 0:
                    hs.append((sem, c, q, f"d{q}{i}"))
        for e in ("pe", "act", "dve", "pool", "sp"):
            for h in hs:
                if e == "pe" and h[2] == "pe":
                    continue
                self._wait(e, h)
        self.res = {}


class Prog:
    def __init__(self, plan, final_norm=True):
        self.plan = plan
        self.final_norm = final_norm
        nc = bass.Bass("TRN2", target_bir_lowering=False)
        self.nc = nc
        self.T = Tracker(nc)
        dt = nc.dram_tensor
        self.d_xT = dt("xT", [D, S], F32, kind="ExternalInput").ap()
        self.d_vec = dt("vec", [128, NV], F32, kind="ExternalInput").ap()
        self.d_wg = dt("ffn_w_gate", [DEPTH, D, FH], F32, kind="ExternalInput").ap()
        self.d_wu = dt("ffn_w_up", [DEPTH, D, FH], F32, kind="ExternalInput").ap()
        self.d_wd = dt("ffn_w_down", [DEPTH, FH, D], F32, kind="ExternalInput").ap()
        self.d_cwin = dt("conv_w_in_r", [2, D, 8, 256], F32, kind="ExternalInput").ap()
        self.d_cwout = dt("conv_w_out", [2, D, D], F32, kind="ExternalInput").ap()
        self.d_cwdw = dt("conv_w_dw_r", [128, 2, 8, CW], F32, kind="ExternalInput").ap()
        self.d_pw = dt("pool_w", [4, 256, 256], F32, kind="ExternalInput").ap()
        self.d_pinv = dt("pool_inv", [128, 4, 16], F32, kind="ExternalInput").ap()
        self.d_wattn = dt("attn_w_r", [D, 4, 3, 640], F32, kind="ExternalInput").ap()
        self.d_wo = dt("attn_w_o", [512, D], F32, kind="ExternalInput").ap()
        self.d_cos = dt("rope_cos", [128, S], F32, kind="ExternalInput").ap()
        self.d_sin = dt("rope_sin", [128, S], F32, kind="ExternalInput").ap()
        self.d_cbf = dt("const_bf", [128, 128 + 128 + 256], BF16, kind="ExternalInput").ap()
        self.d_c32 = dt("const_f32", [128, 128 + 128], F32, kind="ExternalInput").ap()
        self.d_kctx = dt("kctx", [4, 3, 128, TS], BF16, kind="Internal").ap()
        self.d_vctx = dt("vctx", [4, 3, 128, 16 * 256], BF16, kind="Internal").ap()
        self.d_dgd = dt("dgd", [2, NC8, 128, CW * 128], BF16, kind="Internal").ap()
        self.d_out = dt("yT", [D, S], F32, kind="ExternalOutput").ap()
        self.wq = []
        self.wissued = 0
        self.wretired = set()
        self.wloaded = {}

    def wplan(self, key, src_aps):
        self.wq.append((key, src_aps))

    def wpump(self):
        T = self.T
        while self.wissued < len(self.wq) and (self.wissued < NSLOT or (self.wissued - NSLOT) in self.wretired):
            i = self.wissued
            slot = i % NSLOT
            key, srcs = self.wq[i]
            assert len(srcs) == 1
            (off, k, n, ap) = srcs[0]
            dst = self.ring[slot][:, off:off + k * n].rearrange("p (k n) -> p k n", k=k)
            T.dma("pool", dst, ap, reads=(), writes=[("ring", slot)])
            self.wissued += 1

    def wretire(self, idx):
        self.wretired.add(idx)
        self.wpump()

    def wget(self, idx):
        self.wpump()
        assert idx < self.wissued, f"weight ring deadlock at piece {idx}"
        return self.ring[idx % NSLOT], ("ring", idx % NSLOT)

    def rmsnorm(self, gname, emit):
        for t in range(NT):
            self.rmsnorm_tile(t, emit)

    def rmsnorm_tile(self, t, emit):
        nc, T = self.nc, self.T
        ts = slice(t * TT, (t + 1) * TT)
        ssb = self.bank[7]
        for c in range(NC8):
            sq = self.sq[c % 2]
            T.op("act", lambda e, c=c, sq=sq: e.activation(out=sq[:], in_=self.xs[:, c, ts], func=AF.Square),
                 reads=[("xs", c, t)], writes=[("sq", c % 2)])
            self.mm_one(ssb[:], self.ones_bf[:], sq[:], c == 0, c == NC8 - 1, [("sq", c % 2)], ("bank", 7))
        rs = self.rstd[t % 2]
        T.op("act", lambda e, rs=rs: e.activation(out=rs[:], in_=ssb[:], func=AF.Ln, bias=self.eps_rms[:, 0:1], scale=1.0 / D),
             reads=[("bank", 7)], writes=[("rstd", t % 2)])
        T.op("act", lambda e, rs=rs: e.activation(out=rs[:], in_=rs[:], func=AF.Exp, scale=-0.5),
             reads=[("rstd", t % 2)], writes=[("rstd", t % 2)])
        emit(t, rs, ("rstd", t % 2))

    def h_emit(self, gname):
        T = self.T
        g0 = VOFF[gname]

        def emit(t, rs, rkey):
            ts = slice(t * TT, (t + 1) * TT)
            for c in range(NC8):
                T.op("dve", lambda e, c=c: e.scalar_tensor_tensor(
                    out=self.h[:, c, ts], in0=self.xs[:, c, ts], scalar=self.vec[:, g0 + c:g0 + c + 1],
                    in1=rs[:], op0=ALU.mult, op1=ALU.mult),
                    reads=[("xs", c, t), rkey], writes=[("h", c, t)])
        return emit

    def norm_to_h(self, gname):
        self.rmsnorm(gname, self.h_emit(gname))

    def plan_ffn(self, l):
        groups = [(0, 4), (4, 4), (8, 4), (12, 4), (16, 4), (20, 2)]
        idx = []
        for (j0, nj) in groups:
            a = len(self.wq)
            self.wplan(("wg", l, j0), [(0, 8, nj * 128, self.d_wg[l, :, j0 * 128:(j0 + nj) * 128].rearrange("(k p) n -> p k n", p=128))])
            self.wplan(("wu", l, j0), [(0, 8, nj * 128, self.d_wu[l, :, j0 * 128:(j0 + nj) * 128].rearrange("(k p) n -> p k n", p=128))])
            self.wplan(("wd", l, j0), [(0, nj, D, self.d_wd[l, j0 * 128:(j0 + nj) * 128, :].rearrange("(k p) n -> p k n", p=128))])
            idx.append((j0, nj, a))
        return idx

    def ffn(self, l, widx):
        nc, T = self.nc, self.T
        hemit = self.h_emit(f"nffn{l}")
        self.rmsnorm_tile(0, hemit)
        self.rmsnorm_tile(1, hemit)
        first_group = True
        pend = None
        step = 0
        gu_banks = [(0, 1), (2, 3)]
        d_banks = [4, 5]
        dcount = [0]

        def down(st):
            (t, nj, wdk, wdslot, ab, wdi) = st
            ts = slice(t * TT, (t + 1) * TT)
            wdv = wdslot[:, 0:nj * D].rearrange("p (k n) -> p k n", k=nj)
            for m in range(NC8):
                b = d_banks[dcount[0] % 2]
                dcount[0] += 1
                T.mm_group([(self.bank[b][:], wdv[:, j, m * 128:(m + 1) * 128], self.abuf[ab][:, j, :]) for j in range(nj)],
                           reads=[wdk] + [("a", ab, j) for j in range(nj)], writes=[("bank", b)])
                T.op("dve", lambda e, b=b, m=m: e.tensor_tensor(out=self.xs[:, m, ts], in0=self.bank[b][:], in1=self.xs[:, m, ts], op=ALU.add),
                     reads=[("bank", b), ("xs", m, t)], writes=[("xs", m, t)])
            if t == NT - 1:
                self.wretire(wdi)

        for (j0, nj, wi) in widx:
            wgs, wgk = self.wget(wi)
            wus, wuk = self.wget(wi + 1)
            wds, wdk = self.wget(wi + 2)
            wgv = wgs[:, 0:8 * nj * 128].rearrange("p (k n) -> p k n", k=8)
            wuv = wus[:, 0:8 * nj * 128].rearrange("p (k n) -> p k n", k=8)
            for t in range(NT):
                ts = slice(t * TT, (t + 1) * TT)
                ab = step % 2
                for j in range(nj):
                    gb, ub = gu_banks[j % 2]
                    hr = [("h", k, t) for k in range(NC8)]
                    T.mm_group([(self.bank[gb][:], wgv[:, k, j * 128:(j + 1) * 128], self.h[:, k, ts]) for k in range(NC8)],
                               reads=[wgk] + hr, writes=[("bank", gb)])
                    T.mm_group([(self.bank[ub][:], wuv[:, k, j * 128:(j + 1) * 128], self.h[:, k, ts]) for k in range(NC8)],
                               reads=[wuk] + hr, writes=[("bank", ub)])
                    sg = self.sg[j % 2]
                    T.op("act", lambda e, gb=gb, sg=sg: e.activation(out=sg[:], in_=self.bank[gb][:], func=AF.Silu),
                         reads=[("bank", gb)], writes=[("sg", j % 2)])
                    T.op("dve", lambda e, ub=ub, sg=sg, j=j, ab=ab: e.tensor_tensor(out=self.abuf[ab][:, j, :], in0=self.bank[ub][:], in1=sg[:], op=ALU.mult),
                         reads=[("bank", ub), ("sg", j % 2)], writes=[("a", ab, j)])
                if first_group and t + 2 < NT:
                    self.rmsnorm_tile(t + 2, hemit)
                if pend is not None:
                    down(pend)
                pend = (t, nj, wdk, wds, ab, wi + 2)
                step += 1
            first_group = False
            self.wretire(wi)
            self.wretire(wi + 1)
        down(pend)


    def mm_one(self, out, lhsT, rhs, first, last, reads, wkey):
        T = self.T
        T._deps("pe", reads, [wkey] if first else [])
        inst = self.nc.tensor.matmul(out, lhsT=lhsT, rhs=rhs, start=first, stop=last)
        T.cnt["pe"] += 1
        inst.then_inc(T.sem["pe"], 1)
        h = (T.sem["pe"], T.cnt["pe"], "pe", "pe")
        T._commit(h, reads, [wkey] if last else [])
        return h

    def plan_conv(self, l):
        j = l // 3
        a = len(self.wq)
        for c in range(NC8):
            self.wplan(("cwin", j, c), [(0, 8, 256, self.d_cwin[j, :, c, :].rearrange("(k p) n -> p k n", p=128))])
        idx = {"win": a, "d": {}, "wo": {}}

        def add_d(t):
            for c in range(NC8):
                idx["d"][(t, c)] = len(self.wq)
                self.wplan(("dg", j, c), [(0, CW, 128, self.d_dgd[j, c, :, :].rearrange("p (k n) -> p k n", k=CW))])

        def add_wo(t):
            i0 = len(self.wq)
            for hf in range(2):
                self.wplan(("cwout", j, hf), [(0, 8, 512, self.d_cwout[j, :, hf * 512:(hf + 1) * 512].rearrange("(k p) n -> p k n", p=128))])
            idx["wo"][t] = (i0, i0 + 1)
        for t in range(NT + 1):
            if t < NT:
                add_d(t)
            if t > 0:
                add_wo(t - 1)
        return idx

    def conv(self, l, widx, seg):
        nc, T = self.nc, self.T
        j = l // 3
        wi = widx["win"]
        from contextlib import ExitStack
        self.uid = getattr(self, "uid", 0) + 1
        uid = self.uid
        with ExitStack() as outer:
            O = lambda name, shape, dtype: outer.enter_context(nc.sbuf_tensor(f"{name}_{uid}", shape, dtype))
            u = O("u", [128, NC8, 32 + TS], BF16)
            with ExitStack() as ph:
                B = lambda name, shape, dtype: ph.enter_context(nc.sbuf_tensor(f"{name}_{uid}", shape, dtype))
                self.h = B("h", [128, NC8, TS], BF16)
                sig = [B(f"sig{i}", [128, TT], F32) for i in range(2)]
                hemit = self.h_emit(f"nmix{l}")
                self.rmsnorm_tile(0, hemit)
                self.rmsnorm_tile(1, hemit)
                for c in range(NC8):
                    T.op("dve", lambda e, c=c: e.tensor_copy(out=u[:, c, 0:32], in_=self.ccarry[:, j, c, :]),
                         reads=[("ccarry", j, c)], writes=[("u", c, -1)])
                pair = 0
                for c in range(NC8):
                    ws, wk = self.wget(wi + c)
                    wv = ws[:, 0:2048].rearrange("p (k n) -> p k n", k=8)
                    for t in range(NT):
                        ts = slice(t * TT, (t + 1) * TT)
                        ab, gb = ((0, 1), (2, 3))[pair % 2]
                        pair += 1
                        hr = [("h", k, t) for k in range(NC8)]
                        T.mm_group([(self.bank[ab][:], wv[:, k, 0:128], self.h[:, k, ts]) for k in range(NC8)],
                                   reads=[wk] + hr, writes=[("bank", ab)])
                        T.mm_group([(self.bank[gb][:], wv[:, k, 128:256], self.h[:, k, ts]) for k in range(NC8)],
                                   reads=[wk] + hr, writes=[("bank", gb)])
                        sg = sig[pair % 2]
                        T.op("act", lambda e, gb=gb, sg=sg, c=c: e.activation(
                            out=sg[:], in_=self.bank[gb][:], func=AF.Sigmoid,
                            bias=self.vec[:, VOFF[f"cbg{j}"] + c:VOFF[f"cbg{j}"] + c + 1], scale=1.0),
                            reads=[("bank", gb), "vec"], writes=[("sig", pair % 2)])
                        T.op("dve", lambda e, ab=ab, sg=sg, c=c, t=t: e.scalar_tensor_tensor(
                            out=u[:, c, 32 + t * TT:32 + (t + 1) * TT], in0=self.bank[ab][:],
                            scalar=self.vec[:, VOFF[f"cba{j}"] + c:VOFF[f"cba{j}"] + c + 1], in1=sg[:],
                            op0=ALU.add, op1=ALU.mult),
                            reads=[("bank", ab), ("sig", pair % 2)], writes=[("u", c, t)])
                        if c == 0 and t + 2 < NT:
                            self.rmsnorm_tile(t + 2, hemit)
                    self.wretire(wi + c)
                for c in range(NC8):
                    T.op("dve", lambda e, c=c: e.tensor_copy(out=self.ccarry[:, j, c, :], in_=u[:, c, TS:TS + 32]),
                         reads=[("u", c, NT - 1)], writes=[("ccarry", j, c)])
                T.barrier()
            with ExitStack() as ph:
                B = lambda name, shape, dtype: ph.enter_context(nc.sbuf_tensor(f"{name}_{uid}", shape, dtype))
                v32 = B("v32", [128, 12, TT], F32)
                vslot = lambda t, c: ((t % 2) * 4 + c) if c < 4 else (8 + c - 4)
                vsq = self.sq
                z = B("z", [128, NC8, TT], BF16)
                mean = self.rstd[0]
                var = self.rstd[1]
                VO = lambda n, c: self.vec[:, VOFF[f"{n}{j}"] + c:VOFF[f"{n}{j}"] + c + 1]
                itc = [0]

                def conv_chunk(t, c):
                    it = itc[0]
                    itc[0] += 1
                    vs = vslot(t, c)
                    di = widx["d"][(t, c)]
                    dslot, dkey = self.wget(di)
                    d = dslot[:, 0:CW * 128].rearrange("p (k n) -> p k n", k=CW)
                    cb = it % 2
                    base = 32 + t * TT - (CW - 1)
                    T.mm_group([(self.bank[cb][:], d[:, k, :], u[:, c, base + k:base + k + TT]) for k in range(CW)],
                               reads=[dkey, ("u", c, t), ("u", c, t - 1)], writes=[("bank", cb)])
                    self.wretire(di)
                    T.op("act", lambda e, cb=cb, c=c, vs=vs: e.activation(out=v32[:, vs, :], in_=self.bank[cb][:], func=AF.Identity,
                                                                   bias=VO("cbdw", c), scale=1.0),
                         reads=[("bank", cb), "vec"], writes=[("v32", vs)])
                    T.op("act", lambda e, cb=cb, c=c: e.activation(out=vsq[it % 2][:], in_=self.bank[cb][:], func=AF.Square,
                                                                   bias=VO("cbdw", c), scale=1.0),
                         reads=[("bank", cb), "vec"], writes=[("vsq", it % 2)])
                    s1b, s2b = (2, 3) if t % 2 == 0 else (6, 7)
                    self.mm_one(self.bank[s1b][:], self.ones32, v32[:, vs, :], c == 0, c == NC8 - 1, [("v32", vs), "c32"], ("bank", s1b))
                    self.mm_one(self.bank[s2b][:], self.ones_bf, vsq[it % 2][:], c == 0, c == NC8 - 1, [("vsq", it % 2), "cbf"], ("bank", s2b))

                def ln_dve(t):
                    s1b, s2b = (2, 3) if t % 2 == 0 else (6, 7)
                    T.op("act", lambda e: e.activation(out=mean[:], in_=self.bank[s1b][:], func=AF.Identity, scale=1.0 / D),
                         reads=[("bank", s1b)], writes=["mean"])
                    T.op("dve", lambda e: e.tensor_tensor(out=var[:], in0=mean[:], in1=mean[:], op=ALU.mult),
                         reads=["mean"], writes=["var"])
                    T.op("dve", lambda e: e.scalar_tensor_tensor(out=var[:], in0=self.bank[s2b][:], scalar=1.0 / D, in1=var[:],
                                                                 op0=ALU.mult, op1=ALU.subtract),
                         reads=[("bank", s2b), "var"], writes=["var"])
                    T.op("act", lambda e: e.activation(out=var[:], in_=var[:], func=AF.Ln, bias=self.eps_ln[:, 0:1], scale=1.0),
                         reads=["var"], writes=["var"])
                    T.op("act", lambda e: e.activation(out=var[:], in_=var[:], func=AF.Exp, scale=-0.5),
                         reads=["var"], writes=["var"])
                    for c in range(NC8):
                        vs = vslot(t, c)
                        T.op("dve", lambda e, vs=vs: e.tensor_tensor(out=v32[:, vs, :], in0=v32[:, vs, :], in1=mean[:], op=ALU.subtract),
                             reads=[("v32", vs), "mean"], writes=[("v32", vs)])
                        T.op("dve", lambda e, vs=vs: e.tensor_tensor(out=v32[:, vs, :], in0=v32[:, vs, :], in1=var[:], op=ALU.mult),
                             reads=[("v32", vs), "var"], writes=[("v32", vs)])

                def ln_act(t, cs):
                    for c in cs:
                        vs = vslot(t, c)
                        T.op("act", lambda e, c=c, vs=vs: e.activation(out=z[:, c, :], in_=v32[:, vs, :], func=AF.Silu,
                                                                bias=VO("clnb", c), scale=VO("clng", c)),
                             reads=[("v32", vs), "vec"], writes=[("z", c)])

                def outproj(t):
                    ts = slice(t * TT, (t + 1) * TT)
                    wo = [self.wget(widx["wo"][t][0]), self.wget(widx["wo"][t][1])]
                    wov = [w[0][:, 0:4096].rearrange("p (k n) -> p k n", k=8) for w in wo]
                    for m in range(NC8):
                        ob = 4 + m % 2
                        T.mm_group([(self.bank[ob][:], wov[m // 4][:, c, (m % 4) * 128:(m % 4 + 1) * 128], z[:, c, :]) for c in range(NC8)],
                                   reads=[wo[0][1], wo[1][1]] + [("z", c) for c in range(NC8)], writes=[("bank", ob)])
                        T.op("dve", lambda e, ob=ob, m=m: e.scalar_tensor_tensor(
                            out=self.xs[:, m, ts], in0=self.bank[ob][:], scalar=VO("cbout", m), in1=self.xs[:, m, ts],
                            op0=ALU.add, op1=ALU.add),
                            reads=[("bank", ob), ("xs", m, t), "vec"], writes=[("xs", m, t)])
                    self.wretire(widx["wo"][t][0])
                    self.wretire(widx["wo"][t][1])

                for t in range(NT + 1):
                    if t < NT:
                        conv_chunk(t, 0)
                        conv_chunk(t, 1)
                    if t > 0:
                        ln_dve(t - 1)
                    if t < NT:
                        conv_chunk(t, 2)
                    if t > 0:
                        ln_act(t - 1, range(0, 4))
                    if t < NT:
                        conv_chunk(t, 3)
                    if t > 0:
                        ln_act(t - 1, range(4, 8))
                    if t < NT:
                        for c in range(4, NC8):
                            conv_chunk(t, c)
                    if t > 0:
                        outproj(t - 1)
                T.barrier()

    def plan_pool(self, l):
        a = len(self.wq)
        self.wplan(("pw",), [(0, 8, 256, self.d_pw.rearrange("g (kc p) n -> p (g kc) n", p=128))])
        return a

    def pool(self, l, wi, seg):
        nc, T = self.nc, self.T
        from contextlib import ExitStack
        self.uid = getattr(self, "uid", 0) + 1
        uid = self.uid
        WIN = (2, 4, 8, 16)
        with ExitStack() as ph:
            B = lambda name, shape, dtype: ph.enter_context(nc.sbuf_tensor(f"{name}_{uid}", shape, dtype))
            hp = B("hp", [128, NC8, 16 + TT], F32)
            sA = [B(f"sA{i}", [128, 16 + TT], F32) for i in range(2)]
            sB = [B(f"sB{i}", [128, 16 + TT], F32) for i in range(2)]
            pb = B("pb", [128, NC8, TT], BF16)
            pinv = B("pinv", [128, 4, 16], F32)
            T.dma("sp", pinv[:], self.d_pinv[:, :, :], writes=["pinv"])
            ws, wk = self.wget(wi)
            pw = ws[:, 0:2048].rearrange("p (k n) -> p k n", k=8)
            g0 = VOFF[f"nmix{l}"]
            tcount = [0]

            def emit(t, rs, rkey):
                ts = slice(t * TT, (t + 1) * TT)
                first_tile = (seg == 0 and t == 0)
                for c in range(NC8):
                    T.op("dve", lambda e, c=c: e.tensor_copy(out=hp[:, c, 0:16], in_=self.pcarry[:, c, :]),
                         reads=[("pcarry", c)], writes=[("hp", c)])
                    T.op("dve", lambda e, c=c: e.scalar_tensor_tensor(
                        out=hp[:, c, 16:16 + TT], in0=self.xs[:, c, ts], scalar=self.vec[:, g0 + c:g0 + c + 1],
                        in1=rs[:], op0=ALU.mult, op1=ALU.mult),
                        reads=[("xs", c, t), rkey, ("hp", c)], writes=[("hp", c)])
                    T.op("dve", lambda e, c=c: e.tensor_copy(out=self.pcarry[:, c, :], in_=hp[:, c, TT:TT + 16]),
                         reads=[("hp", c)], writes=[("pcarry", c)])
                for c in range(NC8):
                    gi = c // 2
                    w = WIN[gi]
                    a_, b_ = sA[c % 2], sB[c % 2]
                    ka, kb = ("sA", c % 2), ("sB", c % 2)
                    T.op("dve", lambda e, c=c, a_=a_: e.tensor_tensor(out=a_[:, 2:16 + TT], in0=hp[:, c, 2:16 + TT], in1=hp[:, c, 1:15 + TT], op=ALU.add),
                         reads=[("hp", c)], writes=[ka])
                    cur, ck, oth, ok = a_, ka, b_, kb
                    lvl = 2
                    while lvl < w:
                        lo = 2 * lvl
                        T.op("dve", lambda e, cur=cur, oth=oth, lo=lo, lvl=lvl: e.tensor_tensor(
                            out=oth[:, lo:16 + TT], in0=cur[:, lo:16 + TT], in1=cur[:, lo - lvl:16 + TT - lvl], op=ALU.add),
                            reads=[ck], writes=[ok])
                        cur, ck, oth, ok = oth, ok, cur, ck
                        lvl *= 2
                    T.op("dve", lambda e, c=c, cur=cur, w=w: e.scalar_tensor_tensor(
                        out=pb[:, c, :], in0=cur[:, 16:16 + TT], scalar=1.0 / w, in1=hp[:, c, 16:16 + TT],
                        op0=ALU.mult, op1=ALU.subtract),
                        reads=[ck, ("hp", c)], writes=[("pb", c)])
                    if first_tile:
                        T.op("dve", lambda e, cur=cur, gi=gi: e.tensor_tensor(out=cur[:, 16:32], in0=cur[:, 16:32], in1=pinv[:, gi, :], op=ALU.mult),
                             reads=[ck, "pinv", ("pb", c)], writes=[ck])
                        T.op("dve", lambda e, c=c, cur=cur: e.tensor_tensor(out=pb[:, c, 0:16], in0=cur[:, 16:32], in1=hp[:, c, 16:32], op=ALU.subtract),
                             reads=[ck, ("hp", c)], writes=[("pb", c)])
                for m in range(NC8):
                    gi = m // 2
                    ob = 4 + m % 2
                    T.mm_group([(self.bank[ob][:], pw[:, 2 * gi + kc, (m % 2) * 128:(m % 2 + 1) * 128], pb[:, 2 * gi + kc, :]) for kc in range(2)],
                               reads=[wk, ("pb", 2 * gi), ("pb", 2 * gi + 1)], writes=[("bank", ob)])
                    T.op("dve", lambda e, ob=ob, m=m: e.scalar_tensor_tensor(
                        out=self.xs[:, m, ts], in0=self.bank[ob][:], scalar=self.vec[:, VOFF["pscale"] + m:VOFF["pscale"] + m + 1],
                        in1=self.xs[:, m, ts], op0=ALU.mult, op1=ALU.add),
                        reads=[("bank", ob), ("xs", m, t), "vec"], writes=[("xs", m, t)])
            self.rmsnorm(f"nmix{l}", emit)
            self.wretire(wi)
            T.barrier()


    def plan_attn(self, l):
        a = len(self.wq)
        for pr in range(4):
            for g in range(3):
                self.wplan(("wa", pr, g), [(0, 8, 512, self.d_wattn[:, pr, g, 0:512].rearrange("(k p) n -> p k n", p=128))])
                self.wplan(("wv", pr, g), [(0, 8, 128, self.d_wattn[:, pr, g, 512:640].rearrange("(k p) n -> p k n", p=128))])
            self.wplan(("wo", pr), [(0, 1, D, self.d_wo[pr * 128:(pr + 1) * 128, :].rearrange("(k p) n -> p k n", p=128))])
        return a

    def attn(self, l, wi, seg):
        nc, T = self.nc, self.T
        from contextlib import ExitStack
        self.uid = getattr(self, "uid", 0) + 1
        uid = self.uid
        DIL = (1, 4, 16)
        with ExitStack() as ph:
            B = lambda name, shape, dtype: ph.enter_context(nc.sbuf_tensor(f"{name}_{uid}", shape, dtype))
            self.h = B("h", [128, NC8, TS], BF16)
            cs = [B(f"cs{i}", [128, TT], F32) for i in range(2)]
            sn = [B(f"sn{i}", [128, TT], F32) for i in range(2)]
            Qt = B("Qt", [128, TS], BF16)
            Kt = B("Kt", [128, TS], BF16)
            Va = B("Va", [128, 16, 256], BF16)
            kc = B("kc", [128, TS], BF16)
            vc = B("vc", [128, 16, 256], BF16)
            acc = B("acc", [128, 2, TS], F32)
            OT = B("OT", [128, TS], BF16)
            pt = [B(f"pt{i}", [128, 256], BF16) for i in range(2)]
            tm = [B(f"tm{i}", [128, TT], F32) for i in range(2)]
            rd = tm[0]
            self.norm_to_h(f"nmix{l}")
            T.op("pool", lambda e: e.memset(Va[:], 1.0), writes=["Va"])
            T.op("pool", lambda e: e.memset(vc[:], 1.0), writes=["vc"])
            tabn = [0]
            sbn = [0]
            obn = [0]
            ptn = [0]
            pw = wi
            for pr in range(4):
                T.op("pool", lambda e: e.memset(acc[:], 0.0), reads=[], writes=[("acc", 0), ("acc", 1)])
                for g in range(3):
                    d = DIL[g]
                    nblk = TS // (128 * d)
                    run = TS // d
                    wA, wAk = self.wget(pw)
                    wB, wBk = self.wget(pw + 1)
                    wAv = wA[:, 0:4096].rearrange("p (k n) -> p k n", k=8)
                    wBv = wB[:, 0:1024].rearrange("p (k n) -> p k n", k=8)
                    pairn = 0
                    for kind in range(2):
                        dst = Qt if kind == 0 else Kt
                        dkey = "Qt" if kind == 0 else "Kt"
                        off = kind * 256
                        for t in range(NT):
                            ts = slice(t * TT, (t + 1) * TT)
                            ab, bb = ((0, 1), (2, 3))[pairn % 2]
                            pairn += 1
                            hr = [("h", k, t) for k in range(NC8)]
                            T.mm_group([(self.bank[ab][:], wAv[:, k, off:off + 128], self.h[:, k, ts]) for k in range(NC8)],
                                       reads=[wAk] + hr, writes=[("bank", ab)])
                            T.mm_group([(self.bank[bb][:], wAv[:, k, off + 128:off + 256], self.h[:, k, ts]) for k in range(NC8)],
                                       reads=[wAk] + hr, writes=[("bank", bb)])
                            ti = tabn[0] % 2
                            tabn[0] += 1
                            p0 = seg * TS + t * TT
                            T.dma("sp", cs[ti][:], self.d_cos[:, p0:p0 + TT], writes=[("cs", ti)])
                            T.dma("sp", sn[ti][:], self.d_sin[:, p0:p0 + TT], writes=[("sn", ti)])
                            T.op("dve", lambda e, ab=ab, ti=ti: e.tensor_tensor(out=tm[0][:], in0=self.bank[ab][:], in1=cs[ti][:], op=ALU.mult),
                                 reads=[("bank", ab), ("cs", ti)], writes=[("tm", 0)])
                            T.op("dve", lambda e, bb=bb, ti=ti: e.tensor_tensor(out=tm[1][:], in0=self.bank[bb][:], in1=sn[ti][:], op=ALU.mult),
                                 reads=[("bank", bb), ("sn", ti)], writes=[("tm", 1)])
                            ov = dst[:, :].rearrange("p (r i) -> p r i", r=d)[:, :, t * TT // d:(t + 1) * TT // d]
                            i0 = tm[0][:, :].rearrange("p (i r) -> p r i", r=d)
                            i1 = tm[1][:, :].rearrange("p (i r) -> p r i", r=d)
                            T.op("dve", lambda e, ov=ov, i0=i0, i1=i1: e.tensor_tensor(out=ov, in0=i0, in1=i1, op=ALU.add),
                                 reads=[("tm", 0), ("tm", 1)], writes=[(dkey, t)])
                    qk_all = [("Qt", t) for t in range(NT)] + [("Kt", t) for t in range(NT)]
                    for b4 in range(4):
                        vb = 6 + b4 % 2
                        for bi in range(4):
                            bidx = b4 * 4 + bi
                            r, jb = bidx // nblk, bidx % nblk
                            st = jb * 128 * d + r
                            T.mm_group([(self.bank[vb][:, bi * 128:(bi + 1) * 128], self.h[:, k, st:st + 127 * d + 1:d], wBv[:, k, :]) for k in range(NC8)],
                                       reads=[wBk] + [("h", k, tt) for k in range(NC8) for tt in range(NT)],
                                       writes=[("bank", vb)] if bi == 0 else [])
                        T.res[("bank", vb)] = [(T.sem["pe"], T.cnt["pe"], "pe", "pe"), []]
                        src = self.bank[vb][:, :].rearrange("p (b f) -> p b f", b=4)
                        T.op("act", lambda e, src=src, b4=b4: e.activation(out=Va[:, b4 * 4:(b4 + 1) * 4, 0:64], in_=src[:, :, 0:64], func=AF.Identity),
                             reads=[("bank", vb), "Va"], writes=[("Va", b4, 0)])
                        T.op("act", lambda e, src=src, b4=b4: e.activation(out=Va[:, b4 * 4:(b4 + 1) * 4, 192:256], in_=src[:, :, 64:128], func=AF.Identity),
                             reads=[("bank", vb), "Va"], writes=[("Va", b4, 1)])
                    va_all = [("Va", b4, e_) for b4 in range(4) for e_ in range(2)]
                    self.wretire(pw)
                    self.wretire(pw + 1)
                    pw += 2
                    if seg == 0:
                        T.dma("sp", self.d_kctx[pr, g, :, 0:d * 128].rearrange("p (r i) -> p r i", r=d),
                              Kt[:, :].rearrange("p (r i) -> p r i", r=d)[:, :, (nblk - 1) * 128:nblk * 128],
                              reads=qk_all, writes=[("kctx", pr, g)])
                        T.dma("sp", self.d_vctx[pr, g, :, 0:d * 256].rearrange("p (r f) -> p r f", r=d),
                              Va[:, :, :].rearrange("p (r j) f -> p r j f", r=d)[:, :, nblk - 1, :],
                              reads=va_all, writes=[("vctx", pr, g)])
                    else:
                        T.dma("sp", kc[:, 0:d * 128], self.d_kctx[pr, g, :, 0:d * 128], reads=[("kctx", pr, g)], writes=["kc"])
                        T.dma("sp", vc[:, 0:d, :], self.d_vctx[pr, g, :, 0:d * 256].rearrange("p (r f) -> p r f", r=d),
                              reads=[("vctx", pr, g), "vc"], writes=["vc2"])
                    def emit_pv(job):
                        (e_, r, kb, qlo, nq, vlhs, kr, pi) = job
                        for qi in range(nq):
                            qb = qlo + qi
                            closing = (kb == qb)
                            opening = not closing
                            if opening:
                                ob = 2 + obn[0] % 4
                                obn[0] += 1
                                self._ob_cur = ob
                            else:
                                ob = self._ob_cur if (seg == 1 or qb > 0) else None
                                if ob is None:
                                    ob = 2 + obn[0] % 4
                                    obn[0] += 1
                            first = opening or (seg == 0 and qb == 0)
                            T._deps("pe", [("pt", pi)] + kr, [("bank", ob)] if first else [])
                            inst = nc.tensor.matmul(self.bank[ob][:, 0:128], lhsT=vlhs, rhs=pt[pi][:, qi * 128:(qi + 1) * 128],
                                                    start=first, stop=closing)
                            T.cnt["pe"] += 1
                            inst.then_inc(T.sem["pe"], 1)
                            hh = (T.sem["pe"], T.cnt["pe"], "pe", "pe")
                            T._commit(hh, [("pt", pi)], [("bank", ob)] if closing else [])
                            if closing:
                                av = acc[:, e_, :].rearrange("p (i r) -> p r i", r=d)[:, r, qb * 128:(qb + 1) * 128]
                                T.op("dve", lambda e, ob=ob, av=av: e.tensor_tensor(out=av, in0=self.bank[ob][:, 0:128], in1=av, op=ALU.add),
                                     reads=[("bank", ob), ("acc", e_)], writes=[("acc", e_)])

                    pend_pv = None
                    for e_ in range(2):
                        ps = slice(64 * e_, 64 * e_ + 64)
                        for r in range(d):
                            kbs = ([-1] if seg == 1 else []) + list(range(nblk))
                            for kb in kbs:
                                if kb < 0:
                                    klhs = kc[ps, r * 128:(r + 1) * 128]
                                    vlhs = vc[:, r, e_ * 128:(e_ + 1) * 128]
                                    kr = ["kc", "vc2"]
                                    qlo, nq, moff = 0, 1, 128
                                else:
                                    klhs = Kt[ps, r * run + kb * 128:r * run + (kb + 1) * 128]
                                    vlhs = Va[:, r * nblk + kb, e_ * 128:(e_ + 1) * 128]
                                    kr = qk_all + va_all
                                    qlo, nq, moff = kb, (2 if kb + 1 < nblk else 1), 0
                                sb = sbn[0] % 2
                                sbn[0] += 1
                                ncol = nq * 128
                                T.mm_group([(self.bank[sb][:, 0:ncol], klhs, Qt[ps, r * run + qlo * 128:r * run + qlo * 128 + ncol]),
                                            (self.bank[sb][:, 0:ncol], self.ident_bf, self.maskb[:, moff:moff + ncol])],
                                           reads=kr + ["cbf"], writes=[("bank", sb)])
                                pi = ptn[0] % 2
                                ptn[0] += 1
                                T.op("act", lambda e, sb=sb, pi=pi, ncol=ncol: e.activation(out=pt[pi][:, 0:ncol], in_=self.bank[sb][:, 0:ncol], func=AF.Exp, scale=0.125),
                                     reads=[("bank", sb)], writes=[("pt", pi)])
                                if pend_pv is not None:
                                    emit_pv(pend_pv)
                                pend_pv = (e_, r, kb, qlo, nq, vlhs, kr, pi)
                    emit_pv(pend_pv)
                wo, wok = self.wget(pw)
                wov = wo[:, 0:D]
                for t in range(NT):
                    ts = slice(t * TT, (t + 1) * TT)
                    for e_ in range(2):
                        ps = slice(64 * e_, 64 * e_ + 64)
                        fb = e_
                        rdb = tm[e_]
                        T.mm_group([(self.bank[fb][:], self.shift32, acc[:, e_, ts])], reads=[("acc", e_), "c32"], writes=[("bank", fb)])
                        T.op("act", lambda e, ps=ps, fb=fb, rdb=rdb: e.activation(out=rdb[ps, :], in_=self.bank[fb][ps, :], func=AF.Ln),
                             reads=[("bank", fb)], writes=[("tm", e_)])
                        T.op("act", lambda e, ps=ps, rdb=rdb: e.activation(out=rdb[ps, :], in_=rdb[ps, :], func=AF.Exp, scale=-1.0),
                             reads=[("tm", e_)], writes=[("tm", e_)])
                        T.op("dve", lambda e, ps=ps, e_=e_, rdb=rdb: e.tensor_tensor(out=OT[ps, ts], in0=acc[ps, e_, ts], in1=rdb[ps, :], op=ALU.mult),
                             reads=[("tm", e_), ("acc", e_)], writes=[("OT", t, e_)])
                    for m in range(NC8):
                        ob = 4 + m % 2
                        T.mm_group([(self.bank[ob][:], wov[:, m * 128:(m + 1) * 128], OT[:, ts])],
                                   reads=[wok, ("OT", t, 0), ("OT", t, 1)], writes=[("bank", ob)])
                        T.op("dve", lambda e, ob=ob, m=m: e.tensor_tensor(out=self.xs[:, m, ts], in0=self.bank[ob][:], in1=self.xs[:, m, ts], op=ALU.add),
                             reads=[("bank", ob), ("xs", m, t)], writes=[("xs", m, t)])
                self.wretire(pw)
                pw += 1
            T.barrier()

    def store_out(self, seg, final_norm):
        T = self.T
        if final_norm:
            g0 = VOFF["nfinal"]

            def emit(t, rs, rkey):
                ts = slice(t * TT, (t + 1) * TT)
                for c in range(NC8):
                    T.op("dve", lambda e, c=c: e.scalar_tensor_tensor(
                        out=self.xs[:, c, ts], in0=self.xs[:, c, ts], scalar=self.vec[:, g0 + c:g0 + c + 1],
                        in1=rs[:], op0=ALU.mult, op1=ALU.mult),
                        reads=[("xs", c, t), rkey], writes=[("xs", c, t)])
            self.rmsnorm("nfinal", emit)
        for c in range(NC8):
            T.dma("sp", self.d_out[c * 128:(c + 1) * 128, seg * TS:(seg + 1) * TS], self.xs[:, c, :],
                  reads=[("xs", c, t) for t in range(NT)], writes=[("out", c, seg)])

    def build(self):
        nc, T = self.nc, self.T
        from contextlib import ExitStack
        with ExitStack() as es:
            A = lambda name, shape, dtype: es.enter_context(nc.sbuf_tensor(name, shape, dtype))
            self.xs = A("xs", [128, NC8, TS], F32)
            self.vec = A("vec_sb", [128, NV], F32)
            self.cbf = A("cbf", [128, 512], BF16)
            self.c32 = A("c32", [128, 256], F32)
            self.eps_rms = A("eps_rms", [128, 1], F32)
            self.eps_ln = A("eps_ln", [128, 1], F32)
            self.ring = [A(f"ring{i}", [128, SLOT], BF16) for i in range(NSLOT)]
            self.sq = [A(f"sq{i}", [128, TT], BF16) for i in range(2)]
            self.rstd = [A(f"rstd{i}", [128, TT], F32) for i in range(2)]
            self.ccarry = A("ccarry", [128, 2, NC8, 32], BF16)
            self.pcarry = A("pcarry", [128, NC8, 16], F32)
            self.bank = [es.enter_context(nc.psum_tensor(f"bank{i}", [128, TT], F32)) for i in range(8)]
            self.ident_bf = self.cbf[:, 0:128]
            self.ones_bf = self.cbf[:, 128:256]
            self.maskb = self.cbf[:, 256:512]
            self.ones32 = self.c32[:, 0:128]
            self.shift32 = self.c32[:, 128:256]

            T.dma("sp", self.vec[:], self.d_vec[:, :], writes=["vec"])
            T.dma("sp", self.cbf[:], self.d_cbf[:, :], writes=["cbf"])
            T.dma("sp", self.c32[:], self.d_c32[:, :], writes=["c32"])
            T.op("dve", lambda e: e.memset(self.eps_rms[:], RMS_EPS), writes=["eps"])
            T.op("dve", lambda e: e.memset(self.eps_ln[:], LN_EPS), writes=["eps2"])
            T.op("dve", lambda e: e.memset(self.ccarry[:], 0.0), writes=["cc0"])
            T.op("dve", lambda e: e.memset(self.pcarry[:], 0.0), writes=["pc0"])
            if any(k == "conv" for (k, _) in self.plan):
                with ExitStack() as st:
                    dgs = [st.enter_context(nc.sbuf_tensor(f"dgst{i}", [128, CW, 128], BF16)) for i in range(2)]
                    wdw = st.enter_context(nc.sbuf_tensor("wdw_all", [128, 2, NC8, CW], F32))
                    T.dma("sp", wdw[:], self.d_cwdw[:, :, :, :], writes=["wdw"])
                    n = 0
                    for j in range(2):
                        for c in range(NC8):
                            dd = dgs[n % 2]
                            for k in range(CW):
                                eng = "dve" if k < 22 else "act"
                                if eng == "dve":
                                    T.op("dve", lambda e, dd=dd, k=k, j=j, c=c: e.tensor_scalar(
                                        out=dd[:, k, :], in0=self.ident_bf, scalar1=wdw[:, j, c, k:k + 1], scalar2=None, op0=ALU.mult),
                                        reads=["wdw", "cbf"], writes=[("dgs", n % 2, k)])
                                else:
                                    T.op("act", lambda e, dd=dd, k=k, j=j, c=c: e.activation(
                                        out=dd[:, k, :], in_=self.ident_bf, func=AF.Copy, scale=wdw[:, j, c, k:k + 1]),
                                        reads=["wdw", "cbf"], writes=[("dgs", n % 2, k)])
                            T.dma("sp", self.d_dgd[j, c, :, :].rearrange("p (k n) -> p k n", k=CW), dd[:],
                                  reads=[("dgs", n % 2, k) for k in range(CW)], writes=[("dgd", j, c)])
                            n += 1
                    T.barrier()
            T.barrier()

            sched = []
            for seg in range(NSEG):
                for (kind, l) in self.plan:
                    if kind == "ffn":
                        sched.append((seg, kind, l, self.plan_ffn(l)))
                    elif kind == "conv":
                        sched.append((seg, kind, l, self.plan_conv(l)))
                    elif kind == "pool":
                        sched.append((seg, kind, l, self.plan_pool(l)))
                    else:
                        sched.append((seg, kind, l, self.plan_attn(l)))

            cur_seg = -1
            for (seg, kind, l, widx) in sched:
                if seg != cur_seg:
                    if cur_seg >= 0:
                        self.store_out(cur_seg, self.final_norm)
                    cur_seg = seg
                    for c in range(NC8):
                        T.dma("sp", self.xs[:, c, :], self.d_xT[c * 128:(c + 1) * 128, seg * TS:(seg + 1) * TS],
                              writes=[("xs", c, t) for t in range(NT)])
                if kind == "ffn":
                    with ExitStack() as ph:
                        self.uid = getattr(self, "uid", 0) + 1
                        B = lambda name, shape, dtype: ph.enter_context(nc.sbuf_tensor(f"{name}_{self.uid}", shape, dtype))
                        self.h = B("h", [128, NC8, TS], BF16)
                        self.abuf = [B(f"abuf{i}", [128, 4, TT], BF16) for i in range(2)]
                        self.sg = [B(f"sg{i}", [128, TT], F32) for i in range(2)]
                        self.ffn(l, widx)
                        T.barrier()
                elif kind == "conv":
                    self.conv(l, widx, seg)
                elif kind == "pool":
                    self.pool(l, widx, seg)
                elif kind == "attn":
                    self.attn(l, widx, seg)
            self.store_out(cur_seg, self.final_norm)
            T.barrier()
        return nc


def _chunked(v):
    v = np.asarray(v, np.float32)
    return np.ascontiguousarray(v.reshape(-1, 128).T)


def prep_shared(inp):
    f32 = np.float32
    vec = np.zeros((128, NV), f32)
    for l in range(DEPTH):
        vec[:, VOFF[f"nmix{l}"]:VOFF[f"nmix{l}"] + 8] = _chunked(inp["norm_mix_g"][l])
        vec[:, VOFF[f"nffn{l}"]:VOFF[f"nffn{l}"] + 8] = _chunked(inp["norm_ffn_g"][l])
    vec[:, VOFF["nfinal"]:VOFF["nfinal"] + 8] = _chunked(inp["final_norm_g"])
    for j in range(2):
        vec[:, VOFF[f"cba{j}"]:VOFF[f"cba{j}"] + 8] = _chunked(inp["conv_b_in"][j][:D])
        vec[:, VOFF[f"cbg{j}"]:VOFF[f"cbg{j}"] + 8] = _chunked(inp["conv_b_in"][j][D:])
        vec[:, VOFF[f"cbdw{j}"]:VOFF[f"cbdw{j}"] + 8] = _chunked(inp["conv_b_dw"][j])
        vec[:, VOFF[f"clng{j}"]:VOFF[f"clng{j}"] + 8] = _chunked(inp["conv_ln_g"][j])
        vec[:, VOFF[f"clnb{j}"]:VOFF[f"clnb{j}"] + 8] = _chunked(inp["conv_ln_b"][j])
        vec[:, VOFF[f"cbout{j}"]:VOFF[f"cbout{j}"] + 8] = _chunked(inp["conv_b_out"][j])
    vec[:, VOFF["pscale"]:VOFF["pscale"] + 8] = _chunked(inp["pool_scale"][0])

    cw = np.asarray(inp["conv_w_in"], f32)
    cwin_r = np.ascontiguousarray(np.stack([cw[:, :, :D].reshape(2, D, 8, 128),
                                            cw[:, :, D:].reshape(2, D, 8, 128)], axis=3).reshape(2, D, 8, 256))
    dw = np.asarray(inp["conv_w_dw"], f32)
    cwdw_r = np.ascontiguousarray(dw.reshape(2, CW, 8, 128).transpose(3, 0, 2, 1))

    pinv = np.zeros((128, 4, 16), f32)
    for g, w in enumerate((2, 4, 8, 16)):
        pinv[:, g, :] = 1.0 / np.minimum(np.arange(1, 17), w).astype(f32)

    wqkv = np.asarray(inp["attn_w_qkv"], f32)[0]
    wq = wqkv[:, 0:1536].reshape(D, 3, 8, 64)
    wk = wqkv[:, 1536:3072].reshape(D, 3, 8, 64)
    wv = wqkv[:, 3072:4608].reshape(D, 3, 8, 64)
    swap = np.concatenate([np.arange(32, 64), np.arange(0, 32)])
    w_r = np.zeros((D, 4, 3, 5, 2, 64), f32)
    for pr in range(4):
        for g in range(3):
            for e in range(2):
                hs = 2 * pr + e
                w_r[:, pr, g, 0, e] = wq[:, g, hs]
                w_r[:, pr, g, 1, e] = wq[:, g, hs][:, swap]
                w_r[:, pr, g, 2, e] = wk[:, g, hs]
                w_r[:, pr, g, 3, e] = wk[:, g, hs][:, swap]
                w_r[:, pr, g, 4, e] = wv[:, g, hs]
    w_r = np.ascontiguousarray(w_r.reshape(D, 4, 3, 640))

    half = 32
    inv_freq = (10000.0 ** (-np.arange(half, dtype=f32) / half)).astype(f32)
    ang = np.arange(S, dtype=f32)[None, :] * inv_freq[:, None]
    cos = np.cos(ang).astype(f32)
    sin = np.sin(ang).astype(f32)
    cos_t = np.concatenate([cos, cos, cos, cos], axis=0)
    sin_t = np.concatenate([-sin, sin, -sin, sin], axis=0)

    cbf = np.zeros((128, 512), f32)
    cbf[:, 0:128] = np.eye(128, dtype=f32)
    cbf[:, 128:256] = 1.0
    a = np.arange(128)[:, None]
    b = np.arange(256)[None, :]
    cbf[:, 256:512] = np.where((b >= a) & (b <= a + 128), 0.0, -30000.0)
    c32 = np.zeros((128, 256), f32)
    c32[:, 0:128] = 1.0
    c32[:, 128:256] = np.roll(np.eye(128, dtype=f32), 64, axis=0)

    return {
        "vec": vec,
        "ffn_w_gate": np.ascontiguousarray(inp["ffn_w_gate"], f32),
        "ffn_w_up": np.ascontiguousarray(inp["ffn_w_up"], f32),
        "ffn_w_down": np.ascontiguousarray(inp["ffn_w_down"], f32),
        "conv_w_in_r": cwin_r,
        "conv_w_out": np.ascontiguousarray(inp["conv_w_out"], f32),
        "conv_w_dw_r": cwdw_r,
        "pool_w": np.ascontiguousarray(inp["pool_w"], f32)[0],
        "pool_inv": pinv,
        "attn_w_r": w_r,
        "attn_w_o": np.ascontiguousarray(inp["attn_w_o"], f32)[0],
        "rope_cos": np.ascontiguousarray(cos_t),
        "rope_sin": np.ascontiguousarray(sin_t),
        "const_bf": cbf.astype(ml_dtypes.bfloat16),
        "const_f32": c32,
    }


FULL_PLAN = [("conv", 0), ("ffn", 0), ("attn", 1), ("ffn", 1), ("pool", 2), ("ffn", 2), ("conv", 3), ("ffn", 3)]


def run(inputs, plan=FULL_PLAN, final_norm=True, cores=8):
    inp = {k: np.asarray(v) for k, v in inputs.items()}
    shared = prep_shared(inp)
    x = np.asarray(inp["x"], np.float32)
    in_maps = []
    for b in range(cores):
        m = dict(shared)
        m["xT"] = np.ascontiguousarray(x[b].T)
        in_maps.append(m)
    nc = Prog(plan, final_norm).build()
    res = run_bass_kernel_spmd(nc, in_maps, core_ids=list(range(cores)))
    out = np.stack([np.ascontiguousarray(r["yT"].T) for r in res.results], axis=0)
    return out.astype(np.float32)


def kernel(**inputs):
    return run(inputs)
```

```python
import numpy as np
import ml_dtypes
import concourse.bass as bass
import concourse.mybir as mybir
from concourse.bass_utils import run_bass_kernel_spmd

F32 = mybir.dt.float32
BF16 = mybir.dt.bfloat16
AF = mybir.ActivationFunctionType
ALU = mybir.AluOpType

D = 1024
S = 4096
NC8 = 8
TS = 2048
NSEG = S // TS
TT = 512
NT = TS // TT
FH = 2816
NHC = FH // 128
DEPTH = 4
CW = 31
RMS_EPS = 1e-6
LN_EPS = 1e-5
SLOT = 4096
NSLOT = 5

VEC_NAMES = []
for _l in range(DEPTH):
    VEC_NAMES += [f"nmix{_l}", f"nffn{_l}"]
VEC_NAMES += ["nfinal"]
for _j in range(2):
    VEC_NAMES += [f"cba{_j}", f"cbg{_j}", f"cbdw{_j}", f"clng{_j}", f"clnb{_j}", f"cbout{_j}"]
VEC_NAMES += ["pscale"]
VOFF = {n: 8 * i for i, n in enumerate(VEC_NAMES)}
NV = 8 * len(VEC_NAMES)


class Tracker:
    def __init__(self, nc):
        self.nc = nc
        self.engs = {"pe": nc.tensor, "act": nc.scalar, "dve": nc.vector,
                     "pool": nc.gpsimd, "sp": nc.sync}
        self.sem = {k: nc.alloc_semaphore(f"prog_{k}") for k in ("pe", "act", "dve", "pool")}
        self.cnt = {k: 0 for k in self.sem}
        self.waited = {k: {} for k in self.engs}
        self.dpool = {}
        for q in ("sp", "pool"):
            self.dpool[q] = [[nc.alloc_semaphore(f"dma_{q}_{i}"), 0] for i in range(12)]
        self.dnext = {"sp": 0, "pool": 0}
        self.res = {}
        self.nwaits = 0

    def _wait(self, eng, h):
        if h is None:
            return
        sem, val, src, key = h
        if eng == "pe" and src == "pe":
            return
        if self.waited[eng].get(key, 0) >= val:
            return
        self.engs[eng].wait_ge(sem, val)
        self.waited[eng][key] = val
        self.nwaits += 1

    def _deps(self, eng, reads, writes):
        for k in reads:
            r = self.res.get(k)
            if r is not None:
                self._wait(eng, r[0])
        for k in writes:
            r = self.res.get(k)
            if r is not None:
                self._wait(eng, r[0])
                for h in r[1]:
                    self._wait(eng, h)

    def _commit(self, h, reads, writes):
        for k in reads:
            r = self.res.setdefault(k, [None, []])
            if h[2] in ("pe", "act", "dve", "pool") and h[3] == h[2]:
                r[1] = [x for x in r[1] if x[3] != h[3]]
            r[1].append(h)
        for k in writes:
            self.res[k] = [h, []]

    def op(self, eng, fn, reads=(), writes=()):
        self._deps(eng, reads, writes)
        inst = fn(self.engs[eng])
        self.cnt[eng] += 1
        inst.then_inc(self.sem[eng], 1)
        h = (self.sem[eng], self.cnt[eng], eng, eng)
        self._commit(h, reads, writes)
        return h

    def mm_group(self, mms, reads=(), writes=()):
        self._deps("pe", reads, writes)
        n = len(mms)
        inst = None
        for i, (o, l, r) in enumerate(mms):
            inst = self.nc.tensor.matmul(o, lhsT=l, rhs=r, start=(i == 0), stop=(i == n - 1))
        self.cnt["pe"] += 1
        inst.then_inc(self.sem["pe"], 1)
        h = (self.sem["pe"], self.cnt["pe"], "pe", "pe")
        self._commit(h, reads, writes)
        return h

    def dma(self, q, out, in_, reads=(), writes=()):
        pool = self.dpool[q]
        i = self.dnext[q]
        self.dnext[q] = (i + 1) % len(pool)
        sem, c = pool[i]
        key = f"d{q}{i}"
        self._wait(q, (sem, c, q, key))
        self._deps(q, reads, writes)
        self.engs[q].dma_start(out=out, in_=in_).then_inc(sem, 16)
        pool[i][1] = c + 16
        h = (sem, c + 16, q, key)
        self._commit(h, reads, writes)
        return h

    def barrier(self):
        hs = [(self.sem[k], self.cnt[k], k, k) for k in self.sem if self.cnt[k] > 0]
        for q in ("sp", "pool"):
            for i, (sem, c) in enumerate(self.dpool[q]):
                if c > 0:
                    hs.append((sem, c, q, f"d{q}{i}"))
        for e in ("pe", "act", "dve", "pool", "sp"):
            for h in hs:
                if e == "pe" and h[2] == "pe":
                    continue
                self._wait(e, h)
        self.res = {}


class Prog:
    def __init__(self, plan, final_norm=True):
        self.plan = plan
        self.final_norm = final_norm
        nc = bass.Bass("TRN2", target_bir_lowering=False)
        self.nc = nc
        self.T = Tracker(nc)
        dt = nc.dram_tensor
        self.d_xT = dt("xT", [D, S], F32, kind="ExternalInput").ap()
        self.d_vec = dt("vec", [128, NV], F32, kind="ExternalInput").ap()
        self.d_wg = dt("ffn_w_gate", [DEPTH, D, FH], F32, kind="ExternalInput").ap()
        self.d_wu = dt("ffn_w_up", [DEPTH, D, FH], F32, kind="ExternalInput").ap()
        self.d_wd = dt("ffn_w_down", [DEPTH, FH, D], F32, kind="ExternalInput").ap()
        self.d_cwin = dt("conv_w_in_r", [2, D, 8, 256], F32, kind="ExternalInput").ap()
        self.d_cwout = dt("conv_w_out", [2, D, D], F32, kind="ExternalInput").ap()
        self.d_cwdw = dt("conv_w_dw_r", [128, 2, 8, CW], F32, kind="ExternalInput").ap()
        self.d_pw = dt("pool_w", [4, 256, 256], F32, kind="ExternalInput").ap()
        self.d_pinv = dt("pool_inv", [128, 4, 16], F32, kind="ExternalInput").ap()
        self.d_wattn = dt("attn_w_r", [D, 4, 3, 384], F32, kind="ExternalInput").ap()
        self.d_wo = dt("attn_w_o", [512, D], F32, kind="ExternalInput").ap()
        self.d_cos = dt("rope_cos", [128, S], F32, kind="ExternalInput").ap()
        self.d_sin = dt("rope_sin", [128, S], F32, kind="ExternalInput").ap()
        self.d_cbf = dt("const_bf", [128, 640], BF16, kind="ExternalInput").ap()
        self.d_c32 = dt("const_f32", [128, 128 + 128], F32, kind="ExternalInput").ap()
        self.d_kctx = dt("kctx", [4, 3, 128, TS], BF16, kind="Internal").ap()
        self.d_vctx = dt("vctx", [4, 3, 128, 16 * 192], BF16, kind="Internal").ap()
        self.d_dgd = dt("dgd", [2, NC8, 128, CW * 128], BF16, kind="Internal").ap()
        self.d_out = dt("yT", [D, S], F32, kind="ExternalOutput").ap()
        self.wq = []
        self.wissued = 0
        self.wretired = set()
        self.wloaded = {}

    def wplan(self, key, src_aps):
        self.wq.append((key, src_aps))

    def wpump(self):
        T = self.T
        while self.wissued < len(self.wq) and (self.wissued < NSLOT or (self.wissued - NSLOT) in self.wretired):
            i = self.wissued
            slot = i % NSLOT
            key, srcs = self.wq[i]
            assert len(srcs) == 1
            (off, k, n, ap) = srcs[0]
            dst = self.ring[slot][:, off:off + k * n].rearrange("p (k n) -> p k n", k=k)
            T.dma("pool", dst, ap, reads=(), writes=[("ring", slot)])
            self.wissued += 1

    def wretire(self, idx):
        self.wretired.add(idx)
        self.wpump()

    def wget(self, idx):
        self.wpump()
        assert idx < self.wissued, f"weight ring deadlock at piece {idx}"
        return self.ring[idx % NSLOT], ("ring", idx % NSLOT)

    def rmsnorm(self, gname, emit):
        for t in range(NT):
            self.rmsnorm_tile(t, emit)

    def rmsnorm_tile(self, t, emit):
        nc, T = self.nc, self.T
        ts = slice(t * TT, (t + 1) * TT)
        ssb = self.bank[7]
        for c in range(NC8):
            sq = self.sq[c % 2]
            T.op("act", lambda e, c=c, sq=sq: e.activation(out=sq[:], in_=self.xs[:, c, ts], func=AF.Square),
                 reads=[("xs", c, t)], writes=[("sq", c % 2)])
            self.mm_one(ssb[:], self.ones_bf[:], sq[:], c == 0, c == NC8 - 1, [("sq", c % 2)], ("bank", 7))
        rs = self.rstd[t % 2]
        T.op("act", lambda e, rs=rs: e.activation(out=rs[:], in_=ssb[:], func=AF.Ln, bias=self.eps_rms[:, 0:1], scale=1.0 / D),
             reads=[("bank", 7)], writes=[("rstd", t % 2)])
        T.op("act", lambda e, rs=rs: e.activation(out=rs[:], in_=rs[:], func=AF.Exp, scale=-0.5),
             reads=[("rstd", t % 2)], writes=[("rstd", t % 2)])
        emit(t, rs, ("rstd", t % 2))

    def h_emit(self, gname):
        T = self.T
        g0 = VOFF[gname]

        def emit(t, rs, rkey):
            ts = slice(t * TT, (t + 1) * TT)
            for c in range(NC8):
                T.op("dve", lambda e, c=c: e.scalar_tensor_tensor(
                    out=self.h[:, c, ts], in0=self.xs[:, c, ts], scalar=self.vec[:, g0 + c:g0 + c + 1],
                    in1=rs[:], op0=ALU.mult, op1=ALU.mult),
                    reads=[("xs", c, t), rkey], writes=[("h", c, t)])
        return emit

    def norm_to_h(self, gname):
        self.rmsnorm(gname, self.h_emit(gname))

    def plan_ffn(self, l):
        groups = [(0, 4), (4, 4), (8, 4), (12, 4), (16, 4), (20, 2)]
        idx = []
        for (j0, nj) in groups:
            a = len(self.wq)
            self.wplan(("wg", l, j0), [(0, 8, nj * 128, self.d_wg[l, :, j0 * 128:(j0 + nj) * 128].rearrange("(k p) n -> p k n", p=128))])
            self.wplan(("wu", l, j0), [(0, 8, nj * 128, self.d_wu[l, :, j0 * 128:(j0 + nj) * 128].rearrange("(k p) n -> p k n", p=128))])
            self.wplan(("wd", l, j0), [(0, nj, D, self.d_wd[l, j0 * 128:(j0 + nj) * 128, :].rearrange("(k p) n -> p k n", p=128))])
            idx.append((j0, nj, a))
        return idx

    def ffn(self, l, widx):
        nc, T = self.nc, self.T
        hemit = self.h_emit(f"nffn{l}")
        self.rmsnorm_tile(0, hemit)
        self.rmsnorm_tile(1, hemit)
        first_group = True
        pend = None
        step = 0
        gu_banks = [(0, 1), (2, 3)]
        d_banks = [4, 5]
        dcount = [0]

        def down(st):
            (t, nj, wdk, wdslot, ab, wdi) = st
            ts = slice(t * TT, (t + 1) * TT)
            wdv = wdslot[:, 0:nj * D].rearrange("p (k n) -> p k n", k=nj)
            for m in range(NC8):
                b = d_banks[dcount[0] % 2]
                dcount[0] += 1
                T.mm_group([(self.bank[b][:], wdv[:, j, m * 128:(m + 1) * 128], self.abuf[ab][:, j, :]) for j in range(nj)],
                           reads=[wdk] + [("a", ab, j) for j in range(nj)], writes=[("bank", b)])
                T.op("dve", lambda e, b=b, m=m: e.tensor_tensor(out=self.xs[:, m, ts], in0=self.bank[b][:], in1=self.xs[:, m, ts], op=ALU.add),
                     reads=[("bank", b), ("xs", m, t)], writes=[("xs", m, t)])
            if t == NT - 1:
                self.wretire(wdi)

        for (j0, nj, wi) in widx:
            wgs, wgk = self.wget(wi)
            wus, wuk = self.wget(wi + 1)
            wds, wdk = self.wget(wi + 2)
            wgv = wgs[:, 0:8 * nj * 128].rearrange("p (k n) -> p k n", k=8)
            wuv = wus[:, 0:8 * nj * 128].rearrange("p (k n) -> p k n", k=8)
            for t in range(NT):
                ts = slice(t * TT, (t + 1) * TT)
                ab = step % 2
                for j in range(nj):
                    gb, ub = gu_banks[j % 2]
                    hr = [("h", k, t) for k in range(NC8)]
                    T.mm_group([(self.bank[gb][:], wgv[:, k, j * 128:(j + 1) * 128], self.h[:, k, ts]) for k in range(NC8)],
                               reads=[wgk] + hr, writes=[("bank", gb)])
                    T.mm_group([(self.bank[ub][:], wuv[:, k, j * 128:(j + 1) * 128], self.h[:, k, ts]) for k in range(NC8)],
                               reads=[wuk] + hr, writes=[("bank", ub)])
                    sg = self.sg[j % 2]
                    T.op("act", lambda e, gb=gb, sg=sg: e.activation(out=sg[:], in_=self.bank[gb][:], func=AF.Silu),
                         reads=[("bank", gb)], writes=[("sg", j % 2)])
                    T.op("dve", lambda e, ub=ub, sg=sg, j=j, ab=ab: e.tensor_tensor(out=self.abuf[ab][:, j, :], in0=self.bank[ub][:], in1=sg[:], op=ALU.mult),
                         reads=[("bank", ub), ("sg", j % 2)], writes=[("a", ab, j)])
                if first_group and t + 2 < NT:
                    self.rmsnorm_tile(t + 2, hemit)
                if pend is not None:
                    down(pend)
                pend = (t, nj, wdk, wds, ab, wi + 2)
                step += 1
            first_group = False
            self.wretire(wi)
            self.wretire(wi + 1)
        down(pend)


    def mm_one(self, out, lhsT, rhs, first, last, reads, wkey):
        T = self.T
        T._deps("pe", reads, [wkey] if first else [])
        inst = self.nc.tensor.matmul(out, lhsT=lhsT, rhs=rhs, start=first, stop=last)
        T.cnt["pe"] += 1
        inst.then_inc(T.sem["pe"], 1)
        h = (T.sem["pe"], T.cnt["pe"], "pe", "pe")
        T._commit(h, reads, [wkey] if last else [])
        return h

    def plan_conv(self, l):
        j = l // 3
        a = len(self.wq)
        for c in range(NC8):
            self.wplan(("cwin", j, c), [(0, 8, 256, self.d_cwin[j, :, c, :].rearrange("(k p) n -> p k n", p=128))])
        idx = {"win": a, "d": {}, "wo": {}}

        def add_d(t):
            for c in range(NC8):
                idx["d"][(t, c)] = len(self.wq)
                self.wplan(("dg", j, c), [(0, CW, 128, self.d_dgd[j, c, :, :].rearrange("p (k n) -> p k n", k=CW))])

        def add_wo(t):
            i0 = len(self.wq)
            for hf in range(2):
                self.wplan(("cwout", j, hf), [(0, 8, 512, self.d_cwout[j, :, hf * 512:(hf + 1) * 512].rearrange("(k p) n -> p k n", p=128))])
            idx["wo"][t] = (i0, i0 + 1)
        for t in range(NT + 1):
            if t < NT:
                add_d(t)
            if t > 0:
                add_wo(t - 1)
        return idx

    def conv(self, l, widx, seg):
        nc, T = self.nc, self.T
        j = l // 3
        wi = widx["win"]
        from contextlib import ExitStack
        self.uid = getattr(self, "uid", 0) + 1
        uid = self.uid
        with ExitStack() as outer:
            O = lambda name, shape, dtype: outer.enter_context(nc.sbuf_tensor(f"{name}_{uid}", shape, dtype))
            u = O("u", [128, NC8, 32 + TS], BF16)
            with ExitStack() as ph:
                B = lambda name, shape, dtype: ph.enter_context(nc.sbuf_tensor(f"{name}_{uid}", shape, dtype))
                self.h = B("h", [128, NC8, TS], BF16)
                sig = [B(f"sig{i}", [128, TT], F32) for i in range(2)]
                hemit = self.h_emit(f"nmix{l}")
                self.rmsnorm_tile(0, hemit)
                self.rmsnorm_tile(1, hemit)
                for c in range(NC8):
                    T.op("dve", lambda e, c=c: e.tensor_copy(out=u[:, c, 0:32], in_=self.ccarry[:, j, c, :]),
                         reads=[("ccarry", j, c)], writes=[("u", c, -1)])
                pair = 0
                for c in range(NC8):
                    ws, wk = self.wget(wi + c)
                    wv = ws[:, 0:2048].rearrange("p (k n) -> p k n", k=8)
                    for t in range(NT):
                        ts = slice(t * TT, (t + 1) * TT)
                        ab, gb = ((0, 1), (2, 3))[pair % 2]
                        pair += 1
                        hr = [("h", k, t) for k in range(NC8)]
                        T.mm_group([(self.bank[ab][:], wv[:, k, 0:128], self.h[:, k, ts]) for k in range(NC8)],
                                   reads=[wk] + hr, writes=[("bank", ab)])
                        T.mm_group([(self.bank[gb][:], wv[:, k, 128:256], self.h[:, k, ts]) for k in range(NC8)],
                                   reads=[wk] + hr, writes=[("bank", gb)])
                        sg = sig[pair % 2]
                        T.op("act", lambda e, gb=gb, sg=sg, c=c: e.activation(
                            out=sg[:], in_=self.bank[gb][:], func=AF.Sigmoid,
                            bias=self.vec[:, VOFF[f"cbg{j}"] + c:VOFF[f"cbg{j}"] + c + 1], scale=1.0),
                            reads=[("bank", gb), "vec"], writes=[("sig", pair % 2)])
                        T.op("dve", lambda e, ab=ab, sg=sg, c=c, t=t: e.scalar_tensor_tensor(
                            out=u[:, c, 32 + t * TT:32 + (t + 1) * TT], in0=self.bank[ab][:],
                            scalar=self.vec[:, VOFF[f"cba{j}"] + c:VOFF[f"cba{j}"] + c + 1], in1=sg[:],
                            op0=ALU.add, op1=ALU.mult),
                            reads=[("bank", ab), ("sig", pair % 2)], writes=[("u", c, t)])
                        if c == 0 and t + 2 < NT:
                            self.rmsnorm_tile(t + 2, hemit)
                    self.wretire(wi + c)
                for c in range(NC8):
                    T.op("dve", lambda e, c=c: e.tensor_copy(out=self.ccarry[:, j, c, :], in_=u[:, c, TS:TS + 32]),
                         reads=[("u", c, NT - 1)], writes=[("ccarry", j, c)])
                T.barrier()
            with ExitStack() as ph:
                B = lambda name, shape, dtype: ph.enter_context(nc.sbuf_tensor(f"{name}_{uid}", shape, dtype))
                v32 = B("v32", [128, 12, TT], F32)
                vslot = lambda t, c: ((t % 2) * 4 + c) if c < 4 else (8 + c - 4)
                vsq = self.sq
                z = B("z", [128, NC8, TT], BF16)
                mean = self.rstd[0]
                var = self.rstd[1]
                VO = lambda n, c: self.vec[:, VOFF[f"{n}{j}"] + c:VOFF[f"{n}{j}"] + c + 1]
                itc = [0]

                def conv_chunk(t, c):
                    it = itc[0]
                    itc[0] += 1
                    vs = vslot(t, c)
                    di = widx["d"][(t, c)]
                    dslot, dkey = self.wget(di)
                    d = dslot[:, 0:CW * 128].rearrange("p (k n) -> p k n", k=CW)
                    cb = it % 2
                    base = 32 + t * TT - (CW - 1)
                    T.mm_group([(self.bank[cb][:], d[:, k, :], u[:, c, base + k:base + k + TT]) for k in range(CW)],
                               reads=[dkey, ("u", c, t), ("u", c, t - 1)], writes=[("bank", cb)])
                    self.wretire(di)
                    T.op("act", lambda e, cb=cb, c=c, vs=vs: e.activation(out=v32[:, vs, :], in_=self.bank[cb][:], func=AF.Identity,
                                                                   bias=VO("cbdw", c), scale=1.0),
                         reads=[("bank", cb), "vec"], writes=[("v32", vs)])
                    T.op("act", lambda e, cb=cb, c=c: e.activation(out=vsq[it % 2][:], in_=self.bank[cb][:], func=AF.Square,
                                                                   bias=VO("cbdw", c), scale=1.0),
                         reads=[("bank", cb), "vec"], writes=[("vsq", it % 2)])
                    s1b, s2b = (2, 3) if t % 2 == 0 else (6, 7)
                    self.mm_one(self.bank[s1b][:], self.ones32, v32[:, vs, :], c == 0, c == NC8 - 1, [("v32", vs), "c32"], ("bank", s1b))
                    self.mm_one(self.bank[s2b][:], self.ones_bf, vsq[it % 2][:], c == 0, c == NC8 - 1, [("vsq", it % 2), "cbf"], ("bank", s2b))

                def ln_dve(t):
                    s1b, s2b = (2, 3) if t % 2 == 0 else (6, 7)
                    T.op("act", lambda e: e.activation(out=mean[:], in_=self.bank[s1b][:], func=AF.Identity, scale=1.0 / D),
                         reads=[("bank", s1b)], writes=["mean"])
                    T.op("dve", lambda e: e.tensor_tensor(out=var[:], in0=mean[:], in1=mean[:], op=ALU.mult),
                         reads=["mean"], writes=["var"])
                    T.op("dve", lambda e: e.scalar_tensor_tensor(out=var[:], in0=self.bank[s2b][:], scalar=1.0 / D, in1=var[:],
                                                                 op0=ALU.mult, op1=ALU.subtract),
                         reads=[("bank", s2b), "var"], writes=["var"])
                    T.op("act", lambda e: e.activation(out=var[:], in_=var[:], func=AF.Ln, bias=self.eps_ln[:, 0:1], scale=1.0),
                         reads=["var"], writes=["var"])
                    T.op("act", lambda e: e.activation(out=var[:], in_=var[:], func=AF.Exp, scale=-0.5),
                         reads=["var"], writes=["var"])
                    for c in range(NC8):
                        vs = vslot(t, c)
                        T.op("dve", lambda e, vs=vs: e.tensor_tensor(out=v32[:, vs, :], in0=v32[:, vs, :], in1=mean[:], op=ALU.subtract),
                             reads=[("v32", vs), "mean"], writes=[("v32", vs)])
                        T.op("dve", lambda e, vs=vs: e.tensor_tensor(out=v32[:, vs, :], in0=v32[:, vs, :], in1=var[:], op=ALU.mult),
                             reads=[("v32", vs), "var"], writes=[("v32", vs)])

                def ln_act(t, cs):
                    for c in cs:
                        vs = vslot(t, c)
                        T.op("act", lambda e, c=c, vs=vs: e.activation(out=z[:, c, :], in_=v32[:, vs, :], func=AF.Silu,
                                                                bias=VO("clnb", c), scale=VO("clng", c)),
                             reads=[("v32", vs), "vec"], writes=[("z", c)])

                def outproj(t):
                    ts = slice(t * TT, (t + 1) * TT)
                    wo = [self.wget(widx["wo"][t][0]), self.wget(widx["wo"][t][1])]
                    wov = [w[0][:, 0:4096].rearrange("p (k n) -> p k n", k=8) for w in wo]
                    for m in range(NC8):
                        ob = 4 + m % 2
                        T.mm_group([(self.bank[ob][:], wov[m // 4][:, c, (m % 4) * 128:(m % 4 + 1) * 128], z[:, c, :]) for c in range(NC8)],
                                   reads=[wo[0][1], wo[1][1]] + [("z", c) for c in range(NC8)], writes=[("bank", ob)])
                        T.op("dve", lambda e, ob=ob, m=m: e.scalar_tensor_tensor(
                            out=self.xs[:, m, ts], in0=self.bank[ob][:], scalar=VO("cbout", m), in1=self.xs[:, m, ts],
                            op0=ALU.add, op1=ALU.add),
                            reads=[("bank", ob), ("xs", m, t), "vec"], writes=[("xs", m, t)])
                    self.wretire(widx["wo"][t][0])
                    self.wretire(widx["wo"][t][1])

                for t in range(NT + 1):
                    if t < NT:
                        conv_chunk(t, 0)
                        conv_chunk(t, 1)
                    if t > 0:
                        ln_dve(t - 1)
                    if t < NT:
                        conv_chunk(t, 2)
                    if t > 0:
                        ln_act(t - 1, range(0, 4))
                    if t < NT:
                        conv_chunk(t, 3)
                    if t > 0:
                        ln_act(t - 1, range(4, 8))
                    if t < NT:
                        for c in range(4, NC8):
                            conv_chunk(t, c)
                    if t > 0:
                        outproj(t - 1)
                T.barrier()

    def plan_pool(self, l):
        a = len(self.wq)
        self.wplan(("pw",), [(0, 8, 256, self.d_pw.rearrange("g (kc p) n -> p (g kc) n", p=128))])
        return a

    def pool(self, l, wi, seg):
        nc, T = self.nc, self.T
        from contextlib import ExitStack
        self.uid = getattr(self, "uid", 0) + 1
        uid = self.uid
        WIN = (2, 4, 8, 16)
        with ExitStack() as ph:
            B = lambda name, shape, dtype: ph.enter_context(nc.sbuf_tensor(f"{name}_{uid}", shape, dtype))
            hp = B("hp", [128, NC8, 16 + TT], F32)
            sA = [B(f"sA{i}", [128, 16 + TT], F32) for i in range(2)]
            sB = [B(f"sB{i}", [128, 16 + TT], F32) for i in range(2)]
            pb = B("pb", [128, NC8, TT], BF16)
            pinv = B("pinv", [128, 4, 16], F32)
            T.dma("sp", pinv[:], self.d_pinv[:, :, :], writes=["pinv"])
            ws, wk = self.wget(wi)
            pw = ws[:, 0:2048].rearrange("p (k n) -> p k n", k=8)
            g0 = VOFF[f"nmix{l}"]
            tcount = [0]

            def emit(t, rs, rkey):
                ts = slice(t * TT, (t + 1) * TT)
                first_tile = (seg == 0 and t == 0)
                for c in range(NC8):
                    T.op("dve", lambda e, c=c: e.tensor_copy(out=hp[:, c, 0:16], in_=self.pcarry[:, c, :]),
                         reads=[("pcarry", c)], writes=[("hp", c)])
                    T.op("dve", lambda e, c=c: e.scalar_tensor_tensor(
                        out=hp[:, c, 16:16 + TT], in0=self.xs[:, c, ts], scalar=self.vec[:, g0 + c:g0 + c + 1],
                        in1=rs[:], op0=ALU.mult, op1=ALU.mult),
                        reads=[("xs", c, t), rkey, ("hp", c)], writes=[("hp", c)])
                    T.op("dve", lambda e, c=c: e.tensor_copy(out=self.pcarry[:, c, :], in_=hp[:, c, TT:TT + 16]),
                         reads=[("hp", c)], writes=[("pcarry", c)])
                for c in range(NC8):
                    gi = c // 2
                    w = WIN[gi]
                    a_, b_ = sA[c % 2], sB[c % 2]
                    ka, kb = ("sA", c % 2), ("sB", c % 2)
                    T.op("dve", lambda e, c=c, a_=a_: e.tensor_tensor(out=a_[:, 2:16 + TT], in0=hp[:, c, 2:16 + TT], in1=hp[:, c, 1:15 + TT], op=ALU.add),
                         reads=[("hp", c)], writes=[ka])
                    cur, ck, oth, ok = a_, ka, b_, kb
                    lvl = 2
                    while lvl < w:
                        lo = 2 * lvl
                        T.op("dve", lambda e, cur=cur, oth=oth, lo=lo, lvl=lvl: e.tensor_tensor(
                            out=oth[:, lo:16 + TT], in0=cur[:, lo:16 + TT], in1=cur[:, lo - lvl:16 + TT - lvl], op=ALU.add),
                            reads=[ck], writes=[ok])
                        cur, ck, oth, ok = oth, ok, cur, ck
                        lvl *= 2
                    T.op("dve", lambda e, c=c, cur=cur, w=w: e.scalar_tensor_tensor(
                        out=pb[:, c, :], in0=cur[:, 16:16 + TT], scalar=1.0 / w, in1=hp[:, c, 16:16 + TT],
                        op0=ALU.mult, op1=ALU.subtract),
                        reads=[ck, ("hp", c)], writes=[("pb", c)])
                    if first_tile:
                        T.op("dve", lambda e, cur=cur, gi=gi: e.tensor_tensor(out=cur[:, 16:32], in0=cur[:, 16:32], in1=pinv[:, gi, :], op=ALU.mult),
                             reads=[ck, "pinv", ("pb", c)], writes=[ck])
                        T.op("dve", lambda e, c=c, cur=cur: e.tensor_tensor(out=pb[:, c, 0:16], in0=cur[:, 16:32], in1=hp[:, c, 16:32], op=ALU.subtract),
                             reads=[ck, ("hp", c)], writes=[("pb", c)])
                for m in range(NC8):
                    gi = m // 2
                    ob = 4 + m % 2
                    T.mm_group([(self.bank[ob][:], pw[:, 2 * gi + kc, (m % 2) * 128:(m % 2 + 1) * 128], pb[:, 2 * gi + kc, :]) for kc in range(2)],
                               reads=[wk, ("pb", 2 * gi), ("pb", 2 * gi + 1)], writes=[("bank", ob)])
                    T.op("dve", lambda e, ob=ob, m=m: e.scalar_tensor_tensor(
                        out=self.xs[:, m, ts], in0=self.bank[ob][:], scalar=self.vec[:, VOFF["pscale"] + m:VOFF["pscale"] + m + 1],
                        in1=self.xs[:, m, ts], op0=ALU.mult, op1=ALU.add),
                        reads=[("bank", ob), ("xs", m, t), "vec"], writes=[("xs", m, t)])
            self.rmsnorm(f"nmix{l}", emit)
            self.wretire(wi)
            T.barrier()


    def plan_attn(self, l):
        a = len(self.wq)
        for pr in range(4):
            for g in range(3):
                self.wplan(("wa", pr, g), [(0, 8, 256, self.d_wattn[:, pr, g, 0:256].rearrange("(k p) n -> p k n", p=128))])
                self.wplan(("wv", pr, g), [(0, 8, 128, self.d_wattn[:, pr, g, 256:384].rearrange("(k p) n -> p k n", p=128))])
            self.wplan(("wo", pr), [(0, 1, D, self.d_wo[pr * 128:(pr + 1) * 128, :].rearrange("(k p) n -> p k n", p=128))])
        return a

    def attn(self, l, wi, seg):
        nc, T = self.nc, self.T
        from contextlib import ExitStack
        self.uid = getattr(self, "uid", 0) + 1
        uid = self.uid
        DIL = (1, 4, 16)
        with ExitStack() as ph:
            B = lambda name, shape, dtype: ph.enter_context(nc.sbuf_tensor(f"{name}_{uid}", shape, dtype))
            self.h = B("h", [128, NC8, TS], BF16)
            cs = [B(f"cs{i}", [128, TT], F32) for i in range(2)]
            sn = [B(f"sn{i}", [128, TT], F32) for i in range(2)]
            Qt = B("Qt", [128, TS], BF16)
            Kt = B("Kt", [128, TS], BF16)
            Va = B("Va", [128, 16, 192], BF16)
            kc = B("kc", [128, TS], BF16)
            vc = B("vc", [128, 16, 192], BF16)
            acc = B("acc", [128, 2, TS], F32)
            OT = B("OT", [128, TS], BF16)
            pt = [B(f"pt{i}", [128, 256], BF16) for i in range(3)]
            qb16 = self.sq
            tm = [B(f"tm{i}", [128, TT], F32) for i in range(2)]
            rd = tm[0]
            self.norm_to_h(f"nmix{l}")
            T.op("pool", lambda e: e.memset(Va[:], 1.0), writes=["Va"])
            T.op("pool", lambda e: e.memset(vc[:], 1.0), writes=["vc"])
            tabn = [0]
            sbn = [0]
            obn = [0]
            ptn = [0]
            pw = wi
            for pr in range(4):
                T.op("pool", lambda e: e.memset(acc[:], 0.0), reads=[], writes=[("acc", 0), ("acc", 1)])
                for g in range(3):
                    d = DIL[g]
                    nblk = TS // (128 * d)
                    run = TS // d
                    wA, wAk = self.wget(pw)
                    wB, wBk = self.wget(pw + 1)
                    wAv = wA[:, 0:2048].rearrange("p (k n) -> p k n", k=8)
                    wBv = wB[:, 0:1024].rearrange("p (k n) -> p k n", k=8)
                    items = [(kind, t) for kind in range(2) for t in range(NT)]

                    def qk_main(i):
                        kind, t = items[i]
                        ts = slice(t * TT, (t + 1) * TT)
                        ab = (0, 2)[i % 2]
                        off = kind * 128
                        hr = [("h", k, t) for k in range(NC8)]
                        T.mm_group([(self.bank[ab][:], wAv[:, k, off:off + 128], self.h[:, k, ts]) for k in range(NC8)],
                                   reads=[wAk] + hr, writes=[("bank", ab)])
                        T.op("act", lambda e, ab=ab, i=i: e.activation(out=qb16[i % 2][:], in_=self.bank[ab][:], func=AF.Identity),
                             reads=[("bank", ab)], writes=[("sq", i % 2)])

                    def qk_rope(i):
                        kind, t = items[i]
                        dst = Qt if kind == 0 else Kt
                        dkey = "Qt" if kind == 0 else "Kt"
                        ab = (0, 2)[i % 2]
                        bb = ab + 1
                        T.mm_group([(self.bank[bb][:], self.perm_bf, qb16[i % 2][:])], reads=[("sq", i % 2), "cbf"], writes=[("bank", bb)])
                        ti = tabn[0] % 2
                        tabn[0] += 1
                        p0 = seg * TS + t * TT
                        T.dma("sp", cs[ti][:], self.d_cos[:, p0:p0 + TT], writes=[("cs", ti)])
                        T.dma("sp", sn[ti][:], self.d_sin[:, p0:p0 + TT], writes=[("sn", ti)])
                        T.op("dve", lambda e, ab=ab, ti=ti: e.tensor_tensor(out=tm[0][:], in0=self.bank[ab][:], in1=cs[ti][:], op=ALU.mult),
                             reads=[("bank", ab), ("cs", ti), ("sq", i % 2)], writes=[("tm", 0)])
                        T.op("dve", lambda e, bb=bb, ti=ti: e.tensor_tensor(out=tm[1][:], in0=self.bank[bb][:], in1=sn[ti][:], op=ALU.mult),
                             reads=[("bank", bb), ("sn", ti)], writes=[("tm", 1)])
                        ov = dst[:, :].rearrange("p (r i) -> p r i", r=d)[:, :, t * TT // d:(t + 1) * TT // d]
                        i0_ = tm[0][:, :].rearrange("p (i r) -> p r i", r=d)
                        i1_ = tm[1][:, :].rearrange("p (i r) -> p r i", r=d)
                        T.op("dve", lambda e, ov=ov, i0_=i0_, i1_=i1_: e.tensor_tensor(out=ov, in0=i0_, in1=i1_, op=ALU.add),
                             reads=[("tm", 0), ("tm", 1)], writes=[(dkey, t)])

                    for i in range(len(items) + 1):
                        if i < len(items):
                            qk_main(i)
                        if i > 0:
                            qk_rope(i - 1)
                    qk_all = [("Qt", t) for t in range(NT)] + [("Kt", t) for t in range(NT)]
                    for b4 in range(4):
                        vb = 6 + b4 % 2
                        for bi in range(4):
                            bidx = b4 * 4 + bi
                            r, jb = bidx // nblk, bidx % nblk
                            st = jb * 128 * d + r
                            T.mm_group([(self.bank[vb][:, bi * 128:(bi + 1) * 128], self.h[:, k, st:st + 127 * d + 1:d], wBv[:, k, :]) for k in range(NC8)],
                                       reads=[wBk] + [("h", k, tt) for k in range(NC8) for tt in range(NT)],
                                       writes=[("bank", vb)] if bi == 0 else [])
                        T.res[("bank", vb)] = [(T.sem["pe"], T.cnt["pe"], "pe", "pe"), []]
                        src = self.bank[vb][:, :].rearrange("p (b f) -> p b f", b=4)
                        T.op("act", lambda e, src=src, b4=b4: e.activation(out=Va[:, b4 * 4:(b4 + 1) * 4, 0:64], in_=src[:, :, 0:64], func=AF.Identity),
                             reads=[("bank", vb), "Va"], writes=[("Va", b4, 0)])
                        T.op("act", lambda e, src=src, b4=b4: e.activation(out=Va[:, b4 * 4:(b4 + 1) * 4, 128:192], in_=src[:, :, 64:128], func=AF.Identity),
                             reads=[("bank", vb), "Va"], writes=[("Va", b4, 1)])
                    va_all = [("Va", b4, e_) for b4 in range(4) for e_ in range(2)]
                    self.wretire(pw)
                    self.wretire(pw + 1)
                    pw += 2
                    if seg == 0:
                        T.dma("sp", self.d_kctx[pr, g, :, 0:d * 128].rearrange("p (r i) -> p r i", r=d),
                              Kt[:, :].rearrange("p (r i) -> p r i", r=d)[:, :, (nblk - 1) * 128:nblk * 128],
                              reads=qk_all, writes=[("kctx", pr, g)])
                        T.dma("sp", self.d_vctx[pr, g, :, 0:d * 192].rearrange("p (r f) -> p r f", r=d),
                              Va[:, :, :].rearrange("p (r j) f -> p r j f", r=d)[:, :, nblk - 1, :],
                              reads=va_all, writes=[("vctx", pr, g)])
                    else:
                        T.dma("sp", kc[:, 0:d * 128], self.d_kctx[pr, g, :, 0:d * 128], reads=[("kctx", pr, g)], writes=["kc"])
                        T.dma("sp", vc[:, 0:d, :], self.d_vctx[pr, g, :, 0:d * 192].rearrange("p (r f) -> p r f", r=d),
                              reads=[("vctx", pr, g), "vc"], writes=["vc2"])
                    def emit_pv(job):
                        (e_, r, kb, qlo, nq, vlhs, kr, pi) = job
                        for qi in range(nq):
                            qb = qlo + qi
                            closing = (kb == qb)
                            opening = not closing
                            if opening:
                                ob = 2 + obn[0] % 4
                                obn[0] += 1
                                self._ob_cur = ob
                            else:
                                ob = self._ob_cur if (seg == 1 or qb > 0) else None
                                if ob is None:
                                    ob = 2 + obn[0] % 4
                                    obn[0] += 1
                            first = opening or (seg == 0 and qb == 0)
                            T._deps("pe", [("pt", pi)] + kr, [("bank", ob)] if first else [])
                            inst = nc.tensor.matmul(self.bank[ob][:, 0:128], lhsT=vlhs, rhs=pt[pi][:, qi * 128:(qi + 1) * 128],
                                                    start=first, stop=closing)
                            T.cnt["pe"] += 1
                            inst.then_inc(T.sem["pe"], 1)
                            hh = (T.sem["pe"], T.cnt["pe"], "pe", "pe")
                            T._commit(hh, [("pt", pi)], [("bank", ob)] if closing else [])
                            if closing:
                                av = acc[:, e_, :].rearrange("p (i r) -> p r i", r=d)[:, r, qb * 128:(qb + 1) * 128]
                                T.op("dve", lambda e, ob=ob, av=av: e.tensor_tensor(out=av, in0=self.bank[ob][:, 0:128], in1=av, op=ALU.add),
                                     reads=[("bank", ob), ("acc", e_)], writes=[("acc", e_)])

                    pend_pv = []
                    for e_ in range(2):
                        ps = slice(64 * e_, 64 * e_ + 64)
                        for r in range(d):
                            kbs = ([-1] if seg == 1 else []) + list(range(nblk))
                            for kb in kbs:
                                if kb < 0:
                                    klhs = kc[ps, r * 128:(r + 1) * 128]
                                    vlhs = vc[:, r, e_ * 64:e_ * 64 + 128]
                                    kr = ["kc", "vc2"]
                                    qlo, nq, moff = 0, 1, 128
                                else:
                                    klhs = Kt[ps, r * run + kb * 128:r * run + (kb + 1) * 128]
                                    vlhs = Va[:, r * nblk + kb, e_ * 64:e_ * 64 + 128]
                                    kr = qk_all + va_all
                                    qlo, nq, moff = kb, (2 if kb + 1 < nblk else 1), 0
                                sb = sbn[0] % 2
                                sbn[0] += 1
                                ncol = nq * 128
                                T.mm_group([(self.bank[sb][:, 0:ncol], klhs, Qt[ps, r * run + qlo * 128:r * run + qlo * 128 + ncol])],
                                           reads=kr, writes=[("bank", sb)])
                                pi = ptn[0] % 3
                                ptn[0] += 1
                                T.op("act", lambda e, sb=sb, pi=pi, ncol=ncol: e.activation(out=pt[pi][:, 0:ncol], in_=self.bank[sb][:, 0:ncol], func=AF.Exp, scale=0.125),
                                     reads=[("bank", sb)], writes=[("pt", pi)])
                                T.op("dve", lambda e, pi=pi, ncol=ncol, moff=moff: e.tensor_tensor(out=pt[pi][:, 0:ncol], in0=pt[pi][:, 0:ncol], in1=self.maskb[:, moff:moff + ncol], op=ALU.mult),
                                     reads=[("pt", pi), "cbf"], writes=[("pt", pi)])
                                pend_pv.append((e_, r, kb, qlo, nq, vlhs, kr, pi))
                                if len(pend_pv) > 2:
                                    emit_pv(pend_pv.pop(0))
                    while pend_pv:
                        emit_pv(pend_pv.pop(0))
                wo, wok = self.wget(pw)
                wov = wo[:, 0:D]
                for t in range(NT):
                    ts = slice(t * TT, (t + 1) * TT)
                    for e_ in range(2):
                        ps = slice(64 * e_, 64 * e_ + 64)
                        fb = e_
                        rdb = tm[e_]
                        T.mm_group([(self.bank[fb][:], self.shift32, acc[:, e_, ts])], reads=[("acc", e_), "c32"], writes=[("bank", fb)])
                        T.op("act", lambda e, ps=ps, fb=fb, rdb=rdb: e.activation(out=rdb[ps, :], in_=self.bank[fb][ps, :], func=AF.Ln),
                             reads=[("bank", fb)], writes=[("tm", e_)])
                        T.op("act", lambda e, ps=ps, rdb=rdb: e.activation(out=rdb[ps, :], in_=rdb[ps, :], func=AF.Exp, scale=-1.0),
                             reads=[("tm", e_)], writes=[("tm", e_)])
                        T.op("dve", lambda e, ps=ps, e_=e_, rdb=rdb: e.tensor_tensor(out=OT[ps, ts], in0=acc[ps, e_, ts], in1=rdb[ps, :], op=ALU.mult),
                             reads=[("tm", e_), ("acc", e_)], writes=[("OT", t, e_)])
                    for m in range(NC8):
                        ob = 4 + m % 2
                        T.mm_group([(self.bank[ob][:], wov[:, m * 128:(m + 1) * 128], OT[:, ts])],
                                   reads=[wok, ("OT", t, 0), ("OT", t, 1)], writes=[("bank", ob)])
                        T.op("dve", lambda e, ob=ob, m=m: e.tensor_tensor(out=self.xs[:, m, ts], in0=self.bank[ob][:], in1=self.xs[:, m, ts], op=ALU.add),
                             reads=[("bank", ob), ("xs", m, t)], writes=[("xs", m, t)])
                self.wretire(pw)
                pw += 1
            T.barrier()

    def store_out(self, seg, final_norm):
        T = self.T
        if final_norm:
            g0 = VOFF["nfinal"]

            def emit(t, rs, rkey):
                ts = slice(t * TT, (t + 1) * TT)
                for c in range(NC8):
                    T.op("dve", lambda e, c=c: e.scalar_tensor_tensor(
                        out=self.xs[:, c, ts], in0=self.xs[:, c, ts], scalar=self.vec[:, g0 + c:g0 + c + 1],
                        in1=rs[:], op0=ALU.mult, op1=ALU.mult),
                        reads=[("xs", c, t), rkey], writes=[("xs", c, t)])
            self.rmsnorm("nfinal", emit)
        for c in range(NC8):
            T.dma("sp", self.d_out[c * 128:(c + 1) * 128, seg * TS:(seg + 1) * TS], self.xs[:, c, :],
                  reads=[("xs", c, t) for t in range(NT)], writes=[("out", c, seg)])

    def build(self):
        nc, T = self.nc, self.T
        from contextlib import ExitStack
        with ExitStack() as es:
            A = lambda name, shape, dtype: es.enter_context(nc.sbuf_tensor(name, shape, dtype))
            self.xs = A("xs", [128, NC8, TS], F32)
            self.vec = A("vec_sb", [128, NV], F32)
            self.cbf = A("cbf", [128, 640], BF16)
            self.c32 = A("c32", [128, 256], F32)
            self.eps_rms = A("eps_rms", [128, 1], F32)
            self.eps_ln = A("eps_ln", [128, 1], F32)
            self.ring = [A(f"ring{i}", [128, SLOT], BF16) for i in range(NSLOT)]
            self.sq = [A(f"sq{i}", [128, TT], BF16) for i in range(2)]
            self.rstd = [A(f"rstd{i}", [128, TT], F32) for i in range(2)]
            self.ccarry = A("ccarry", [128, 2, NC8, 32], BF16)
            self.pcarry = A("pcarry", [128, NC8, 16], F32)
            self.bank = [es.enter_context(nc.psum_tensor(f"bank{i}", [128, TT], F32)) for i in range(8)]
            self.ident_bf = self.cbf[:, 0:128]
            self.ones_bf = self.cbf[:, 128:256]
            self.maskb = self.cbf[:, 256:512]
            self.perm_bf = self.cbf[:, 512:640]
            self.ones32 = self.c32[:, 0:128]
            self.shift32 = self.c32[:, 128:256]

            T.dma("sp", self.vec[:], self.d_vec[:, :], writes=["vec"])
            T.dma("sp", self.cbf[:], self.d_cbf[:, :], writes=["cbf"])
            T.dma("sp", self.c32[:], self.d_c32[:, :], writes=["c32"])
            T.op("dve", lambda e: e.memset(self.eps_rms[:], RMS_EPS), writes=["eps"])
            T.op("dve", lambda e: e.memset(self.eps_ln[:], LN_EPS), writes=["eps2"])
            T.op("dve", lambda e: e.memset(self.ccarry[:], 0.0), writes=["cc0"])
            T.op("dve", lambda e: e.memset(self.pcarry[:], 0.0), writes=["pc0"])
            if any(k == "conv" for (k, _) in self.plan):
                with ExitStack() as st:
                    dgs = [st.enter_context(nc.sbuf_tensor(f"dgst{i}", [128, CW, 128], BF16)) for i in range(2)]
                    wdw = st.enter_context(nc.sbuf_tensor("wdw_all", [128, 2, NC8, CW], F32))
                    T.dma("sp", wdw[:], self.d_cwdw[:, :, :, :], writes=["wdw"])
                    n = 0
                    for j in range(2):
                        for c in range(NC8):
                            dd = dgs[n % 2]
                            for k in range(CW):
                                eng = "dve" if k < 22 else "act"
                                if eng == "dve":
                                    T.op("dve", lambda e, dd=dd, k=k, j=j, c=c: e.tensor_scalar(
                                        out=dd[:, k, :], in0=self.ident_bf, scalar1=wdw[:, j, c, k:k + 1], scalar2=None, op0=ALU.mult),
                                        reads=["wdw", "cbf"], writes=[("dgs", n % 2, k)])
                                else:
                                    T.op("act", lambda e, dd=dd, k=k, j=j, c=c: e.activation(
                                        out=dd[:, k, :], in_=self.ident_bf, func=AF.Copy, scale=wdw[:, j, c, k:k + 1]),
                                        reads=["wdw", "cbf"], writes=[("dgs", n % 2, k)])
                            T.dma("sp", self.d_dgd[j, c, :, :].rearrange("p (k n) -> p k n", k=CW), dd[:],
                                  reads=[("dgs", n % 2, k) for k in range(CW)], writes=[("dgd", j, c)])
                            n += 1
                    T.barrier()
            T.barrier()

            sched = []
            for seg in range(NSEG):
                for (kind, l) in self.plan:
                    if kind == "ffn":
                        sched.append((seg, kind, l, self.plan_ffn(l)))
                    elif kind == "conv":
                        sched.append((seg, kind, l, self.plan_conv(l)))
                    elif kind == "pool":
                        sched.append((seg, kind, l, self.plan_pool(l)))
                    else:
                        sched.append((seg, kind, l, self.plan_attn(l)))

            cur_seg = -1
            for (seg, kind, l, widx) in sched:
                if seg != cur_seg:
                    if cur_seg >= 0:
                        self.store_out(cur_seg, self.final_norm)
                    cur_seg = seg
                    for c in range(NC8):
                        T.dma("sp", self.xs[:, c, :], self.d_xT[c * 128:(c + 1) * 128, seg * TS:(seg + 1) * TS],
                              writes=[("xs", c, t) for t in range(NT)])
                if kind == "ffn":
                    with ExitStack() as ph:
                        self.uid = getattr(self, "uid", 0) + 1
                        B = lambda name, shape, dtype: ph.enter_context(nc.sbuf_tensor(f"{name}_{self.uid}", shape, dtype))
                        self.h = B("h", [128, NC8, TS], BF16)
                        self.abuf = [B(f"abuf{i}", [128, 4, TT], BF16) for i in range(2)]
                        self.sg = [B(f"sg{i}", [128, TT], F32) for i in range(2)]
                        self.ffn(l, widx)
                        T.barrier()
                elif kind == "conv":
                    self.conv(l, widx, seg)
                elif kind == "pool":
                    self.pool(l, widx, seg)
                elif kind == "attn":
                    self.attn(l, widx, seg)
            self.store_out(cur_seg, self.final_norm)
            T.barrier()
        return nc


def _chunked(v):
    v = np.asarray(v, np.float32)
    return np.ascontiguousarray(v.reshape(-1, 128).T)


def prep_shared(inp):
    f32 = np.float32
    vec = np.zeros((128, NV), f32)
    for l in range(DEPTH):
        vec[:, VOFF[f"nmix{l}"]:VOFF[f"nmix{l}"] + 8] = _chunked(inp["norm_mix_g"][l])
        vec[:, VOFF[f"nffn{l}"]:VOFF[f"nffn{l}"] + 8] = _chunked(inp["norm_ffn_g"][l])
    vec[:, VOFF["nfinal"]:VOFF["nfinal"] + 8] = _chunked(inp["final_norm_g"])
    for j in range(2):
        vec[:, VOFF[f"cba{j}"]:VOFF[f"cba{j}"] + 8] = _chunked(inp["conv_b_in"][j][:D])
        vec[:, VOFF[f"cbg{j}"]:VOFF[f"cbg{j}"] + 8] = _chunked(inp["conv_b_in"][j][D:])
        vec[:, VOFF[f"cbdw{j}"]:VOFF[f"cbdw{j}"] + 8] = _chunked(inp["conv_b_dw"][j])
        vec[:, VOFF[f"clng{j}"]:VOFF[f"clng{j}"] + 8] = _chunked(inp["conv_ln_g"][j])
        vec[:, VOFF[f"clnb{j}"]:VOFF[f"clnb{j}"] + 8] = _chunked(inp["conv_ln_b"][j])
        vec[:, VOFF[f"cbout{j}"]:VOFF[f"cbout{j}"] + 8] = _chunked(inp["conv_b_out"][j])
    vec[:, VOFF["pscale"]:VOFF["pscale"] + 8] = _chunked(inp["pool_scale"][0])

    cw = np.asarray(inp["conv_w_in"], f32)
    cwin_r = np.ascontiguousarray(np.stack([cw[:, :, :D].reshape(2, D, 8, 128),
                                            cw[:, :, D:].reshape(2, D, 8, 128)], axis=3).reshape(2, D, 8, 256))
    dw = np.asarray(inp["conv_w_dw"], f32)
    cwdw_r = np.ascontiguousarray(dw.reshape(2, CW, 8, 128).transpose(3, 0, 2, 1))

    pinv = np.zeros((128, 4, 16), f32)
    for g, w in enumerate((2, 4, 8, 16)):
        pinv[:, g, :] = 1.0 / np.minimum(np.arange(1, 17), w).astype(f32)

    wqkv = np.asarray(inp["attn_w_qkv"], f32)[0]
    wq = wqkv[:, 0:1536].reshape(D, 3, 8, 64)
    wk = wqkv[:, 1536:3072].reshape(D, 3, 8, 64)
    wv = wqkv[:, 3072:4608].reshape(D, 3, 8, 64)
    swap = np.concatenate([np.arange(32, 64), np.arange(0, 32)])
    w_r = np.zeros((D, 4, 3, 3, 2, 64), f32)
    for pr in range(4):
        for g in range(3):
            for e in range(2):
                hs = 2 * pr + e
                w_r[:, pr, g, 0, e] = wq[:, g, hs]
                w_r[:, pr, g, 1, e] = wk[:, g, hs]
                w_r[:, pr, g, 2, e] = wv[:, g, hs]
    w_r = np.ascontiguousarray(w_r.reshape(D, 4, 3, 384))

    half = 32
    inv_freq = (10000.0 ** (-np.arange(half, dtype=f32) / half)).astype(f32)
    ang = np.arange(S, dtype=f32)[None, :] * inv_freq[:, None]
    cos = np.cos(ang).astype(f32)
    sin = np.sin(ang).astype(f32)
    cos_t = np.concatenate([cos, cos, cos, cos], axis=0)
    sin_t = np.concatenate([-sin, sin, -sin, sin], axis=0)

    cbf = np.zeros((128, 640), f32)
    cbf[:, 0:128] = np.eye(128, dtype=f32)
    cbf[:, 128:256] = 1.0
    a = np.arange(128)[:, None]
    b = np.arange(256)[None, :]
    cbf[:, 256:512] = np.where((b >= a) & (b <= a + 128), 1.0, 0.0)
    mm = np.arange(128)
    cbf[(mm % 64 + 32) % 64 + 64 * (mm // 64), 512 + mm] = 1.0
    c32 = np.zeros((128, 256), f32)
    c32[:, 0:128] = 1.0
    c32[:, 128:256] = np.roll(np.eye(128, dtype=f32), 64, axis=0)

    return {
        "vec": vec,
        "ffn_w_gate": np.ascontiguousarray(inp["ffn_w_gate"], f32),
        "ffn_w_up": np.ascontiguousarray(inp["ffn_w_up"], f32),
        "ffn_w_down": np.ascontiguousarray(inp["ffn_w_down"], f32),
        "conv_w_in_r": cwin_r,
        "conv_w_out": np.ascontiguousarray(inp["conv_w_out"], f32),
        "conv_w_dw_r": cwdw_r,
        "pool_w": np.ascontiguousarray(inp["pool_w"], f32)[0],
        "pool_inv": pinv,
        "attn_w_r": w_r,
        "attn_w_o": np.ascontiguousarray(inp["attn_w_o"], f32)[0],
        "rope_cos": np.ascontiguousarray(cos_t),
        "rope_sin": np.ascontiguousarray(sin_t),
        "const_bf": cbf.astype(ml_dtypes.bfloat16),
        "const_f32": c32,
    }


FULL_PLAN = [("conv", 0), ("ffn", 0), ("attn", 1), ("ffn", 1), ("pool", 2), ("ffn", 2), ("conv", 3), ("ffn", 3)]


def run(inputs, plan=FULL_PLAN, final_norm=True, cores=8):
    inp = {k: np.asarray(v) for k, v in inputs.items()}
    shared = prep_shared(inp)
    x = np.asarray(inp["x"], np.float32)
    in_maps = []
    for b in range(cores):
        m = dict(shared)
        m["xT"] = np.ascontiguousarray(x[b].T)
        in_maps.append(m)
    nc = Prog(plan, final_norm).build()
    res = run_bass_kernel_spmd(nc, in_maps, core_ids=list(range(cores)))
    out = np.stack([np.ascontiguousarray(r["yT"].T) for r in res.results], axis=0)
    return out.astype(np.float32)


def kernel(**inputs):
    return run(inputs)
```
